# Optimizing a Trainium2 kernel written in Bass

```python
import math
import jax
import jax.numpy as jnp
from jax import lax
import numpy as np

D_MODEL = 2048
BATCH = 16
SEQ = 2048
DEPTH = 4

CHUNK = 64
N_MIXERS = 2
EPS = 1e-6

DN_QK_HEADS = 16
DN_V_HEADS = 32
DN_HEAD_DIM = 128
DN_CONV = 4
DN_KEY_DIM = DN_QK_HEADS * DN_HEAD_DIM
DN_VAL_DIM = DN_V_HEADS * DN_HEAD_DIM
DN_CONV_DIM = 2 * DN_KEY_DIM + DN_VAL_DIM
DN_IN_DIM = DN_CONV_DIM + DN_VAL_DIM + 2 * DN_V_HEADS

DSA_HEADS = 16
DSA_HEAD_DIM = 128
Q_LORA = 512
KV_LORA = 256
IDX_HEADS = 16
IDX_DIM = 128
IDX_TOPK = 256
QBLOCK = 128
DSA_IN_DIM = Q_LORA + KV_LORA + IDX_DIM + IDX_HEADS
DSA_UQ_DIM = DSA_HEADS * DSA_HEAD_DIM + IDX_HEADS * IDX_DIM

D_FF = 4 * D_MODEL
N_DN = (DEPTH + 1) // 2
N_DSA = DEPTH // 2

kernel_name = "hybrid_deltanet_dsa_streaming_trunk"


def rms_norm(x, g):
    xf = x.astype(jnp.float32)
    y = xf * lax.rsqrt(jnp.mean(xf * xf, axis=-1, keepdims=True) + EPS)
    return (y * g.astype(jnp.float32)).astype(x.dtype)


def _l2norm(t):
    return t * lax.rsqrt(jnp.sum(t * t, axis=-1, keepdims=True) + EPS)


def _causal_conv(u, w):
    k_taps = w.shape[0]
    t_len = u.shape[1]
    up = jnp.pad(u, ((0, 0), (k_taps - 1, 0), (0, 0)))
    y = up[:, 0:t_len] * w[0]
    for j in range(1, k_taps):
        y = y + up[:, j:j + t_len] * w[j]
    return y


def _chunk_gated_delta_rule(q, k, v, g, beta):
    b_sz, t_len, n_h, dk = q.shape
    dv = v.shape[-1]
    n_c = t_len // CHUNK

    def chunks(t):
        t = t.reshape((b_sz, n_c, CHUNK, n_h) + t.shape[3:])
        return jnp.moveaxis(t, (1, 3), (0, 2))

    q, k, v, g, beta = map(chunks, (q, k, v, g, beta))
    gc = jnp.cumsum(g, axis=-1)
    pos = jnp.arange(CHUNK)
    lower = pos[:, None] >= pos[None, :]
    strict = pos[:, None] > pos[None, :]
    diff = gc[..., :, None] - gc[..., None, :]
    decay = jnp.where(lower, jnp.exp(jnp.where(lower, diff, 0.0)), 0.0)
    kb = k * beta[..., None]
    vb = v * beta[..., None]
    l_mat = jnp.where(strict, jnp.einsum("nbhid,nbhjd->nbhij", kb, k) * decay, 0.0)
    eye = jnp.eye(CHUNK, dtype=jnp.float32)
    t_mat = lax.linalg.triangular_solve(eye + l_mat, jnp.broadcast_to(eye, l_mat.shape),
                                        left_side=True, lower=True, unit_diagonal=True)
    u = t_mat @ vb
    w = t_mat @ (kb * jnp.exp(gc)[..., None])
    intra = jnp.where(lower, jnp.einsum("nbhid,nbhjd->nbhij", q, k) * decay, 0.0)
    q_dec = q * jnp.exp(gc)[..., None]
    g_last = gc[..., -1]
    k_dec = k * jnp.exp(g_last[..., None] - gc)[..., None]

    def step(state, inp):
        q_i, k_i, u_i, w_i, a_i, gl = inp
        v_new = u_i - w_i @ state
        o_i = q_i @ state + a_i @ v_new
        state = state * jnp.exp(gl)[..., None, None] + jnp.einsum("bhcd,bhce->bhde", k_i, v_new)
        return state, o_i

    s0 = jnp.zeros((b_sz, n_h, dk, dv), jnp.float32)
    _, o = lax.scan(step, s0, (q_dec, k_dec, u, w, intra, g_last))
    return jnp.moveaxis(o, (0, 2), (1, 3)).reshape(b_sz, t_len, n_h, dv)


def gated_deltanet(h, w_in, conv_w, a_log, dt_bias, out_norm, w_out):
    b_sz, t_len, _ = h.shape
    f32 = jnp.float32
    proj = h @ w_in
    s1 = DN_CONV_DIM
    s2 = s1 + DN_VAL_DIM
    s3 = s2 + DN_V_HEADS
    qkv, z, b, a = jnp.split(proj, [s1, s2, s3], axis=-1)
    qkv = jax.nn.silu(_causal_conv(qkv, conv_w)).astype(f32)
    q, k, v = jnp.split(qkv, [DN_KEY_DIM, 2 * DN_KEY_DIM], axis=-1)
    rep = DN_V_HEADS // DN_QK_HEADS
    q = jnp.repeat(_l2norm(q.reshape(b_sz, t_len, DN_QK_HEADS, DN_HEAD_DIM)), rep, axis=2) * (DN_HEAD_DIM ** -0.5)
    k = jnp.repeat(_l2norm(k.reshape(b_sz, t_len, DN_QK_HEADS, DN_HEAD_DIM)), rep, axis=2)
    v = v.reshape(b_sz, t_len, DN_V_HEADS, DN_HEAD_DIM)
    beta = jax.nn.sigmoid(b.astype(f32))
    g = -jnp.exp(a_log.astype(f32)) * jax.nn.softplus(a.astype(f32) + dt_bias.astype(f32))
    o = _chunk_gated_delta_rule(q, k, v, g, beta)
    o = rms_norm(o, out_norm) * jax.nn.silu(z.reshape(b_sz, t_len, DN_V_HEADS, DN_HEAD_DIM).astype(f32))
    return o.reshape(b_sz, t_len, DN_VAL_DIM).astype(h.dtype) @ w_out


def dsa_attention(h, w_in, q_norm, kv_norm, kidx_norm, w_uq, w_uk, w_uv, w_out):
    b_sz, t_len, _ = h.shape
    f32 = jnp.float32
    proj = h @ w_in
    q_lat, c_kv, k_idx, w_idx = jnp.split(
        proj, [Q_LORA, Q_LORA + KV_LORA, Q_LORA + KV_LORA + IDX_DIM], axis=-1)
    q_all = rms_norm(q_lat, q_norm) @ w_uq
    q, q_idx = jnp.split(q_all, [DSA_HEADS * DSA_HEAD_DIM], axis=-1)
    q = q.reshape(b_sz, t_len, DSA_HEADS, DSA_HEAD_DIM)
    q_idx = q_idx.reshape(b_sz, t_len, IDX_HEADS, IDX_DIM).astype(f32)
    c_kv = rms_norm(c_kv, kv_norm).astype(f32)
    k_idx = rms_norm(k_idx, kidx_norm).astype(f32)
    w_idx = w_idx.astype(f32) * (IDX_HEADS ** -0.5 * IDX_DIM ** -0.5)
    q_abs = jnp.einsum("bthd,hdr->bthr", q, w_uk).astype(f32) * (DSA_HEAD_DIM ** -0.5)
    top_k = min(IDX_TOPK, t_len // 4)
    key_chunk = jnp.arange(t_len) // CHUNK
    n_blk = t_len // QBLOCK

    def blocks(t):
        return jnp.moveaxis(t.reshape((b_sz, n_blk, QBLOCK) + t.shape[2:]), 1, 0)

    def attend_block(args):
        qa_b, qi_b, wi_b, blk = args
        q_chunk = (blk * QBLOCK + jnp.arange(QBLOCK)) // CHUNK
        admissible = key_chunk[None, :] <= q_chunk[:, None]
        rel = jax.nn.relu(jnp.einsum("bthd,bsd->bths", qi_b, k_idx))
        score = jnp.einsum("bths,bth->bts", rel, wi_b)
        score = jnp.where(admissible[None], score, -jnp.inf)
        _, sel = lax.top_k(score, top_k)
        valid = key_chunk[sel] <= q_chunk[None, :, None]
        kv_sel = jax.vmap(lambda c, i: c[i])(c_kv, sel)
        logits = jnp.einsum("bthr,btkr->bthk", qa_b, kv_sel)
        logits = jnp.where(valid[:, :, None, :], logits, -jnp.inf)
        p = jax.nn.softmax(logits, axis=-1)
        return jnp.einsum("bthk,btkr->bthr", p, kv_sel)

    o_lat = lax.map(attend_block, (blocks(q_abs), blocks(q_idx), blocks(w_idx), jnp.arange(n_blk)))
    o_lat = jnp.moveaxis(o_lat, 0, 1).reshape(b_sz, t_len, DSA_HEADS, KV_LORA)
    o = jnp.einsum("bthr,hrd->bthd", o_lat.astype(h.dtype), w_uv)
    return o.reshape(b_sz, t_len, DSA_HEADS * DSA_HEAD_DIM) @ w_out


def squared_relu_mlp(h, w_up, w_down):
    return jnp.square(jax.nn.relu(h @ w_up)) @ w_down


def _dense(key, shape, fan_in):
    return jax.random.normal(key, shape, jnp.float32) * (fan_in ** -0.5)


def _gain(key, shape):
    return 1.0 + 0.02 * jax.random.normal(key, shape, jnp.float32)


def setup_inputs(seed: int = 0) -> dict:
    key = jax.random.key(seed)
    ks = jax.random.split(key, 20)
    x = jax.random.normal(ks[0], (BATCH, SEQ, D_MODEL), jnp.float32)
    dt = jnp.exp(jax.random.uniform(ks[7], (N_DN, DN_V_HEADS), dtype=jnp.float32,
                                    minval=math.log(1e-3), maxval=math.log(1e-1)))
    return {
        "x": x,
        "norm_mix": _gain(ks[1], (DEPTH, D_MODEL)),
        "norm_mlp": _gain(ks[2], (DEPTH, D_MODEL)),
        "norm_final": _gain(ks[3], (D_MODEL,)),
        "dn_w_in": _dense(ks[4], (N_DN, D_MODEL, DN_IN_DIM), D_MODEL),
        "dn_conv_w": _dense(ks[5], (N_DN, DN_CONV, DN_CONV_DIM), DN_CONV),
        "dn_a_log": jnp.log(jax.random.uniform(ks[6], (N_DN, DN_V_HEADS), dtype=jnp.float32,
                                               minval=1.0, maxval=16.0)),
        "dn_dt_bias": dt + jnp.log(-jnp.expm1(-dt)),
        "dn_out_norm": _gain(ks[8], (N_DN, DN_HEAD_DIM)),
        "dn_w_out": _dense(ks[9], (N_DN, DN_VAL_DIM, D_MODEL), DN_VAL_DIM),
        "dsa_w_in": _dense(ks[10], (N_DSA, D_MODEL, DSA_IN_DIM), D_MODEL),
        "dsa_q_norm": _gain(ks[11], (N_DSA, Q_LORA)),
        "dsa_kv_norm": _gain(ks[12], (N_DSA, KV_LORA)),
        "dsa_kidx_norm": _gain(ks[13], (N_DSA, IDX_DIM)),
        "dsa_w_uq": _dense(ks[14], (N_DSA, Q_LORA, DSA_UQ_DIM), Q_LORA),
        "dsa_w_uk": _dense(ks[15], (N_DSA, DSA_HEADS, DSA_HEAD_DIM, KV_LORA), KV_LORA),
        "dsa_w_uv": _dense(ks[16], (N_DSA, DSA_HEADS, KV_LORA, DSA_HEAD_DIM), KV_LORA),
        "dsa_w_out": _dense(ks[17], (N_DSA, DSA_HEADS * DSA_HEAD_DIM, D_MODEL), DSA_HEADS * DSA_HEAD_DIM),
        "mlp_w_up": _dense(ks[18], (DEPTH, D_MODEL, D_FF), D_MODEL),
        "mlp_w_down": _dense(ks[19], (DEPTH, D_FF, D_MODEL), D_FF),
    }


def reference(x, norm_mix, norm_mlp, norm_final, dn_w_in, dn_conv_w, dn_a_log, dn_dt_bias,
              dn_out_norm, dn_w_out, dsa_w_in, dsa_q_norm, dsa_kv_norm, dsa_kidx_norm,
              dsa_w_uq, dsa_w_uk, dsa_w_uv, dsa_w_out, mlp_w_up, mlp_w_down):
    h = x
    for i in range(DEPTH):
        j = i // N_MIXERS
        u = rms_norm(h, norm_mix[i])
        if i % N_MIXERS == 0:
            mix = gated_deltanet(u, dn_w_in[j], dn_conv_w[j], dn_a_log[j], dn_dt_bias[j],
                                 dn_out_norm[j], dn_w_out[j])
        else:
            mix = dsa_attention(u, dsa_w_in[j], dsa_q_norm[j], dsa_kv_norm[j], dsa_kidx_norm[j],
                                dsa_w_uq[j], dsa_w_uk[j], dsa_w_uv[j], dsa_w_out[j])
        h = h + mix
        h = h + squared_relu_mlp(rms_norm(h, norm_mlp[i]), mlp_w_up[i], mlp_w_down[i])
    return rms_norm(h, norm_final)
```

```python
import numpy as np
from contextlib import ExitStack
import concourse.bass as bass
import concourse.mybir as mybir
from concourse.bass_utils import run_bass_kernel_spmd

F32 = mybir.dt.float32
BF16 = mybir.dt.bfloat16
AF = mybir.ActivationFunctionType
ALU = mybir.AluOpType
AX = mybir.AxisListType
ARENA_BYTES = 200 * 1024


class Dep:
    __slots__ = ("w", "r", "parent", "children")

    def __init__(self, parent=None):
        self.w = None
        self.r = []
        self.parent = parent
        self.children = []

    def related(self):
        out = [self]
        if self.parent is not None:
            out.append(self.parent)
        out.extend(self.children)
        return out


class Tl:
    def __init__(self, t):
        self.t = t
        self.d = Dep()
        self.subs = {}

    def sub(self, key):
        s = self.subs.get(key)
        if s is None:
            s = self.subs[key] = Dep(self.d)
            self.d.children.append(s)
        return s

    def __getitem__(self, idx):
        return self.t[idx]


def _dep(x):
    return x.d if isinstance(x, Tl) else x


class Eng:
    def __init__(self, name, h, sem, is_pe=False):
        self.name = name
        self.h = h
        self.sem = sem
        self.cnt = 0
        self.pending = False
        self.seen = {}
        self.prog = []
        self.is_pe = is_pe
        self.dsems = []
        self.dcnt = []
        self.dnext = 0


class K:
    def __init__(self, nc, n_dma_sems=8):
        self.nc = nc
        self.es = ExitStack()
        self.e = {}
        for name, h, pe in (("pe", nc.tensor, True), ("act", nc.scalar, False),
                            ("dve", nc.vector, False), ("pool", nc.gpsimd, False),
                            ("sp", nc.sync, False)):
            sem = self.es.enter_context(nc.semaphore("s_" + name))
            self.e[name] = Eng(name, h, sem, pe)
        for q in ("sp", "pool", "act"):
            en = self.e[q]
            for i in range(n_dma_sems):
                en.dsems.append(self.es.enter_context(nc.semaphore(f"d_{q}{i}")))
                en.dcnt.append(0)
        self.n_inst = 0
        self.asize = ARENA_BYTES
        self.aoff = 0
        self.arena = self.es.enter_context(nc.sbuf_tensor("arena", [128, ARENA_BYTES // 4], F32))
        self.psum = self.es.enter_context(nc.psum_tensor("psum", [128, 8 * 512], F32))
        self.pb = [Tl(self.psum[:, i * 512:(i + 1) * 512]) for i in range(8)]

    def sb(self, name, shape, dt):
        shape = list(shape)
        esz = 2 if dt == BF16 else 4
        n = 1
        for x in shape[1:]:
            n *= x
        nbytes = (n * esz + 63) // 64 * 64
        if self.aoff + nbytes > self.asize:
            raise RuntimeError(f"SBUF arena overflow at {name}: {self.aoff}+{nbytes}>{self.asize}")
        v = self.arena[0:shape[0], self.aoff // 4:(self.aoff + nbytes) // 4]
        self.aoff += nbytes
        if dt != F32:
            v = v.bitcast(dt)
        v = v[:, 0:n]
        if len(shape) == 3:
            v = v.rearrange("p (a b) -> p a b", a=shape[1])
        elif len(shape) == 4:
            v = v.rearrange("p (a b c) -> p a b c", a=shape[1], b=shape[2])
        return Tl(v)

    def mark(self):
        return self.aoff

    def release(self, m):
        self.aoff = m

    def dram(self, name, shape, dt, kind="Internal"):
        return Tl(self.nc.dram_tensor(name, list(shape), dt, kind=kind).ap())

    def barrier(self):
        tgt = []
        for name, en in self.e.items():
            if en.pending:
                raise RuntimeError("barrier with pending unsignalled instr on " + name)
            if en.cnt > 0:
                tgt.append((en.sem, en.cnt))
            for s_, c in zip(en.dsems, en.dcnt):
                if c > 0:
                    tgt.append((s_, c))
        for name, en in self.e.items():
            for (s_, v) in tgt:
                if s_ is en.sem:
                    continue
                if en.seen.get(s_, 0) < v:
                    en.seen[s_] = v
                    en.prog.append(("w", s_, v))

    def _waits(self, en, r, w):
        need = {}
        for x in r:
            for d in _dep(x).related():
                if d.w is not None:
                    s, v = d.w
                    if need.get(s, 0) < v:
                        need[s] = v
        for x in w:
            for d in _dep(x).related():
                if d.w is not None:
                    s, v = d.w
                    if need.get(s, 0) < v:
                        need[s] = v
                for (s, v) in d.r:
                    if need.get(s, 0) < v:
                        need[s] = v
        for s, v in need.items():
            if s is en.sem and en.is_pe:
                continue
            if en.seen.get(s, 0) >= v:
                continue
            en.seen[s] = v
            en.prog.append(("w", s, v))

    def _mark(self, ident, r, w):
        for x in w:
            d = _dep(x)
            d.w = ident
            d.r = []
        for x in r:
            d = _dep(x)
            d.r.append(ident)
            if len(d.r) > 64:
                mx = {}
                for (s, v) in d.r:
                    if mx.get(s, 0) < v:
                        mx[s] = v
                d.r = list(mx.items())

    def op(self, eng, fn, r=(), w=(), sig=True):
        en = self.e[eng]
        self._waits(en, r, w)
        if sig:
            en.cnt += 1
            en.pending = False
            en.prog.append(("i", fn, True))
            ident = (en.sem, en.cnt)
        else:
            en.pending = True
            en.prog.append(("i", fn, False))
            ident = (en.sem, en.cnt + 1)
        self._mark(ident, r, w)
        self.n_inst += 1

    def dma(self, q, out, in_, r=(), w=(), **kw):
        en = self.e[q]
        self._waits(en, r, w)
        i = en.dnext
        en.dnext = (i + 1) % len(en.dsems)
        s = en.dsems[i]
        if en.dcnt[i] > 0 and en.seen.get(s, 0) < en.dcnt[i]:
            en.seen[s] = en.dcnt[i]
            en.prog.append(("w", s, en.dcnt[i]))
        en.dcnt[i] += 16
        en.prog.append(("d", out, in_, s, kw))
        self._mark((s, en.dcnt[i]), r, w)
        self.n_inst += 1

    def finish(self, final_deps=()):
        nc = self.nc
        for name, en in self.e.items():
            if en.pending:
                raise RuntimeError(f"engine {name} has unsignalled trailing instructions")
        sp = self.e["sp"]
        self._waits(sp, list(final_deps), [])
        for q in ("sp", "pool", "act"):
            en = self.e[q]
            for s, c in zip(en.dsems, en.dcnt):
                if c > 0 and sp.seen.get(s, 0) < c:
                    sp.seen[s] = c
                    sp.prog.append(("w", s, c))
        for name in ("pe", "act", "dve", "pool"):
            en = self.e[name]
            if en.cnt > 0 and sp.seen.get(en.sem, 0) < en.cnt:
                sp.prog.append(("w", en.sem, en.cnt))

        def run(en, h):
            for it in en.prog:
                if it[0] == "w":
                    h.wait_ge(it[1], it[2])
                elif it[0] == "i":
                    ins = it[1](h)
                    if it[2]:
                        ins.then_inc(en.sem, 1)
                else:
                    h.dma_start(out=it[1], in_=it[2], **it[4]).then_inc(it[3], 16)

        with nc.Block() as block:
            @block.tensor
            def _(h):
                run(self.e["pe"], h)

            @block.scalar
            def _(h):
                run(self.e["act"], h)

            @block.vector
            def _(h):
                run(self.e["dve"], h)

            @block.gpsimd
            def _(h):
                run(self.e["pool"], h)

            @block.sync
            def _(h):
                run(self.e["sp"], h)
        self.es.close()


D = 2048
KC = D // 128
EPS = 1e-6
TT = 512


def load_cols(k, name, src, n):
    t = k.sb(name, [128, n], F32)
    k.dma("sp", t[:, :], src.rearrange("(c p) -> p c", p=128), w=[t],
          allow_slow_non_contiguous=True)
    return t


def rms_to_uT(k, hT, t0, gcol, hbuf, uT, sq, rstd, ones, pss, epsc, u32=False):
    for kc in range(KC):
        k.dma("sp", hbuf[:, kc, :], hT[kc * 128:(kc + 1) * 128, t0:t0 + TT], w=[hbuf.sub(kc)])
    for kc in range(KC):
        k.op("act", lambda h, kc=kc: h.activation(out=sq[kc % 2][:, :], in_=hbuf[:, kc, :], func=AF.Square),
             r=[hbuf.sub(kc)], w=[sq[kc % 2]])
        k.op("pe", lambda h, kc=kc: h.matmul(pss[:, :], lhsT=ones[:, :], rhs=sq[kc % 2][:, :],
                                            start=(kc == 0), stop=(kc == KC - 1)),
             r=[sq[kc % 2], ones], w=[pss])
    k.op("act", lambda h: h.activation(out=rstd[:, :], in_=pss[:, :], func=AF.Sqrt, scale=1.0 / D, bias=epsc[:, :]),
         r=[pss, epsc], w=[rstd])
    k.op("dve", lambda h: h.reciprocal(out=rstd[:, :], in_=rstd[:, :]), r=[rstd], w=[rstd])
    for kc in range(KC):
        k.op("dve", lambda h, kc=kc: h.scalar_tensor_tensor(out=uT[:, kc, :], in0=hbuf[:, kc, :],
                                                            scalar=gcol[:, kc:kc + 1], in1=rstd[:, :],
                                                            op0=ALU.mult, op1=ALU.mult),
             r=[hbuf.sub(kc), gcol, rstd], w=[uT])
        if u32:
            k.op("dve", lambda h, kc=kc: h.scalar_tensor_tensor(out=hbuf[:, kc, :], in0=hbuf[:, kc, :],
                                                                scalar=gcol[:, kc:kc + 1], in1=rstd[:, :],
                                                                op0=ALU.mult, op1=ALU.mult),
                 r=[hbuf.sub(kc), gcol, rstd], w=[hbuf.sub(kc)])


def stage_mlp(k, hT, ntok, g_d, wup_d, wdn_d, F):
    m = k.mark()
    FC = F // 128
    gcol = load_cols(k, "gcol", g_d, KC)
    ones = k.sb("ones", [128, 128], BF16)
    k.op("dve", lambda h: h.memset(ones[:, :], 1.0), w=[ones])
    epsc = k.sb("epsc", [128, 1], F32)
    k.op("dve", lambda h: h.memset(epsc[:, :], EPS), w=[epsc])
    hbuf = k.sb("hbuf", [128, KC, TT], F32)
    uT = [k.sb(f"uT{i}", [128, KC, TT], BF16) for i in range(1)]
    aT = k.sb("aT", [128, FC, TT], BF16)
    sq = [k.sb(f"sq{i}", [128, TT], BF16) for i in range(2)]
    rstd = k.sb("rstd", [128, TT], F32)
    tmp = [k.sb(f"tmp{i}", [128, TT], BF16) for i in range(2)]
    wu = [k.sb(f"wu{i}", [128, KC, 512], BF16) for i in range(2)]
    wd = [k.sb(f"wd{i}", [128, FC, 128], BF16) for i in range(2)]
    wupv = wup_d.rearrange("(kc p) f -> p kc f", p=128)
    wdnv = wdn_d.rearrange("(fc p) d -> p fc d", p=128)
    pb = k.pb
    nmm = 0
    for ti in range(ntok // TT):
        t0 = ti * TT
        u = uT[0]
        rms_to_uT(k, hT, t0, gcol, hbuf, u, sq, rstd, ones, pb[4], epsc)
        for fb in range(F // 512):
            W = wu[fb % 2]
            k.dma("pool", W[:, :, :], wupv[:, :, fb * 512:(fb + 1) * 512], w=[W])
            for fc in range(4):
                fi = fb * 4 + fc
                ps = pb[fi % 2]
                for kc in range(KC):
                    k.op("pe", lambda h, ps=ps, W=W, kc=kc, fc=fc, u=u: h.matmul(
                        ps[:, :], lhsT=W[:, kc, fc * 128:(fc + 1) * 128], rhs=u[:, kc, :],
                        start=(kc == 0), stop=(kc == KC - 1)),
                        r=[W, u], w=[ps], sig=(kc == KC - 1))
                tm = tmp[fi % 2]
                k.op("act", lambda h, ps=ps, tm=tm: h.activation(out=tm[:, :], in_=ps[:, :], func=AF.Relu),
                     r=[ps], w=[tm])
                k.op("dve", lambda h, tm=tm, fi=fi: h.tensor_tensor(out=aT[:, fi, :], in0=tm[:, :], in1=tm[:, :],
                                                                    op=ALU.mult),
                     r=[tm], w=[aT])
        for dc in range(KC):
            W = wd[dc % 2]
            k.dma("pool", W[:, :, :], wdnv[:, :, dc * 128:(dc + 1) * 128], w=[W])
            ps = pb[2 + dc % 2]
            for fc in range(FC):
                k.op("pe", lambda h, ps=ps, W=W, fc=fc: h.matmul(
                    ps[:, :], lhsT=W[:, fc, :], rhs=aT[:, fc, :], start=(fc == 0), stop=(fc == FC - 1)),
                    r=[W, aT], w=[ps], sig=(fc == FC - 1))
            k.op("dve", lambda h, ps=ps, dc=dc: h.tensor_tensor(out=hbuf[:, dc, :], in0=hbuf[:, dc, :],
                                                                in1=ps[:, :], op=ALU.add),
                 r=[ps, hbuf.sub(dc)], w=[hbuf.sub(dc)])
            k.dma("sp", hT[dc * 128:(dc + 1) * 128, t0:t0 + TT], hbuf[:, dc, :],
                  r=[hbuf.sub(dc)], w=[hT.sub((dc, ti))])
    k.barrier()
    k.release(m)


NH = 32
NQK = 16
CH = 64
DN_IN = 12352


def bcast_row(k, name, src, n, q="sp"):
    t = k.sb(name, [128, n], F32)
    k.dma(q, t[:, :], src.partition_broadcast(128), w=[t])
    return t


def cols_via_T(k, name, src2d, R, ident, ps):
    tmp = k.sb(name + "_r", [R, 128], F32)
    out = k.sb(name, [128, R], F32)
    k.dma("sp", tmp[:, :], src2d, w=[tmp])
    k.op("pe", lambda h: h.transpose(ps[:, 0:R], tmp[:, :], ident[0:R, 0:R]), r=[tmp, ident], w=[ps])
    k.op("dve", lambda h: h.tensor_copy(out=out[:, :], in_=ps[:, 0:R]), r=[ps], w=[out])
    return out


def stage_dn_a(k, hT, ntok, seqlen, ident_d, g_d, win_d, convw_d, alog_d, dtb_d, qk_s, v_s, z_s, bg_s):
    m = k.mark()
    pb = k.pb
    ident = k.sb("ident", [128, 128], F32)
    k.dma("sp", ident[:, :], ident_d, w=[ident])
    gcol = cols_via_T(k, "gcol", g_d.rearrange("(c p) -> c p", p=128), KC, ident, pb[7])
    cwv = convw_d.rearrange("j (c p) -> (j c) p", p=128)
    cw0 = cols_via_T(k, "cw0", cwv[0:128, :], 128, ident, pb[7])
    cw1 = cols_via_T(k, "cw1", cwv[128:256, :], 128, ident, pb[7])

    def cwcol(j, c):
        idx = j * 64 + c
        t = cw0 if idx < 128 else cw1
        idx = idx % 128
        return t[:, idx:idx + 1], t

    ones = k.sb("ones", [128, 128], BF16)
    k.op("dve", lambda h: h.memset(ones[:, :], 1.0), w=[ones])
    epsc = k.sb("epsc", [128, 1], F32)
    k.op("dve", lambda h: h.memset(epsc[:, :], EPS), w=[epsc])
    onec = k.sb("onec", [128, 1], F32)
    k.op("dve", lambda h: h.memset(onec[:, :], 1.0), w=[onec])
    nA = bcast_row(k, "nA", alog_d, NH)
    k.op("act", lambda h: h.activation(out=nA[:, :], in_=nA[:, :], func=AF.Exp), r=[nA], w=[nA])
    k.op("dve", lambda h: h.tensor_scalar(out=nA[:, :], in0=nA[:, :], scalar1=-1.0, scalar2=None, op0=ALU.mult),
         r=[nA], w=[nA])
    dtb = bcast_row(k, "dtb", dtb_d, NH)
    hbuf = k.sb("hbuf", [128, KC, TT], F32)
    uT = k.sb("uT", [128, KC, TT], BF16)
    sq = [k.sb(f"sq{i}", [128, TT], BF16) for i in range(2)]
    rstd = k.sb("rstd", [128, TT], F32)
    wt = [k.sb(f"wt{i}", [128, KC, 512], BF16) for i in range(2)]
    wba = k.sb("wba", [128, KC, 64], BF16)
    wt32 = [k.sb(f"wt32_{i}", [128, KC, 512], F32) for i in range(2)]
    hist = k.sb("hist", [128, 64, 3], F32)
    pre = [k.sb(f"pre{i}", [128, TT + 3], F32) for i in range(2)]
    acc = [k.sb(f"acc{i}", [128, TT], F32) for i in range(2)]
    sl = [k.sb(f"sl{i}", [128, TT], F32) for i in range(2)]
    rn = [k.sb(f"rn{i}", [128, TT], F32) for i in range(2)]
    ost = [k.sb(f"ost{i}", [128, TT], F32) for i in range(3)]
    bgt = [k.sb(f"bgt{i}", [128, 64], F32) for i in range(2)]
    bgx = [k.sb(f"bgx{i}", [128, 32], F32) for i in range(2)]
    winv = win_d.rearrange("(kc p) f -> p kc f", p=128)
    k.dma("pool", wba[:, :, :], winv[:, :, 12288:12352], w=[wba])
    ci = 0
    for ti in range(ntok // TT):
        t0 = ti * TT
        rms_to_uT(k, hT, t0, gcol, hbuf, uT, sq, rstd, ones, pb[4], epsc, u32=True)
        if t0 % seqlen == 0:
            k.op("dve", lambda h: h.memset(hist[:, :, :], 0.0), w=[hist])
        for fb in range(24):
            hi = fb < 8
            if hi:
                W = wt32[fb % 2]
                k.dma("act" if fb % 2 else "sp", W[:, :, :], winv[:, :, fb * 512:(fb + 1) * 512], w=[W])
            else:
                W = wt[fb % 2]
                k.dma("pool", W[:, :, :], winv[:, :, fb * 512:(fb + 1) * 512], w=[W])
            for fc in range(4):
                c = fb * 4 + fc
                ps = pb[c % 2]
                for kc in range(KC):
                    if hi:
                        k.op("pe", lambda h, ps=ps, W=W, kc=kc, fc=fc: h.matmul(
                            ps[:, :], lhsT=W[:, kc, fc * 128:(fc + 1) * 128], rhs=hbuf[:, kc, :],
                            start=(kc == 0), stop=(kc == KC - 1)), r=[W, hbuf.sub(kc)], w=[ps], sig=(kc == KC - 1))
                    else:
                        k.op("pe", lambda h, ps=ps, W=W, kc=kc, fc=fc: h.matmul(
                            ps[:, :], lhsT=W[:, kc, fc * 128:(fc + 1) * 128], rhs=uT[:, kc, :],
                            start=(kc == 0), stop=(kc == KC - 1)), r=[W, uT], w=[ps], sig=(kc == KC - 1))
                o_t = ost[ci % 3]
                ci += 1
                if c < 64:
                    P = pre[c % 2]
                    A = acc[c % 2]
                    S = sl[c % 2]
                    k.op("act", lambda h, P=P, c=c: h.copy(out=P[:, 0:3], in_=hist[:, c, :]), r=[hist.sub(c)], w=[P])
                    k.op("act", lambda h, P=P, ps=ps: h.copy(out=P[:, 3:TT + 3], in_=ps[:, :]), r=[ps], w=[P])
                    k.op("act", lambda h, P=P, c=c: h.copy(out=hist[:, c, :], in_=P[:, TT:TT + 3]), r=[P], w=[hist.sub(c)])
                    for j in range(4):
                        col, ct = cwcol(j, c)
                        if j == 0:
                            k.op("dve", lambda h, A=A, P=P, col=col: h.tensor_scalar(
                                out=A[:, :], in0=P[:, 0:TT], scalar1=col, scalar2=None, op0=ALU.mult),
                                r=[P, ct], w=[A])
                        else:
                            k.op("dve", lambda h, A=A, P=P, col=col, j=j: h.scalar_tensor_tensor(
                                out=A[:, :], in0=P[:, j:j + TT], scalar=col, in1=A[:, :], op0=ALU.mult, op1=ALU.add),
                                r=[P, ct, A], w=[A])
                    if c < 32:
                        k.op("act", lambda h, S=S, A=A: h.activation(out=S[:, :], in_=A[:, :], func=AF.Silu), r=[A], w=[S])
                        q2 = sq[c % 2]
                        k.op("act", lambda h, S=S, q2=q2: h.activation(out=q2[:, :], in_=S[:, :], func=AF.Square), r=[S], w=[q2])
                        k.op("pe", lambda h, q2=q2: h.matmul(pb[5][:, :], lhsT=ones[:, :], rhs=q2[:, :], start=True, stop=True),
                             r=[q2, ones], w=[pb[5]])
                        R = rn[c % 2]
                        k.op("act", lambda h, R=R: h.activation(out=R[:, :], in_=pb[5][:, :], func=AF.Sqrt, scale=1.0,
                                                                bias=epsc[:, :]), r=[pb[5], epsc], w=[R])
                        k.op("dve", lambda h, R=R: h.reciprocal(out=R[:, :], in_=R[:, :]), r=[R], w=[R])
                        scl = (128.0 ** -0.5) if c < 16 else 1.0
                        k.op("dve", lambda h, o_t=o_t, S=S, R=R, scl=scl: h.scalar_tensor_tensor(
                            out=o_t[:, :], in0=S[:, :], scalar=scl, in1=R[:, :], op0=ALU.mult, op1=ALU.mult),
                            r=[S, R], w=[o_t])
                        k.dma("sp", qk_s[c, :, t0:t0 + TT], o_t[:, :], r=[o_t], w=[qk_s.sub((c, ti))])
                    else:
                        k.op("act", lambda h, o_t=o_t, A=A: h.activation(out=o_t[:, :], in_=A[:, :], func=AF.Silu), r=[A], w=[o_t])
                        k.dma("sp", v_s[c - 32, :, t0:t0 + TT], o_t[:, :], r=[o_t], w=[v_s.sub((c, ti))])
                else:
                    k.op("act", lambda h, o_t=o_t, ps=ps: h.activation(out=o_t[:, :], in_=ps[:, :], func=AF.Silu), r=[ps], w=[o_t])
                    k.dma("sp", z_s[c - 64, :, t0:t0 + TT], o_t[:, :], r=[o_t], w=[z_s.sub((c, ti))])
        for sbi in range(TT // 128):
            ps = pb[2 + sbi % 2]
            for kc in range(KC):
                k.op("pe", lambda h, ps=ps, kc=kc, sbi=sbi: h.matmul(
                    ps[:, 0:64], lhsT=uT[:, kc, sbi * 128:(sbi + 1) * 128], rhs=wba[:, kc, :],
                    start=(kc == 0), stop=(kc == KC - 1)), r=[uT, wba], w=[ps], sig=(kc == KC - 1))
            B = bgt[sbi % 2]
            X = bgx[sbi % 2]
            k.op("act", lambda h, B=B, ps=ps: h.activation(out=B[:, 0:32], in_=ps[:, 0:32], func=AF.Sigmoid), r=[ps], w=[B])
            k.op("dve", lambda h, X=X, ps=ps: h.tensor_tensor(out=X[:, :], in0=ps[:, 32:64], in1=dtb[:, :], op=ALU.add),
                 r=[ps, dtb], w=[X])
            k.op("act", lambda h, X=X: h.activation(out=X[:, :], in_=X[:, :], func=AF.Exp), r=[X], w=[X])
            k.op("act", lambda h, X=X: h.activation(out=X[:, :], in_=X[:, :], func=AF.Ln, scale=1.0, bias=onec[:, :]),
                 r=[X, onec], w=[X])
            k.op("dve", lambda h, B=B, X=X: h.tensor_tensor(out=B[:, 32:64], in0=X[:, :], in1=nA[:, :], op=ALU.mult),
                 r=[X, nA, B], w=[B])
            k.dma("sp", bg_s[t0 + sbi * 128:t0 + (sbi + 1) * 128, :], B[:, :], r=[B], w=[bg_s.sub((ti, sbi))])
    k.barrier()
    k.release(m)


def bc(ap, shape, axis):
    return ap.unsqueeze(axis).to_broadcast(list(shape))


def make_consts():
    c = np.zeros((128, 320), np.float32)
    c[:, 0:128] = np.eye(128, dtype=np.float32)
    i = np.arange(64)
    c[0:64, 128:192] = (i[:, None] <= i[None, :]).astype(np.float32)
    c[0:64, 192:256] = np.where(i[:, None] > i[None, :], 0.0, -30000.0)
    c[0:64, 256:320] = np.where(i[None, :] >= i[:, None], 0.0, -30000.0)
    return c


def stage_dn_b(k, ntok, seqlen, consts_d, qk_s, v_s, bg_s, o_s):
    m = k.mark()
    pb = k.pb
    cst = k.sb("cst", [128, 320], F32)
    k.dma("sp", cst[:, :], consts_d, w=[cst])
    ident = cst[:, 0:128]
    U = cst[0:64, 128:192]
    NEG12 = cst[0:64, 192:320]
    ones64 = k.sb("ones64", [64, 128], F32)
    k.op("dve", lambda h: h.memset(ones64[:, :], 1.0), w=[ones64])
    negones = k.sb("negones", [64, 64], F32)
    k.op("dve", lambda h: h.memset(negones[:, :], -1.0), w=[negones])
    onesneg = k.sb("onesneg", [64, 128], F32)
    k.op("dve", lambda h: h.memset(onesneg[:, 0:64], 1.0), w=[onesneg])
    k.op("dve", lambda h: h.memset(onesneg[:, 64:128], -1.0), w=[onesneg])
    S = k.sb("S", [128, NH, 128], F32)
    QK = k.sb("QK", [128, NQK, 128], F32)
    Vt = k.sb("Vt", [128, NH, 64], F32)
    bg = k.sb("bg", [64, 64], F32)
    sm = {n: k.sb(n, [64, 32], F32) for n in ("gc", "egc", "ekd", "c1", "nbeta", "tmpd")}
    dS = k.sb("dS", [128, 32], F32)
    kdec = k.sb("kdec", [64, NH, 128], F32)
    Vb = k.sb("Vb", [64, NH, 128], F32)
    GU = k.sb("GU", [64, NH, 64], F32)
    big1 = k.sb("big1", [64, NH, 128], F32)
    Pb = [k.sb(f"P{i}", [64, NH, 64], F32) for i in range(2)]
    Ptb = [k.sb(f"Pt{i}", [64, NH, 64], F32) for i in range(2)]
    R = k.sb("R", [64, NH, 64], F32)
    intraT = k.sb("intraT", [64, NH, 64], F32)
    QdT = k.sb("QdT", [128, NH, 64], F32)
    vn = k.sb("vn", [64, NH, 128], F32)
    ost = k.sb("ost", [128, NH, 64], F32)
    D12 = big1
    Rr = big1
    cstd = [cst]

    def subs(T, keys):
        return [T.sub(x) for x in keys]

    G8 = range(4)
    for c in range(ntok // CH):
        t0 = c * CH
        if t0 % seqlen == 0:
            k.op("dve", lambda h: h.memset(S[:, :, :], 0.0), w=subs(S, range(NH)))
        k.dma("sp", QK[:, :, 0:64], qk_s[16:32, :, t0:t0 + CH].rearrange("c p t -> p c t"), w=[QK])
        k.dma("sp", QK[:, :, 64:128], qk_s[0:16, :, t0:t0 + CH].rearrange("c p t -> p c t"), w=[QK])
        k.dma("sp", Vt[:, :, :], v_s[:, :, t0:t0 + CH].rearrange("c p t -> p c t"), w=[Vt])
        k.dma("sp", bg[:, :], bg_s[t0:t0 + CH, :], w=[bg])
        beta = bg[:, 0:32]
        g = bg[:, 32:64]
        k.op("pe", lambda h: h.matmul(pb[0][0:64, 0:32], lhsT=U, rhs=g, start=True, stop=True), r=[cst, bg], w=[pb[0]])
        k.op("pe", lambda h: h.matmul(pb[1][:, 0:32], lhsT=ones64[:, :], rhs=g, start=True, stop=True), r=[ones64, bg], w=[pb[1]])
        k.op("act", lambda h: h.copy(out=sm["gc"][:, :], in_=pb[0][0:64, 0:32]), r=[pb[0]], w=[sm["gc"]])
        k.op("act", lambda h: h.activation(out=sm["egc"][:, :], in_=pb[0][0:64, 0:32], func=AF.Exp), r=[pb[0]], w=[sm["egc"]])
        k.op("act", lambda h: h.activation(out=dS[:, :], in_=pb[1][:, 0:32], func=AF.Exp), r=[pb[1]], w=[dS])
        k.op("dve", lambda h: h.tensor_tensor(out=sm["tmpd"][:, :], in0=pb[1][0:64, 0:32], in1=sm["gc"][:, :], op=ALU.subtract),
             r=[pb[1], sm["gc"]], w=[sm["tmpd"]])
        k.op("act", lambda h: h.activation(out=sm["ekd"][:, :], in_=sm["tmpd"][:, :], func=AF.Exp), r=[sm["tmpd"]], w=[sm["ekd"]])
        k.op("dve", lambda h: h.scalar_tensor_tensor(out=sm["c1"][:, :], in0=beta, scalar=-1.0, in1=sm["egc"][:, :],
                                                     op0=ALU.mult, op1=ALU.mult), r=[bg, sm["egc"]], w=[sm["c1"]])
        k.op("dve", lambda h: h.tensor_scalar(out=sm["nbeta"][:, :], in0=beta, scalar1=-1.0, scalar2=None, op0=ALU.mult),
             r=[bg], w=[sm["nbeta"]])
        k.op("dve", lambda h: h.tensor_tensor(out=GU[:, :, :], in0=bc(U, [64, NH, 64], 1), in1=bc(g, [64, NH, 64], 2),
                                              op=ALU.mult), r=[cst, bg], w=[GU])
        for b4 in range(8):
            ps = pb[2 + b4 % 4]
            for hh in range(4):
                h_ = b4 * 4 + hh
                o0 = hh * 128
                k.op("pe", lambda h, ps=ps, h_=h_, o0=o0: h.matmul(ps[0:64, o0:o0 + 128], lhsT=GU[:, h_, :], rhs=onesneg[:, :],
                                                                   start=True, stop=False), r=[GU, onesneg], w=[ps], sig=False)
                k.op("pe", lambda h, ps=ps, h_=h_, o0=o0: h.matmul(ps[0:64, o0:o0 + 64], lhsT=negones[:, :], rhs=GU[:, h_, :],
                                                                   start=False, stop=False), r=[GU, negones], w=[ps], sig=False)
                k.op("pe", lambda h, ps=ps, h_=h_, o0=o0: h.matmul(ps[0:64, o0 + 64:o0 + 128], lhsT=ones64[:, 0:64], rhs=GU[:, h_, :],
                                                                   start=False, stop=False), r=[GU, ones64], w=[ps], sig=False)
                k.op("pe", lambda h, ps=ps, h_=h_, o0=o0: h.matmul(ps[0:64, o0:o0 + 128], lhsT=ident[0:64, 0:64], rhs=NEG12,
                                                                   start=False, stop=True), r=[cst], w=[ps], sig=(hh == 3))
            k.op("act", lambda h, ps=ps, b4=b4: h.activation(out=D12[:, b4 * 4:b4 * 4 + 4, :],
                                                             in_=ps[0:64, :].rearrange("p (a b) -> p a b", a=4), func=AF.Exp),
                 r=[ps], w=[D12.sub(b4)])
        N = Ptb[0]
        M = Pb[0]
        for g8 in G8:
            ps = pb[g8 % 2]
            for q4 in range(4):
                hq = g8 * 4 + q4
                k.op("pe", lambda h, ps=ps, hq=hq, q4=q4: h.matmul(ps[0:64, q4 * 128:(q4 + 1) * 128], lhsT=QK[:, hq, 0:64],
                                                                   rhs=QK[:, hq, :], start=True, stop=True),
                     r=[QK], w=[ps], sig=(q4 == 3))
            psv = ps[0:64, :].rearrange("p (a b) -> p a b", a=4)
            for par in range(2):
                hs = slice(g8 * 8 + par, g8 * 8 + 8, 2)
                k.op("dve", lambda h, psv=psv, hs=hs: h.tensor_tensor(out=N[:, hs, :], in0=psv[:, :, 0:64], in1=D12[:, hs, 0:64],
                                                                      op=ALU.mult),
                     r=[ps, D12.sub(2 * g8), D12.sub(2 * g8 + 1)], w=[N.sub(g8)])
                k.op("dve", lambda h, psv=psv, hs=hs: h.tensor_tensor(out=intraT[:, hs, :], in0=psv[:, :, 64:128],
                                                                      in1=D12[:, hs, 64:128], op=ALU.mult),
                     r=[ps, D12.sub(2 * g8), D12.sub(2 * g8 + 1)], w=[intraT.sub(g8)])
            k.op("dve", lambda h, g8=g8: h.tensor_tensor(out=N[:, g8 * 8:g8 * 8 + 8, :], in0=N[:, g8 * 8:g8 * 8 + 8, :],
                                                         in1=bc(sm["nbeta"][:, g8 * 8:g8 * 8 + 8], [64, 8, 64], 2), op=ALU.mult),
                 r=[N.sub(g8), sm["nbeta"]], w=[N.sub(g8)])
        for g8 in G8:
            ps = pb[2 + g8 % 2]
            for i8 in range(8):
                h_ = g8 * 8 + i8
                k.op("pe", lambda h, ps=ps, h_=h_, i8=i8: h.transpose(ps[0:64, i8 * 64:(i8 + 1) * 64], N[:, h_, :], ident[0:64, 0:64]),
                     r=[N.sub(g8), cst], w=[ps], sig=(i8 == 7))
            k.op("act", lambda h, ps=ps, g8=g8: h.copy(out=M[:, g8 * 8:g8 * 8 + 8, :], in_=ps[0:64, :].rearrange("p (a b) -> p a b", a=8)),
                 r=[ps], w=[M.sub(g8)])
            k.op("dve", lambda h, g8=g8: h.tensor_tensor(out=R[:, g8 * 8:g8 * 8 + 8, :], in0=M[:, g8 * 8:g8 * 8 + 8, :],
                                                         in1=bc(ident[0:64, 0:64], [64, 8, 64], 1), op=ALU.add),
                 r=[M.sub(g8), cst], w=[R.sub(g8)])
        cur = 0
        for lvl in range(1, 6):
            Po, Pto = Pb[cur], Ptb[cur]
            Pn, Ptn = Pb[1 - cur], Ptb[1 - cur]
            for g8 in G8:
                psP = pb[0 + g8 % 2]
                psT = pb[2 + g8 % 2]
                psR = pb[4 + g8 % 2]
                for i8 in range(8):
                    h_ = g8 * 8 + i8
                    cs = slice(i8 * 64, (i8 + 1) * 64)
                    if lvl < 5:
                        k.op("pe", lambda h, psP=psP, h_=h_, cs=cs, Po=Po, Pto=Pto: h.matmul(
                            psP[0:64, cs], lhsT=Pto[:, h_, :], rhs=Po[:, h_, :], start=True, stop=True),
                            r=[Po.sub(g8), Pto.sub(g8)], w=[psP], sig=(i8 == 7))
                    k.op("pe", lambda h, psT=psT, h_=h_, cs=cs, Po=Po, Pto=Pto: h.matmul(
                        psT[0:64, cs], lhsT=Po[:, h_, :], rhs=Pto[:, h_, :], start=True, stop=True),
                        r=[Po.sub(g8), Pto.sub(g8)], w=[psT], sig=(i8 == 7))
                if lvl < 5:
                    k.op("act", lambda h, psP=psP, g8=g8, Pn=Pn: h.copy(out=Pn[:, g8 * 8:g8 * 8 + 8, :],
                                                                        in_=psP[0:64, :].rearrange("p (a b) -> p a b", a=8)),
                         r=[psP], w=[Pn.sub(g8)])
                k.op("dve", lambda h, psT=psT, g8=g8, Ptn=Ptn: h.tensor_copy(out=Ptn[:, g8 * 8:g8 * 8 + 8, :],
                                                                             in_=psT[0:64, :].rearrange("p (a b) -> p a b", a=8)),
                     r=[psT], w=[Ptn.sub(g8)])
                for i8 in range(8):
                    h_ = g8 * 8 + i8
                    cs = slice(i8 * 64, (i8 + 1) * 64)
                    k.op("pe", lambda h, psR=psR, h_=h_, cs=cs, Ptn=Ptn: h.matmul(
                        psR[0:64, cs], lhsT=Ptn[:, h_, :], rhs=R[:, h_, :], start=True, stop=True),
                        r=[Ptn.sub(g8), R.sub(g8)], w=[psR], sig=(i8 == 7))
                k.op("dve", lambda h, psR=psR, g8=g8: h.tensor_tensor(out=R[:, g8 * 8:g8 * 8 + 8, :], in0=R[:, g8 * 8:g8 * 8 + 8, :],
                                                                      in1=psR[0:64, :].rearrange("p (a b) -> p a b", a=8), op=ALU.add),
                     r=[psR, R.sub(g8)], w=[R.sub(g8)])
            cur = 1 - cur
        DG = GU
        k.op("dve", lambda h: h.tensor_tensor(out=DG[:, :, :], in0=bc(ident[0:64, 0:64], [64, NH, 64], 1),
                                              in1=bc(sm["egc"][:, :], [64, NH, 64], 2), op=ALU.mult),
             r=[cst, sm["egc"]], w=[DG])
        for g8 in G8:
            ps = pb[6 + g8 % 2]
            for i8 in range(8):
                h_ = g8 * 8 + i8
                k.op("pe", lambda h, ps=ps, h_=h_, i8=i8: h.matmul(ps[:, i8 * 64:(i8 + 1) * 64], lhsT=ones64[:, :], rhs=DG[:, h_, :],
                                                                   start=True, stop=True), r=[ones64, DG], w=[ps], sig=(i8 == 7))
            psv = ps[:, :].rearrange("p (a b) -> p a b", a=8)
            for par in range(2):
                hs = slice(g8 * 8 + par, g8 * 8 + 8, 2)
                k.op("dve", lambda h, psv=psv, hs=hs, g8=g8, par=par: h.tensor_tensor(
                    out=QdT[:, hs, :], in0=QK[:, g8 * 4:g8 * 4 + 4, 64:128], in1=psv[:, par:8:2, :], op=ALU.mult),
                    r=[ps, QK], w=[QdT.sub(g8)])
        for g8 in G8:
            ps = pb[g8 % 2]
            for q4 in range(4):
                hq = g8 * 4 + q4
                k.op("pe", lambda h, ps=ps, hq=hq, q4=q4: h.transpose(ps[0:64, q4 * 128:(q4 + 1) * 128], QK[:, hq, 0:64], ident),
                     r=[QK, cst], w=[ps], sig=(q4 == 3))
            psv = ps[0:64, :].rearrange("p (a b) -> p a b", a=4)
            for par in range(2):
                hs = slice(g8 * 8 + par, g8 * 8 + 8, 2)
                k.op("dve", lambda h, psv=psv, hs=hs: h.tensor_tensor(out=kdec[:, hs, :], in0=psv,
                                                                      in1=bc(sm["ekd"][:, hs], [64, 4, 128], 2), op=ALU.mult),
                     r=[ps, sm["ekd"]], w=[kdec.sub(g8)])
        for b4 in range(8):
            ps = pb[2 + b4 % 2]
            for hh in range(4):
                h_ = b4 * 4 + hh
                k.op("pe", lambda h, ps=ps, h_=h_, hh=hh: h.transpose(ps[0:64, hh * 128:(hh + 1) * 128], Vt[:, h_, :], ident),
                     r=[Vt, cst], w=[ps], sig=(hh == 3))
            k.op("dve", lambda h, ps=ps, b4=b4: h.tensor_tensor(out=Vb[:, b4 * 4:b4 * 4 + 4, :],
                                                                in0=ps[0:64, :].rearrange("p (a b) -> p a b", a=4),
                                                                in1=bc(beta[:, b4 * 4:b4 * 4 + 4], [64, 4, 128], 2), op=ALU.mult),
                 r=[ps, bg], w=[Vb.sub(b4)])
        for h_ in range(NH):
            hq = h_ // 2
            g8 = h_ // 8
            ks = pb[0][0:64, (h_ % 4) * 128:(h_ % 4 + 1) * 128]
            ksd = pb[0].sub(h_ % 4)
            vp = pb[1][0:64, (h_ % 4) * 128:(h_ % 4 + 1) * 128]
            vpd = pb[1].sub(h_ % 4)
            dsb = pb[2 + (h_ % 8) // 4]
            dsp = dsb[:, (h_ % 4) * 128:(h_ % 4 + 1) * 128]
            dsd = dsb.sub(h_ % 4)
            ob = pb[4 + g8]
            op_ = ob[:, (h_ % 8) * 64:(h_ % 8 + 1) * 64]
            k.op("pe", lambda h, ks=ks, hq=hq, h_=h_: h.matmul(ks, lhsT=QK[:, hq, 0:64], rhs=S[:, h_, :], start=True, stop=True),
                 r=[QK, S.sub(h_)], w=[ksd])
            k.op("dve", lambda h, ks=ks, h_=h_: h.scalar_tensor_tensor(out=Rr[:, h_, :], in0=ks, scalar=sm["c1"][:, h_:h_ + 1],
                                                                       in1=Vb[:, h_, :], op0=ALU.mult, op1=ALU.add),
                 r=[ksd, sm["c1"], Vb.sub(h_ // 4)], w=[Rr.sub(("r", h_))])
            k.op("pe", lambda h, vp=vp, h_=h_: h.matmul(vp, lhsT=R[:, h_, :], rhs=Rr[:, h_, :], start=True, stop=True),
                 r=[R.sub(g8), Rr.sub(("r", h_))], w=[vpd])
            k.op("act", lambda h, vp=vp, h_=h_: h.copy(out=vn[:, h_, :], in_=vp), r=[vpd], w=[vn.sub(h_)])
            k.op("pe", lambda h, op_=op_, h_=h_: h.matmul(op_, lhsT=S[:, h_, :], rhs=QdT[:, h_, :], start=True, stop=False),
                 r=[S.sub(h_), QdT.sub(g8)], w=[ob.sub(h_ % 8)], sig=False)
            k.op("pe", lambda h, op_=op_, h_=h_: h.matmul(op_, lhsT=vn[:, h_, :], rhs=intraT[:, h_, :], start=False, stop=True),
                 r=[vn.sub(h_), intraT.sub(g8)], w=[ob.sub(h_ % 8)])
            k.op("pe", lambda h, dsp=dsp, h_=h_: h.matmul(dsp, lhsT=kdec[:, h_, :], rhs=vn[:, h_, :], start=True, stop=True),
                 r=[kdec.sub(g8), vn.sub(h_)], w=[dsd])
            k.op("dve", lambda h, dsp=dsp, h_=h_: h.scalar_tensor_tensor(out=S[:, h_, :], in0=S[:, h_, :], scalar=dS[:, h_:h_ + 1],
                                                                         in1=dsp, op0=ALU.mult, op1=ALU.add),
                 r=[dsd, dS, S.sub(h_)], w=[S.sub(h_)])
            if h_ % 8 == 7:
                k.op("act", lambda h, ob=ob, g8=g8: h.copy(out=ost[:, g8 * 8:g8 * 8 + 8, :],
                                                           in_=ob[:, :].rearrange("p (a b) -> p a b", a=8)),
                     r=[ob.sub(i) for i in range(8)], w=[ost.sub(g8)])
                k.dma("sp", o_s[g8 * 8:g8 * 8 + 8, :, t0:t0 + CH].rearrange("c p t -> p c t"), ost[:, g8 * 8:g8 * 8 + 8, :],
                      r=[ost.sub(g8)], w=[o_s.sub((g8, c))])
    k.barrier()
    k.release(m)


def stage_dn_c(k, hT, ntok, o_s, z_s, onorm_d, wout_d, ident_d):
    m = k.mark()
    pb = k.pb
    ident = k.sb("ident", [128, 128], F32)
    k.dma("sp", ident[:, :], ident_d, w=[ident])
    oncol = cols_via_T(k, "oncol", onorm_d.rearrange("(c p) -> c p", p=128), 1, ident, pb[7])
    ones = k.sb("ones", [128, 128], BF16)
    k.op("dve", lambda h: h.memset(ones[:, :], 1.0), w=[ones])
    epsc = k.sb("epsc", [128, 1], F32)
    k.op("dve", lambda h: h.memset(epsc[:, :], EPS), w=[epsc])
    yT = k.sb("yT", [128, NH, TT], BF16)
    ob = [k.sb(f"ob{i}", [128, TT], F32) for i in range(2)]
    zb = [k.sb(f"zb{i}", [128, TT], F32) for i in range(2)]
    sq = [k.sb(f"sq{i}", [128, TT], BF16) for i in range(2)]
    rn = [k.sb(f"rn{i}", [128, TT], F32) for i in range(2)]
    hb = [k.sb(f"hb{i}", [128, TT], F32) for i in range(2)]
    wd = [k.sb(f"wd{i}", [128, NH, 128], BF16) for i in range(2)]
    woutv = wout_d.rearrange("(fc p) d -> p fc d", p=128)
    for ti in range(ntok // TT):
        t0 = ti * TT
        for h_ in range(NH):
            O = ob[h_ % 2]
            Z = zb[h_ % 2]
            Q2 = sq[h_ % 2]
            Rn = rn[h_ % 2]
            k.dma("sp", O[:, :], o_s[h_, :, t0:t0 + TT], w=[O])
            k.dma("act", Z[:, :], z_s[h_, :, t0:t0 + TT], w=[Z])
            k.op("act", lambda h, O=O, Q2=Q2: h.activation(out=Q2[:, :], in_=O[:, :], func=AF.Square), r=[O], w=[Q2])
            ps = pb[4 + h_ % 2]
            k.op("pe", lambda h, ps=ps, Q2=Q2: h.matmul(ps[:, :], lhsT=ones[:, :], rhs=Q2[:, :], start=True, stop=True),
                 r=[Q2, ones], w=[ps])
            k.op("act", lambda h, ps=ps, Rn=Rn: h.activation(out=Rn[:, :], in_=ps[:, :], func=AF.Sqrt, scale=1.0 / 128, bias=epsc[:, :]),
                 r=[ps, epsc], w=[Rn])
            k.op("dve", lambda h, Rn=Rn: h.reciprocal(out=Rn[:, :], in_=Rn[:, :]), r=[Rn], w=[Rn])
            k.op("dve", lambda h, O=O, Rn=Rn: h.scalar_tensor_tensor(out=O[:, :], in0=O[:, :], scalar=oncol[:, 0:1], in1=Rn[:, :],
                                                                     op0=ALU.mult, op1=ALU.mult), r=[O, Rn, oncol], w=[O])
            k.op("dve", lambda h, O=O, Z=Z, h_=h_: h.tensor_tensor(out=yT[:, h_, :], in0=O[:, :], in1=Z[:, :], op=ALU.mult),
                 r=[O, Z], w=[yT])
        for dc in range(KC):
            W = wd[dc % 2]
            k.dma("pool", W[:, :, :], woutv[:, :, dc * 128:(dc + 1) * 128], w=[W])
            H = hb[dc % 2]
            k.dma("sp", H[:, :], hT[dc * 128:(dc + 1) * 128, t0:t0 + TT], r=[hT.sub((dc, ti))], w=[H])
            ps = pb[dc % 2]
            for fc in range(NH):
                k.op("pe", lambda h, ps=ps, W=W, fc=fc: h.matmul(ps[:, :], lhsT=W[:, fc, :], rhs=yT[:, fc, :],
                                                                 start=(fc == 0), stop=(fc == NH - 1)),
                     r=[W, yT], w=[ps], sig=(fc == NH - 1))
            k.op("dve", lambda h, ps=ps, H=H: h.tensor_tensor(out=H[:, :], in0=H[:, :], in1=ps[:, :], op=ALU.add),
                 r=[ps, H], w=[H])
            k.dma("sp", hT[dc * 128:(dc + 1) * 128, t0:t0 + TT], H[:, :], r=[H], w=[hT.sub((dc, ti))])
    k.barrier()
    k.release(m)


def dn_layer(k, hT, ntok, seqlen, consts_d, sc, p):
    stage_dn_a(k, hT, ntok, seqlen, consts_d[:, 0:128], p["g"], p["w_in"], p["conv_w"], p["a_log"], p["dt_bias"],
               sc["qk"], sc["v"], sc["z"], sc["bg"])
    stage_dn_b(k, ntok, seqlen, consts_d, sc["qk"], sc["v"], sc["bg"], sc["o"])
    stage_dn_c(k, hT, ntok, sc["o"], sc["z"], p["out_norm"], p["w_out"], consts_d[:, 0:128])


NEGM = -30000.0


def fm_rmsnorm(k, src, n, width, gcols, out, sq, pss, rn, ones, epsc, tag):
    for c in range(n):
        k.op("act", lambda h, c=c: h.activation(out=sq[c % 2][:, :], in_=src[:, c, :], func=AF.Square), r=[src], w=[sq[c % 2]])
        k.op("pe", lambda h, c=c: h.matmul(pss[:, :], lhsT=ones[:, :], rhs=sq[c % 2][:, :], start=(c == 0), stop=(c == n - 1)),
             r=[sq[c % 2], ones], w=[pss])
    k.op("act", lambda h: h.activation(out=rn[:, :], in_=pss[:, :], func=AF.Sqrt, scale=1.0 / width, bias=epsc[:, :]),
         r=[pss, epsc], w=[rn])
    k.op("dve", lambda h: h.reciprocal(out=rn[:, :], in_=rn[:, :]), r=[rn], w=[rn])
    for c in range(n):
        k.op("dve", lambda h, c=c: h.scalar_tensor_tensor(out=out[:, c, :], in0=src[:, c, :], scalar=gcols[:, c:c + 1], in1=rn[:, :],
                                                          op0=ALU.mult, op1=ALU.mult), r=[src, rn, gcols], w=[out])


def stage_dsa_a(k, hT, ntok, ident_d, p, sc):
    m = k.mark()
    pb = k.pb
    ident = k.sb("ident", [128, 128], F32)
    k.dma("sp", ident[:, :], ident_d, w=[ident])
    identb = k.sb("identb", [128, 128], BF16)
    k.op("dve", lambda h: h.tensor_copy(out=identb[:, :], in_=ident[:, :]), r=[ident], w=[identb])
    gcol = cols_via_T(k, "gcol", p["g"].rearrange("(c p) -> c p", p=128), KC, ident, pb[7])
    qncol = cols_via_T(k, "qncol", p["q_norm"].rearrange("(c p) -> c p", p=128), 4, ident, pb[7])
    kvcol = cols_via_T(k, "kvcol", p["kv_norm"].rearrange("(c p) -> c p", p=128), 2, ident, pb[7])
    kicol = cols_via_T(k, "kicol", p["kidx_norm"].rearrange("(c p) -> c p", p=128), 1, ident, pb[7])
    ones = k.sb("ones", [128, 128], BF16)
    k.op("dve", lambda h: h.memset(ones[:, :], 1.0), w=[ones])
    epsc = k.sb("epsc", [128, 1], F32)
    k.op("dve", lambda h: h.memset(epsc[:, :], EPS), w=[epsc])
    hbuf = k.sb("hbuf", [128, KC, TT], F32)
    uT = k.sb("uT", [128, KC, TT], BF16)
    sq = [k.sb(f"sq{i}", [128, TT], BF16) for i in range(2)]
    rstd = k.sb("rstd", [128, TT], F32)
    Wd = k.sb("Wd", [128, KC, 912], BF16)
    k.dma("pool", Wd[:, :, :], p["w_in"].rearrange("(kc p) f -> p kc f", p=128), w=[Wd])
    Wuq = k.sb("Wuq", [128, 4, 4096], BF16)
    k.dma("pool", Wuq[:, :, :], p["w_uq"].rearrange("(kc p) f -> p kc f", p=128), w=[Wuq])
    Wuk = k.sb("Wuk", [128, 16, 256], BF16)
    k.dma("pool", Wuk[:, :, :], p["w_uk"].rearrange("h d r -> d h r"), w=[Wuk])
    pj = k.sb("pj", [128, 7, TT], F32)
    qn = k.sb("qn", [128, 4, TT], BF16)
    ckn = k.sb("ckn", [128, 2, TT], BF16)
    kin = k.sb("kin", [128, 1, TT], BF16)
    rn = k.sb("rn", [128, TT], F32)
    ctm = [k.sb(f"ctm{i}", [128, 256], BF16) for i in range(2)]
    wi = [k.sb(f"wi{i}", [128, 32], F32) for i in range(2)]
    qT = [k.sb(f"qT{i}", [128, TT], BF16) for i in range(2)]
    ob = [k.sb(f"ob{i}", [128, TT], BF16) for i in range(3)]
    oi = 0
    wscale = (16.0 ** -0.5) * (128.0 ** -0.5)
    for ti in range(ntok // TT):
        t0 = ti * TT
        rms_to_uT(k, hT, t0, gcol, hbuf, uT, sq, rstd, ones, pb[4], epsc)
        for fc in range(7):
            ps = pb[fc % 2]
            for kc in range(KC):
                k.op("pe", lambda h, ps=ps, kc=kc, fc=fc: h.matmul(ps[:, :], lhsT=Wd[:, kc, fc * 128:(fc + 1) * 128], rhs=uT[:, kc, :],
                                                                   start=(kc == 0), stop=(kc == KC - 1)),
                     r=[Wd, uT], w=[ps], sig=(kc == KC - 1))
            k.op("act", lambda h, ps=ps, fc=fc: h.copy(out=pj[:, fc, :], in_=ps[:, :]), r=[ps], w=[pj])
        fm_rmsnorm(k, Tl(pj[:, 0:4, :]) if False else pj, 4, 512, qncol, qn, sq, pb[5], rn, ones, epsc, "q")
        pj_kv = Tl(pj[:, 4:6, :]); pj_kv.d = pj.d
        pj_ki = Tl(pj[:, 6:7, :]); pj_ki.d = pj.d
        fm_rmsnorm(k, pj_kv, 2, 256, kvcol, ckn, sq, pb[5], rn, ones, epsc, "kv")
        fm_rmsnorm(k, pj_ki, 1, 128, kicol, kin, sq, pb[5], rn, ones, epsc, "ki")
        k.dma("sp", sc["kidx"][:, t0:t0 + TT], kin[:, 0, :], r=[kin], w=[sc["kidx"].sub(ti)])
        for rc in range(2):
            k.dma("sp", sc["ckvT"][rc, :, t0:t0 + TT], ckn[:, rc, :], r=[ckn], w=[sc["ckvT"].sub((rc, ti))])
        for tb in range(TT // 128):
            psb = pb[2 + tb % 2]
            pv = psb[:, :].bitcast(BF16)
            for rc in range(2):
                k.op("pe", lambda h, pv=pv, rc=rc, tb=tb: h.transpose(pv[:, rc * 128:(rc + 1) * 128], ckn[:, rc, tb * 128:(tb + 1) * 128],
                                                                      identb[:, :]), r=[ckn, identb], w=[psb], sig=(rc == 1))
            C = ctm[tb % 2]
            k.op("act", lambda h, pv=pv, C=C: h.copy(out=C[:, :], in_=pv[:, 0:256]), r=[psb], w=[C])
            k.dma("sp", sc["ckvtm"][t0 + tb * 128:t0 + (tb + 1) * 128, :], C[:, :], r=[C], w=[sc["ckvtm"].sub((ti, tb))])
            ps = pb[6 + tb % 2]
            for kc in range(KC):
                k.op("pe", lambda h, ps=ps, kc=kc, tb=tb: h.matmul(ps[:, 0:16], lhsT=uT[:, kc, tb * 128:(tb + 1) * 128], rhs=Wd[:, kc, 896:912],
                                                                   start=(kc == 0), stop=(kc == KC - 1)),
                     r=[uT, Wd], w=[ps], sig=(kc == KC - 1))
            W_ = wi[tb % 2]
            k.op("act", lambda h, ps=ps, W_=W_: h.activation(out=W_[:, 0:16], in_=ps[:, 0:16], func=AF.Abs, scale=wscale), r=[ps], w=[W_])
            k.op("act", lambda h, ps=ps, W_=W_: h.activation(out=W_[:, 16:32], in_=ps[:, 0:16], func=AF.Sign), r=[ps, W_], w=[W_])
            k.dma("sp", sc["widx"][t0 + tb * 128:t0 + (tb + 1) * 128, :], W_[:, :], r=[W_], w=[sc["widx"].sub((ti, tb))])
        for c in range(32):
            ps = pb[c % 2]
            for kc in range(4):
                k.op("pe", lambda h, ps=ps, kc=kc, c=c: h.matmul(ps[:, :], lhsT=Wuq[:, kc, c * 128:(c + 1) * 128], rhs=qn[:, kc, :],
                                                                 start=(kc == 0), stop=(kc == 3)), r=[Wuq, qn], w=[ps], sig=(kc == 3))
            if c >= 16:
                O = ob[oi % 3]; oi += 1
                k.op("act", lambda h, ps=ps, O=O: h.copy(out=O[:, :], in_=ps[:, :]), r=[ps], w=[O])
                k.dma("sp", sc["qidxT"][c - 16, :, t0:t0 + TT], O[:, :], r=[O], w=[sc["qidxT"].sub((c, ti))])
            else:
                Q = qT[c % 2]
                k.op("act", lambda h, ps=ps, Q=Q: h.copy(out=Q[:, :], in_=ps[:, :]), r=[ps], w=[Q])
                for rc in range(2):
                    ps2 = pb[2 + rc]
                    k.op("pe", lambda h, ps2=ps2, Q=Q, c=c, rc=rc: h.matmul(ps2[:, :], lhsT=Wuk[:, c, rc * 128:(rc + 1) * 128], rhs=Q[:, :],
                                                                            start=True, stop=True), r=[Wuk, Q], w=[ps2])
                    O = ob[oi % 3]; oi += 1
                    k.op("dve", lambda h, ps2=ps2, O=O: h.tensor_scalar(out=O[:, :], in0=ps2[:, :], scalar1=128.0 ** -0.5, scalar2=None,
                                                                        op0=ALU.mult), r=[ps2], w=[O])
                    k.dma("sp", sc["qabsT"][c, rc, :, t0:t0 + TT], O[:, :], r=[O], w=[sc["qabsT"].sub((c, rc, ti))])
    k.barrier()
    k.release(m)


def stage_dsa_b(k, hT, ntok, seqlen, ident_d, p, sc):
    m = k.mark()
    pb = k.pb
    ident = k.sb("ident", [128, 128], F32)
    k.dma("sp", ident[:, :], ident_d, w=[ident])
    identb = k.sb("identb", [128, 128], BF16)
    k.op("dve", lambda h: h.tensor_copy(out=identb[:, :], in_=ident[:, :]), r=[ident], w=[identb])
    ones = k.sb("ones", [128, 128], BF16)
    k.op("dve", lambda h: h.memset(ones[:, :], 1.0), w=[ones])
    negd = k.sb("negd", [128, 128], F32)
    k.op("dve", lambda h: h.memset(negd[:, :], 0.0), w=[negd])
    k.op("dve", lambda h: h.memset(negd[0:64, 64:128], NEGM), r=[negd], w=[negd])
    Wout = k.sb("Wout", [128, 16, 2048], BF16)
    k.dma("pool", Wout[:, :, :], p["w_out"].rearrange("(c p) d -> p c d", p=128), w=[Wout])
    Wuv = k.sb("Wuv", [128, 16, 2, 128], BF16)
    for rc in range(2):
        k.dma("pool", Wuv[:, :, rc, :], p["w_uv"][:, rc * 128:(rc + 1) * 128, :].rearrange("h r d -> r h d"), w=[Wuv])
    nkb_all = seqlen // 128
    kidxT = k.sb("kidxT", [128, seqlen], BF16)
    ckvT = k.sb("ckvT", [128, 2, seqlen], BF16)
    ckvtm = k.sb("ckvtm", [128, nkb_all, 256], BF16)
    qidxT = k.sb("qidxT", [128, 16, 128], BF16)
    qabsT = k.sb("qabsT", [128, 16, 2, 128], BF16)
    widx = k.sb("widx", [128, 32], F32)
    score = k.sb("score", [128, seqlen], F32)
    work = k.sb("work", [128, seqlen], F32)
    m8 = k.sb("m8", [128, 8], F32)
    mask01 = k.sb("mask01", [128, seqlen], BF16)
    maskT = k.sb("maskT", [128, nkb_all, 128], BF16)
    rl = [k.sb(f"rl{i}", [128, 512], F32) for i in range(2)]
    pT = [k.sb(f"pT{i}", [128, 512], BF16) for i in range(2)]
    pTm = [k.sb(f"pTm{i}", [128, 512], BF16) for i in range(2)]
    rs = k.sb("rs", [128, 512], F32)
    olatT = k.sb("olatT", [128, 16, 2, 128], BF16)
    oT = k.sb("oT", [128, 16, 128], BF16)
    H = [k.sb(f"H{i}", [128, 4, 128], F32) for i in range(2)]
    li = 0
    for b in range(ntok // seqlen):
        s0 = b * seqlen
        k.dma("sp", kidxT[:, :], sc["kidx"][:, s0:s0 + seqlen], r=[sc["kidx"].sub(i) for i in range(ntok // TT)], w=[kidxT])
        for rc in range(2):
            k.dma("sp", ckvT[:, rc, :], sc["ckvT"][rc, :, s0:s0 + seqlen],
                  r=[sc["ckvT"].sub((rc, i)) for i in range(ntok // TT)], w=[ckvT])
        k.dma("sp", ckvtm[:, :, :], sc["ckvtm"][s0:s0 + seqlen, :].rearrange("(kb p) r -> p kb r", p=128),
              r=[sc["ckvtm"].sub((i, j)) for i in range(ntok // TT) for j in range(4)], w=[ckvtm])
        for qb in range(seqlen // 128):
            tq = s0 + qb * 128
            ti = tq // TT
            nkb = qb + 1
            n = nkb * 128
            k.dma("sp", qidxT[:, :, :], sc["qidxT"][:, :, tq:tq + 128].rearrange("h p t -> p h t"),
                  r=[sc["qidxT"].sub((c, ti)) for c in range(16, 32)], w=[qidxT])
            for rc in range(2):
                k.dma("sp", qabsT[:, :, rc, :], sc["qabsT"][:, rc, :, tq:tq + 128].rearrange("h p t -> p h t"),
                      r=[sc["qabsT"].sub((c, rc, ti)) for c in range(16)], w=[qabsT])
            k.dma("sp", widx[:, :], sc["widx"][tq:tq + 128, :], r=[sc["widx"].sub((ti, (tq % TT) // 128))], w=[widx])
            for kg in range((n + 511) // 512):
                wdt = min(512, n - kg * 512)
                cs = slice(kg * 512, kg * 512 + wdt)
                for h_ in range(16):
                    ps = pb[h_ % 2]
                    k.op("pe", lambda h, ps=ps, h_=h_, cs=cs, wdt=wdt: h.matmul(ps[:, 0:wdt], lhsT=qidxT[:, h_, :], rhs=kidxT[:, cs],
                                                                                start=True, stop=True), r=[qidxT, kidxT], w=[ps])
                    T_ = rl[li % 2]; li += 1
                    k.op("act", lambda h, ps=ps, T_=T_, h_=h_, wdt=wdt: h.activation(out=T_[:, 0:wdt], in_=ps[:, 0:wdt], func=AF.Relu,
                                                                                     scale=widx[:, h_:h_ + 1]), r=[ps, widx], w=[T_])
                    if h_ == 0:
                        k.op("dve", lambda h, T_=T_, cs=cs, wdt=wdt: h.tensor_scalar(out=score[:, cs], in0=T_[:, 0:wdt],
                                                                                     scalar1=widx[:, 16:17], scalar2=None, op0=ALU.mult),
                             r=[T_, widx], w=[score])
                    else:
                        k.op("dve", lambda h, T_=T_, cs=cs, wdt=wdt, h_=h_: h.scalar_tensor_tensor(
                            out=score[:, cs], in0=T_[:, 0:wdt], scalar=widx[:, 16 + h_:17 + h_], in1=score[:, cs],
                            op0=ALU.mult, op1=ALU.add), r=[T_, widx, score], w=[score])
            k.op("dve", lambda h, n=n: h.tensor_tensor(out=score[:, n - 128:n], in0=score[:, n - 128:n], in1=negd[:, :], op=ALU.add),
                 r=[score, negd], w=[score])
            if n > 256:
                k.op("dve", lambda h, n=n: h.tensor_copy(out=work[:, 0:n], in_=score[:, 0:n]), r=[score], w=[work])
                for rnd in range(32):
                    k.op("dve", lambda h, n=n: h.max(out=m8[:, :], in_=work[:, 0:n]), r=[work], w=[m8])
                    if rnd < 31:
                        k.op("dve", lambda h, n=n: h.match_replace(out=work[:, 0:n], in_to_replace=m8[:, :], in_values=work[:, 0:n],
                                                                   imm_value=-1e9), r=[work, m8], w=[work])
                k.op("dve", lambda h, n=n: h.tensor_scalar(out=mask01[:, 0:n], in0=score[:, 0:n], scalar1=m8[:, 7:8], scalar2=None,
                                                           op0=ALU.is_ge), r=[score, m8], w=[mask01])
            else:
                k.op("dve", lambda h, n=n: h.tensor_scalar(out=mask01[:, 0:n], in0=score[:, 0:n], scalar1=-10000.0, scalar2=None,
                                                           op0=ALU.is_ge), r=[score], w=[mask01])
            for kb in range(nkb):
                psb = pb[kb % 2]
                pv = psb[:, :].bitcast(BF16)
                k.op("pe", lambda h, pv=pv, kb=kb: h.transpose(pv[:, 0:128], mask01[:, kb * 128:(kb + 1) * 128], identb[:, :]),
                     r=[mask01, identb], w=[psb])
                k.op("act", lambda h, pv=pv, kb=kb: h.copy(out=maskT[:, kb, :], in_=pv[:, 0:128]), r=[psb], w=[maskT])
            for hg in range(4):
                acc = [pb[2 + 3 * (hg % 2) + j] for j in range(3)]
                for kb in range(nkb):
                    ps = pb[kb % 2]
                    for rc in range(2):
                        k.op("pe", lambda h, ps=ps, kb=kb, rc=rc, hg=hg: h.matmul(
                            ps[:, :].rearrange("p (a b) -> p a b", a=4), lhsT=ckvT[:, rc, kb * 128:(kb + 1) * 128],
                            rhs=qabsT[:, hg * 4:hg * 4 + 4, rc, :], start=(rc == 0), stop=(rc == 1)),
                            r=[ckvT, qabsT], w=[ps], sig=(rc == 1))
                    P_ = pT[kb % 2]
                    Pm = pTm[kb % 2]
                    k.op("act", lambda h, ps=ps, P_=P_: h.activation(out=P_[:, :], in_=ps[:, :], func=AF.Exp), r=[ps], w=[P_])
                    k.op("pool", lambda h, P_=P_, Pm=Pm, kb=kb: h.tensor_tensor(
                        out=Pm[:, :].rearrange("p (a b) -> p a b", a=4), in0=P_[:, :].rearrange("p (a b) -> p a b", a=4),
                        in1=bc(maskT[:, kb, :], [128, 4, 128], 1), op=ALU.mult), r=[P_, maskT], w=[Pm])
                    for rc in range(2):
                        k.op("pe", lambda h, kb=kb, rc=rc, Pm=Pm, acc=acc: h.matmul(acc[rc][:, :], lhsT=ckvtm[:, kb, rc * 128:(rc + 1) * 128],
                                                                                    rhs=Pm[:, :], start=(kb == 0), stop=(kb == nkb - 1)),
                             r=[ckvtm, Pm], w=[acc[rc]], sig=False)
                    k.op("pe", lambda h, kb=kb, Pm=Pm, acc=acc: h.matmul(acc[2][:, :], lhsT=ones[:, :], rhs=Pm[:, :],
                                                                         start=(kb == 0), stop=(kb == nkb - 1)),
                         r=[ones, Pm], w=[acc[2]])
                k.op("dve", lambda h, acc=acc: h.reciprocal(out=rs[:, :], in_=acc[2][:, :]), r=[acc[2]], w=[rs])
                for rc in range(2):
                    k.op("dve", lambda h, acc=acc, rc=rc, hg=hg: h.tensor_tensor(
                        out=olatT[:, hg * 4:hg * 4 + 4, rc, :], in0=acc[rc][:, :].rearrange("p (a b) -> p a b", a=4),
                        in1=rs[:, :].rearrange("p (a b) -> p a b", a=4), op=ALU.mult), r=[acc[rc], rs], w=[olatT])
            for hg in range(4):
                ps = pb[hg % 2]
                for hh in range(4):
                    h_ = hg * 4 + hh
                    for rc in range(2):
                        k.op("pe", lambda h, ps=ps, h_=h_, hh=hh, rc=rc: h.matmul(ps[:, hh * 128:(hh + 1) * 128], lhsT=Wuv[:, h_, rc, :],
                                                                                  rhs=olatT[:, h_, rc, :], start=(rc == 0), stop=(rc == 1)),
                             r=[Wuv, olatT], w=[ps], sig=(hh == 3 and rc == 1))
                k.op("act", lambda h, ps=ps, hg=hg: h.copy(out=oT[:, hg * 4:hg * 4 + 4, :], in_=ps[:, :].rearrange("p (a b) -> p a b", a=4)),
                     r=[ps], w=[oT])
            for d4 in range(4):
                ps = pb[d4 % 2]
                Hh = H[d4 % 2]
                k.dma("sp", Hh[:, :, :], hT[d4 * 512:(d4 + 1) * 512, tq:tq + 128].rearrange("(c p) t -> p c t", p=128),
                      r=[hT.sub((d4 * 4 + j, ti)) for j in range(4)], w=[Hh])
                for dd in range(4):
                    dc = d4 * 4 + dd
                    for h_ in range(16):
                        k.op("pe", lambda h, ps=ps, dd=dd, dc=dc, h_=h_: h.matmul(ps[:, dd * 128:(dd + 1) * 128], lhsT=Wout[:, h_, dc * 128:(dc + 1) * 128],
                                                                                  rhs=oT[:, h_, :], start=(h_ == 0), stop=(h_ == 15)),
                             r=[Wout, oT], w=[ps], sig=(dd == 3 and h_ == 15))
                k.op("dve", lambda h, ps=ps, Hh=Hh: h.tensor_tensor(out=Hh[:, :, :], in0=Hh[:, :, :], in1=ps[:, :].rearrange("p (a b) -> p a b", a=4),
                                                                    op=ALU.add), r=[ps, Hh], w=[Hh])
                k.dma("sp", hT[d4 * 512:(d4 + 1) * 512, tq:tq + 128].rearrange("(c p) t -> p c t", p=128), Hh[:, :, :],
                      r=[Hh], w=[hT.sub((d4 * 4 + j, ti)) for j in range(4)])
    k.barrier()
    k.release(m)


def dsa_scratch(k, ntok, tag):
    return {"qidxT": k.dram(f"qidxT{tag}", [16, 128, ntok], BF16), "qabsT": k.dram(f"qabsT{tag}", [16, 2, 128, ntok], BF16),
            "kidx": k.dram(f"kidx{tag}", [128, ntok], BF16), "ckvT": k.dram(f"ckvT{tag}", [2, 128, ntok], BF16),
            "ckvtm": k.dram(f"ckvtm{tag}", [ntok, 256], BF16), "widx": k.dram(f"widx{tag}", [ntok, 32], F32)}


def dsa_layer(k, hT, ntok, seqlen, consts_d, sc, p):
    stage_dsa_a(k, hT, ntok, consts_d[:, 0:128], p, sc)
    stage_dsa_b(k, hT, ntok, seqlen, consts_d[:, 0:128], p, sc)


def stage_in(k, x_d, hT, ntok, ident_d):
    m = k.mark()
    pb = k.pb
    ident = k.sb("ident", [128, 128], F32)
    k.dma("sp", ident[:, :], ident_d, w=[ident])
    X = [k.sb(f"X{i}", [128, D], F32) for i in range(2)]
    stg = [k.sb(f"stg{i}", [128, 4, 128], F32) for i in range(2)]
    si = 0
    for tb in range(ntok // 128):
        Xt = X[tb % 2]
        k.dma("sp", Xt[:, :], x_d[tb * 128:(tb + 1) * 128, :], w=[Xt])
        for k4 in range(4):
            ps = pb[k4 % 2]
            for j in range(4):
                kc = k4 * 4 + j
                k.op("pe", lambda h, ps=ps, Xt=Xt, kc=kc, j=j: h.transpose(ps[:, j * 128:(j + 1) * 128], Xt[:, kc * 128:(kc + 1) * 128], ident[:, :]),
                     r=[Xt, ident], w=[ps], sig=(j == 3))
            St = stg[si % 2]; si += 1
            k.op("act" if k4 % 2 else "dve",
                 (lambda h, ps=ps, St=St: h.copy(out=St[:, :, :], in_=ps[:, :].rearrange("p (a b) -> p a b", a=4))) if k4 % 2 else
                 (lambda h, ps=ps, St=St: h.tensor_copy(out=St[:, :, :], in_=ps[:, :].rearrange("p (a b) -> p a b", a=4))),
                 r=[ps], w=[St])
            k.dma("sp", hT[k4 * 512:(k4 + 1) * 512, tb * 128:(tb + 1) * 128].rearrange("(c p) t -> p c t", p=128), St[:, :, :],
                  r=[St], w=[hT.sub((k4 * 4 + j, (tb * 128) // TT)) for j in range(4)])
    k.barrier()
    k.release(m)


def stage_out(k, hT, out_d, ntok, g_d, ident_d):
    m = k.mark()
    pb = k.pb
    ident = k.sb("ident", [128, 128], F32)
    k.dma("sp", ident[:, :], ident_d, w=[ident])
    gcol = cols_via_T(k, "gcol", g_d.rearrange("(c p) -> c p", p=128), KC, ident, pb[7])
    ones = k.sb("ones", [128, 128], BF16)
    k.op("dve", lambda h: h.memset(ones[:, :], 1.0), w=[ones])
    epsc = k.sb("epsc", [128, 1], F32)
    k.op("dve", lambda h: h.memset(epsc[:, :], EPS), w=[epsc])
    hbuf = k.sb("hbuf", [128, KC, TT], F32)
    yT = k.sb("yT", [128, KC, TT], F32)
    sq = [k.sb(f"sq{i}", [128, TT], BF16) for i in range(2)]
    rstd = k.sb("rstd", [128, TT], F32)
    O = [k.sb(f"O{i}", [128, D], F32) for i in range(2)]
    for ti in range(ntok // TT):
        t0 = ti * TT
        rms_to_uT(k, hT, t0, gcol, hbuf, yT, sq, rstd, ones, pb[4], epsc)
        for tb in range(TT // 128):
            Ot = O[tb % 2]
            for k4 in range(4):
                ps = pb[k4 % 2]
                for j in range(4):
                    kc = k4 * 4 + j
                    k.op("pe", lambda h, ps=ps, kc=kc, j=j, tb=tb: h.transpose(ps[:, j * 128:(j + 1) * 128], yT[:, kc, tb * 128:(tb + 1) * 128],
                                                                               ident[:, :]), r=[yT, ident], w=[ps], sig=(j == 3))
                if k4 % 2:
                    k.op("act", lambda h, ps=ps, Ot=Ot, k4=k4: h.copy(out=Ot[:, k4 * 512:(k4 + 1) * 512], in_=ps[:, :]), r=[ps], w=[Ot])
                else:
                    k.op("dve", lambda h, ps=ps, Ot=Ot, k4=k4: h.tensor_copy(out=Ot[:, k4 * 512:(k4 + 1) * 512], in_=ps[:, :]), r=[ps], w=[Ot])
            k.dma("sp", out_d[t0 + tb * 128:t0 + (tb + 1) * 128, :], Ot[:, :], r=[Ot], w=[out_d])
    k.barrier()
    k.release(m)


NTOK = 4096
SEQ = 2048
PARAM_SHAPES = {
    "norm_mix": [4, 2048], "norm_mlp": [4, 2048], "norm_final": [2048],
    "dn_w_in": [2, 2048, 12352], "dn_conv_w": [2, 4, 8192], "dn_a_log": [2, 32], "dn_dt_bias": [2, 32],
    "dn_out_norm": [2, 128], "dn_w_out": [2, 4096, 2048],
    "dsa_w_in": [2, 2048, 912], "dsa_q_norm": [2, 512], "dsa_kv_norm": [2, 256], "dsa_kidx_norm": [2, 128],
    "dsa_w_uq": [2, 512, 4096], "dsa_w_uk": [2, 16, 128, 256], "dsa_w_uv": [2, 16, 256, 128], "dsa_w_out": [2, 2048, 2048],
    "mlp_w_up": [4, 2048, 8192], "mlp_w_down": [4, 8192, 2048],
}


def build_program(ntok=NTOK, seqlen=SEQ, depth=4):
    nc = bass.Bass("TRN2", target_bir_lowering=False)
    k = K(nc)
    x_d = k.dram("x", [ntok, D], F32, kind="ExternalInput")
    consts = k.dram("consts", [128, 320], F32, kind="ExternalInput")
    P = {nm: k.dram(nm, shp, F32, kind="ExternalInput").t for nm, shp in PARAM_SHAPES.items()}
    out_d = k.dram("out", [ntok, D], F32, kind="ExternalOutput")
    hT = k.dram("hT", [D, ntok], F32)
    dn_sc = {"qk": k.dram("qk_s", [32, 128, ntok], F32), "v": k.dram("v_s", [NH, 128, ntok], F32),
             "z": k.dram("z_s", [NH, 128, ntok], F32), "bg": k.dram("bg_s", [ntok, 64], F32),
             "o": k.dram("o_s", [NH, 128, ntok], F32)}
    dsa_sc = dsa_scratch(k, ntok, "0")
    ident_d = consts.t[:, 0:128]
    stage_in(k, x_d.t, hT, ntok, ident_d)
    for i in range(depth):
        j = i // 2
        if i % 2 == 0:
            p = {"g": P["norm_mix"][i], "w_in": P["dn_w_in"][j], "conv_w": P["dn_conv_w"][j], "a_log": P["dn_a_log"][j],
                 "dt_bias": P["dn_dt_bias"][j], "out_norm": P["dn_out_norm"][j], "w_out": P["dn_w_out"][j]}
            dn_layer(k, hT, ntok, seqlen, consts.t, dn_sc, p)
        else:
            p = {"g": P["norm_mix"][i], "w_in": P["dsa_w_in"][j], "q_norm": P["dsa_q_norm"][j], "kv_norm": P["dsa_kv_norm"][j],
                 "kidx_norm": P["dsa_kidx_norm"][j], "w_uq": P["dsa_w_uq"][j], "w_uk": P["dsa_w_uk"][j], "w_uv": P["dsa_w_uv"][j],
                 "w_out": P["dsa_w_out"][j]}
            dsa_layer(k, hT, ntok, seqlen, consts.t, dsa_sc, p)
        stage_mlp(k, hT, ntok, P["norm_mlp"][i], P["mlp_w_up"][i], P["mlp_w_down"][i], 8192)
    stage_out(k, hT, out_d, ntok, P["norm_final"], ident_d)
    k.finish()
    return nc, k


def kernel(**inputs):
    n = 8
    x = np.ascontiguousarray(np.asarray(inputs["x"], dtype=np.float32))
    B, T, Dm = x.shape
    per = B // n
    consts = make_consts()
    params = {nm: np.ascontiguousarray(np.asarray(inputs[nm], dtype=np.float32)) for nm in PARAM_SHAPES}
    nc, _ = build_program(per * T, T, 4)
    in_maps = []
    for c in range(n):
        mp = {"x": x[c * per:(c + 1) * per].reshape(per * T, Dm), "consts": consts}
        mp.update(params)
        in_maps.append(mp)
    res = run_bass_kernel_spmd(nc, in_maps, core_ids=list(range(n)))
    outs = [np.asarray(r["out"]).reshape(per, T, Dm) for r in res.results]
    return np.concatenate(outs, axis=0).astype(np.float32)
```

```python
import numpy as np
from contextlib import ExitStack
import concourse.bass as bass
import concourse.mybir as mybir
from concourse.bass_utils import run_bass_kernel_spmd

F32 = mybir.dt.float32
BF16 = mybir.dt.bfloat16
AF = mybir.ActivationFunctionType
ALU = mybir.AluOpType
AX = mybir.AxisListType
ARENA_BYTES = 200 * 1024


class Dep:
    __slots__ = ("w", "r", "parent", "children")

    def __init__(self, parent=None):
        self.w = None
        self.r = []
        self.parent = parent
        self.children = []

    def related(self):
        out = [self]
        if self.parent is not None:
            out.append(self.parent)
        out.extend(self.children)
        return out


class Tl:
    def __init__(self, t):
        self.t = t
        self.d = Dep()
        self.subs = {}

    def sub(self, key):
        s = self.subs.get(key)
        if s is None:
            s = self.subs[key] = Dep(self.d)
            self.d.children.append(s)
        return s

    def __getitem__(self, idx):
        return self.t[idx]


def _dep(x):
    return x.d if isinstance(x, Tl) else x


class Eng:
    def __init__(self, name, h, sem, is_pe=False):
        self.name = name
        self.h = h
        self.sem = sem
        self.cnt = 0
        self.pending = False
        self.seen = {}
        self.prog = []
        self.is_pe = is_pe
        self.dsems = []
        self.dcnt = []
        self.dnext = 0


class K:
    def __init__(self, nc, n_dma_sems=8):
        self.nc = nc
        self.es = ExitStack()
        self.e = {}
        for name, h, pe in (("pe", nc.tensor, True), ("act", nc.scalar, False),
                            ("dve", nc.vector, False), ("pool", nc.gpsimd, False),
                            ("sp", nc.sync, False)):
            sem = self.es.enter_context(nc.semaphore("s_" + name))
            self.e[name] = Eng(name, h, sem, pe)
        for q in ("sp", "pool", "act"):
            en = self.e[q]
            for i in range(n_dma_sems):
                en.dsems.append(self.es.enter_context(nc.semaphore(f"d_{q}{i}")))
                en.dcnt.append(0)
        self.n_inst = 0
        self.asize = ARENA_BYTES
        self.aoff = 0
        self.arena = self.es.enter_context(nc.sbuf_tensor("arena", [128, ARENA_BYTES // 4], F32))
        self.psum = self.es.enter_context(nc.psum_tensor("psum", [128, 8 * 512], F32))
        self.pb = [Tl(self.psum[:, i * 512:(i + 1) * 512]) for i in range(8)]

    def sb(self, name, shape, dt):
        shape = list(shape)
        esz = 2 if dt == BF16 else 4
        n = 1
        for x in shape[1:]:
            n *= x
        nbytes = (n * esz + 63) // 64 * 64
        if self.aoff + nbytes > self.asize:
            raise RuntimeError(f"SBUF arena overflow at {name}: {self.aoff}+{nbytes}>{self.asize}")
        v = self.arena[0:shape[0], self.aoff // 4:(self.aoff + nbytes) // 4]
        self.aoff += nbytes
        if dt != F32:
            v = v.bitcast(dt)
        v = v[:, 0:n]
        if len(shape) == 3:
            v = v.rearrange("p (a b) -> p a b", a=shape[1])
        elif len(shape) == 4:
            v = v.rearrange("p (a b c) -> p a b c", a=shape[1], b=shape[2])
        return Tl(v)

    def mark(self):
        return self.aoff

    def release(self, m):
        self.aoff = m

    def dram(self, name, shape, dt, kind="Internal"):
        return Tl(self.nc.dram_tensor(name, list(shape), dt, kind=kind).ap())

    def barrier(self):
        tgt = []
        for name, en in self.e.items():
            if en.pending:
                raise RuntimeError("barrier with pending unsignalled instr on " + name)
            if en.cnt > 0:
                tgt.append((en.sem, en.cnt))
            for s_, c in zip(en.dsems, en.dcnt):
                if c > 0:
                    tgt.append((s_, c))
        for name, en in self.e.items():
            for (s_, v) in tgt:
                if s_ is en.sem:
                    continue
                if en.seen.get(s_, 0) < v:
                    en.seen[s_] = v
                    en.prog.append(("w", s_, v))

    def _waits(self, en, r, w):
        need = {}
        for x in r:
            for d in _dep(x).related():
                if d.w is not None:
                    s, v = d.w
                    if need.get(s, 0) < v:
                        need[s] = v
        for x in w:
            for d in _dep(x).related():
                if d.w is not None:
                    s, v = d.w
                    if need.get(s, 0) < v:
                        need[s] = v
                for (s, v) in d.r:
                    if need.get(s, 0) < v:
                        need[s] = v
        for s, v in need.items():
            if s is en.sem and en.is_pe:
                continue
            if en.seen.get(s, 0) >= v:
                continue
            en.seen[s] = v
            en.prog.append(("w", s, v))

    def _mark(self, ident, r, w):
        for x in w:
            d = _dep(x)
            d.w = ident
            d.r = []
        for x in r:
            d = _dep(x)
            d.r.append(ident)
            if len(d.r) > 64:
                mx = {}
                for (s, v) in d.r:
                    if mx.get(s, 0) < v:
                        mx[s] = v
                d.r = list(mx.items())

    def op(self, eng, fn, r=(), w=(), sig=True):
        en = self.e[eng]
        self._waits(en, r, w)
        if sig:
            en.cnt += 1
            en.pending = False
            en.prog.append(("i", fn, True))
            ident = (en.sem, en.cnt)
        else:
            en.pending = True
            en.prog.append(("i", fn, False))
            ident = (en.sem, en.cnt + 1)
        self._mark(ident, r, w)
        self.n_inst += 1

    def dma(self, q, out, in_, r=(), w=(), **kw):
        en = self.e[q]
        self._waits(en, r, w)
        i = en.dnext
        en.dnext = (i + 1) % len(en.dsems)
        s = en.dsems[i]
        if en.dcnt[i] > 0 and en.seen.get(s, 0) < en.dcnt[i]:
            en.seen[s] = en.dcnt[i]
            en.prog.append(("w", s, en.dcnt[i]))
        en.dcnt[i] += 16
        en.prog.append(("d", out, in_, s, kw))
        self._mark((s, en.dcnt[i]), r, w)
        self.n_inst += 1

    def finish(self, final_deps=()):
        nc = self.nc
        for name, en in self.e.items():
            if en.pending:
                raise RuntimeError(f"engine {name} has unsignalled trailing instructions")
        sp = self.e["sp"]
        self._waits(sp, list(final_deps), [])
        for q in ("sp", "pool", "act"):
            en = self.e[q]
            for s, c in zip(en.dsems, en.dcnt):
                if c > 0 and sp.seen.get(s, 0) < c:
                    sp.seen[s] = c
                    sp.prog.append(("w", s, c))
        for name in ("pe", "act", "dve", "pool"):
            en = self.e[name]
            if en.cnt > 0 and sp.seen.get(en.sem, 0) < en.cnt:
                sp.prog.append(("w", en.sem, en.cnt))

        def run(en, h):
            for it in en.prog:
                if it[0] == "w":
                    h.wait_ge(it[1], it[2])
                elif it[0] == "i":
                    ins = it[1](h)
                    if it[2]:
                        ins.then_inc(en.sem, 1)
                else:
                    h.dma_start(out=it[1], in_=it[2], **it[4]).then_inc(it[3], 16)

        with nc.Block() as block:
            @block.tensor
            def _(h):
                run(self.e["pe"], h)

            @block.scalar
            def _(h):
                run(self.e["act"], h)

            @block.vector
            def _(h):
                run(self.e["dve"], h)

            @block.gpsimd
            def _(h):
                run(self.e["pool"], h)

            @block.sync
            def _(h):
                run(self.e["sp"], h)
        self.es.close()


D = 2048
KC = D // 128
EPS = 1e-6
TT = 512


def load_cols(k, name, src, n):
    t = k.sb(name, [128, n], F32)
    k.dma("sp", t[:, :], src.rearrange("(c p) -> p c", p=128), w=[t],
          allow_slow_non_contiguous=True)
    return t


def rms_to_uT(k, hT, t0, gcol, hbuf, uT, sq, rstd, ones, pss, epsc, u32=False):
    for kc in range(KC):
        k.dma("sp", hbuf[:, kc, :], hT[kc * 128:(kc + 1) * 128, t0:t0 + TT], w=[hbuf.sub(kc)])
    for kc in range(KC):
        k.op("act", lambda h, kc=kc: h.activation(out=sq[kc % 2][:, :], in_=hbuf[:, kc, :], func=AF.Square),
             r=[hbuf.sub(kc)], w=[sq[kc % 2]])
        k.op("pe", lambda h, kc=kc: h.matmul(pss[:, :], lhsT=ones[:, :], rhs=sq[kc % 2][:, :],
                                            start=(kc == 0), stop=(kc == KC - 1)),
             r=[sq[kc % 2], ones], w=[pss])
    k.op("act", lambda h: h.activation(out=rstd[:, :], in_=pss[:, :], func=AF.Sqrt, scale=1.0 / D, bias=epsc[:, :]),
         r=[pss, epsc], w=[rstd])
    k.op("dve", lambda h: h.reciprocal(out=rstd[:, :], in_=rstd[:, :]), r=[rstd], w=[rstd])
    for kc in range(KC):
        k.op("dve", lambda h, kc=kc: h.scalar_tensor_tensor(out=uT[:, kc, :], in0=hbuf[:, kc, :],
                                                            scalar=gcol[:, kc:kc + 1], in1=rstd[:, :],
                                                            op0=ALU.mult, op1=ALU.mult),
             r=[hbuf.sub(kc), gcol, rstd], w=[uT])
        if u32:
            k.op("dve", lambda h, kc=kc: h.scalar_tensor_tensor(out=hbuf[:, kc, :], in0=hbuf[:, kc, :],
                                                                scalar=gcol[:, kc:kc + 1], in1=rstd[:, :],
                                                                op0=ALU.mult, op1=ALU.mult),
                 r=[hbuf.sub(kc), gcol, rstd], w=[hbuf.sub(kc)])


def to_bf16_dram(k, dst, src, rows, blk=128):
    deps = []
    for rb in range(rows // blk):
        k.dma("pool", dst[rb * blk:(rb + 1) * blk, :], src[rb * blk:(rb + 1) * blk, :], w=[dst.sub(rb)])
        deps.append(dst.sub(rb))
    return deps


def stage_mlp(k, hT, ntok, g_d, wup_d, wdn_d, F, wsc=None):
    m = k.mark()
    updeps, dndeps = [], []
    if wsc is not None:
        updeps = to_bf16_dram(k, wsc["up"], wup_d, D)
        dndeps = to_bf16_dram(k, wsc["dn"], wdn_d, F)
        wup_d = wsc["up"].t
        wdn_d = wsc["dn"].t
    FC = F // 128
    gcol = load_cols(k, "gcol", g_d, KC)
    ones = k.sb("ones", [128, 128], BF16)
    k.op("dve", lambda h: h.memset(ones[:, :], 1.0), w=[ones])
    epsc = k.sb("epsc", [128, 1], F32)
    k.op("dve", lambda h: h.memset(epsc[:, :], EPS), w=[epsc])
    hbuf = k.sb("hbuf", [128, KC, TT], F32)
    uT = [k.sb(f"uT{i}", [128, KC, TT], BF16) for i in range(1)]
    aT = k.sb("aT", [128, FC, TT], BF16)
    sq = [k.sb(f"sq{i}", [128, TT], BF16) for i in range(2)]
    rstd = k.sb("rstd", [128, TT], F32)
    tmp = [k.sb(f"tmp{i}", [128, TT], BF16) for i in range(2)]
    wu = [k.sb(f"wu{i}", [128, KC, 512], BF16) for i in range(2)]
    wd = [k.sb(f"wd{i}", [128, FC, 128], BF16) for i in range(2)]
    wupv = wup_d.rearrange("(kc p) f -> p kc f", p=128)
    wdnv = wdn_d.rearrange("(fc p) d -> p fc d", p=128)
    pb = k.pb
    nmm = 0
    for ti in range(ntok // TT):
        t0 = ti * TT
        u = uT[0]
        rms_to_uT(k, hT, t0, gcol, hbuf, u, sq, rstd, ones, pb[4], epsc)
        for fb in range(F // 512):
            W = wu[fb % 2]
            k.dma("pool", W[:, :, :], wupv[:, :, fb * 512:(fb + 1) * 512], r=updeps, w=[W])
            for fc in range(4):
                fi = fb * 4 + fc
                ps = pb[fi % 2]
                for kc in range(KC):
                    k.op("pe", lambda h, ps=ps, W=W, kc=kc, fc=fc, u=u: h.matmul(
                        ps[:, :], lhsT=W[:, kc, fc * 128:(fc + 1) * 128], rhs=u[:, kc, :],
                        start=(kc == 0), stop=(kc == KC - 1)),
                        r=[W, u], w=[ps], sig=(kc == KC - 1))
                tm = tmp[fi % 2]
                k.op("act", lambda h, ps=ps, tm=tm: h.activation(out=tm[:, :], in_=ps[:, :], func=AF.Relu),
                     r=[ps], w=[tm])
                k.op("dve", lambda h, tm=tm, fi=fi: h.tensor_tensor(out=aT[:, fi, :], in0=tm[:, :], in1=tm[:, :],
                                                                    op=ALU.mult),
                     r=[tm], w=[aT])
        for dc in range(KC):
            W = wd[dc % 2]
            k.dma("pool", W[:, :, :], wdnv[:, :, dc * 128:(dc + 1) * 128], r=dndeps, w=[W])
            ps = pb[2 + dc % 2]
            for fc in range(FC):
                k.op("pe", lambda h, ps=ps, W=W, fc=fc: h.matmul(
                    ps[:, :], lhsT=W[:, fc, :], rhs=aT[:, fc, :], start=(fc == 0), stop=(fc == FC - 1)),
                    r=[W, aT], w=[ps], sig=(fc == FC - 1))
            k.op("dve", lambda h, ps=ps, dc=dc: h.tensor_tensor(out=hbuf[:, dc, :], in0=hbuf[:, dc, :],
                                                                in1=ps[:, :], op=ALU.add),
                 r=[ps, hbuf.sub(dc)], w=[hbuf.sub(dc)])
            k.dma("sp", hT[dc * 128:(dc + 1) * 128, t0:t0 + TT], hbuf[:, dc, :],
                  r=[hbuf.sub(dc)], w=[hT.sub((dc, ti))])
    k.barrier()
    k.release(m)


NH = 32
NQK = 16
CH = 64
QK_FP32 = False
BF16_WSCRATCH = True
DN_IN = 12352


def bcast_row(k, name, src, n, q="sp"):
    t = k.sb(name, [128, n], F32)
    k.dma(q, t[:, :], src.partition_broadcast(128), w=[t])
    return t


def cols_via_T(k, name, src2d, R, ident, ps):
    tmp = k.sb(name + "_r", [R, 128], F32)
    out = k.sb(name, [128, R], F32)
    k.dma("sp", tmp[:, :], src2d, w=[tmp])
    k.op("pe", lambda h: h.transpose(ps[:, 0:R], tmp[:, :], ident[0:R, 0:R]), r=[tmp, ident], w=[ps])
    k.op("dve", lambda h: h.tensor_copy(out=out[:, :], in_=ps[:, 0:R]), r=[ps], w=[out])
    return out


def stage_dn_a(k, hT, ntok, seqlen, ident_d, g_d, win_d, convw_d, alog_d, dtb_d, qk_s, v_s, z_s, bg_s, wsc=None):
    m = k.mark()
    wdeps = []
    win32_d = win_d
    if wsc is not None:
        wdeps = to_bf16_dram(k, wsc["win"], win_d, D)
        win_d = wsc["win"].t
    pb = k.pb
    ident = k.sb("ident", [128, 128], F32)
    k.dma("sp", ident[:, :], ident_d, w=[ident])
    gcol = cols_via_T(k, "gcol", g_d.rearrange("(c p) -> c p", p=128), KC, ident, pb[7])
    cwv = convw_d.rearrange("j (c p) -> (j c) p", p=128)
    cw0 = cols_via_T(k, "cw0", cwv[0:128, :], 128, ident, pb[7])
    cw1 = cols_via_T(k, "cw1", cwv[128:256, :], 128, ident, pb[7])

    def cwcol(j, c):
        idx = j * 64 + c
        t = cw0 if idx < 128 else cw1
        idx = idx % 128
        return t[:, idx:idx + 1], t

    ones = k.sb("ones", [128, 128], BF16)
    k.op("dve", lambda h: h.memset(ones[:, :], 1.0), w=[ones])
    epsc = k.sb("epsc", [128, 1], F32)
    k.op("dve", lambda h: h.memset(epsc[:, :], EPS), w=[epsc])
    onec = k.sb("onec", [128, 1], F32)
    k.op("dve", lambda h: h.memset(onec[:, :], 1.0), w=[onec])
    nA = bcast_row(k, "nA", alog_d, NH)
    k.op("act", lambda h: h.activation(out=nA[:, :], in_=nA[:, :], func=AF.Exp), r=[nA], w=[nA])
    k.op("dve", lambda h: h.tensor_scalar(out=nA[:, :], in0=nA[:, :], scalar1=-1.0, scalar2=None, op0=ALU.mult),
         r=[nA], w=[nA])
    dtb = bcast_row(k, "dtb", dtb_d, NH)
    hbuf = k.sb("hbuf", [128, KC, TT], F32)
    uT = k.sb("uT", [128, KC, TT], BF16)
    sq = [k.sb(f"sq{i}", [128, TT], BF16) for i in range(2)]
    rstd = k.sb("rstd", [128, TT], F32)
    wt = [k.sb(f"wt{i}", [128, KC, 512], BF16) for i in range(2)]
    wba = k.sb("wba", [128, KC, 64], BF16)
    wt32 = [k.sb(f"wt32_{i}", [128, KC, 512], F32) for i in range(2)]
    hist = k.sb("hist", [128, 64, 3], F32)
    pre = [k.sb(f"pre{i}", [128, TT + 3], F32) for i in range(2)]
    acc = [k.sb(f"acc{i}", [128, TT], F32) for i in range(2)]
    sl = [k.sb(f"sl{i}", [128, TT], F32) for i in range(2)]
    rn = [k.sb(f"rn{i}", [128, TT], F32) for i in range(2)]
    ost = [k.sb(f"ost{i}", [128, TT], F32) for i in range(3)]
    bgt = [k.sb(f"bgt{i}", [128, 64], F32) for i in range(2)]
    bgx = [k.sb(f"bgx{i}", [128, 32], F32) for i in range(2)]
    winv = win_d.rearrange("(kc p) f -> p kc f", p=128)
    winv32 = win32_d.rearrange("(kc p) f -> p kc f", p=128)
    k.dma("pool", wba[:, :, :], winv[:, :, 12288:12352], r=wdeps, w=[wba])
    ci = 0
    for ti in range(ntok // TT):
        t0 = ti * TT
        rms_to_uT(k, hT, t0, gcol, hbuf, uT, sq, rstd, ones, pb[4], epsc, u32=True)
        if t0 % seqlen == 0:
            k.op("dve", lambda h: h.memset(hist[:, :, :], 0.0), w=[hist])
        for fb in range(24):
            hi = QK_FP32 and fb < 8
            if hi:
                W = wt32[fb % 2]
                k.dma("act" if fb % 2 else "sp", W[:, :, :], winv32[:, :, fb * 512:(fb + 1) * 512], w=[W])
            else:
                W = wt[fb % 2]
                k.dma("pool", W[:, :, :], winv[:, :, fb * 512:(fb + 1) * 512], r=wdeps, w=[W])
            for fc in range(4):
                c = fb * 4 + fc
                ps = pb[c % 2]
                for kc in range(KC):
                    if hi:
                        k.op("pe", lambda h, ps=ps, W=W, kc=kc, fc=fc: h.matmul(
                            ps[:, :], lhsT=W[:, kc, fc * 128:(fc + 1) * 128], rhs=hbuf[:, kc, :],
                            start=(kc == 0), stop=(kc == KC - 1)), r=[W, hbuf.sub(kc)], w=[ps], sig=(kc == KC - 1))
                    else:
                        k.op("pe", lambda h, ps=ps, W=W, kc=kc, fc=fc: h.matmul(
                            ps[:, :], lhsT=W[:, kc, fc * 128:(fc + 1) * 128], rhs=uT[:, kc, :],
                            start=(kc == 0), stop=(kc == KC - 1)), r=[W, uT], w=[ps], sig=(kc == KC - 1))
                o_t = ost[ci % 3]
                ci += 1
                if c < 64:
                    P = pre[c % 2]
                    A = acc[c % 2]
                    S = sl[c % 2]
                    k.op("act", lambda h, P=P, c=c: h.copy(out=P[:, 0:3], in_=hist[:, c, :]), r=[hist.sub(c)], w=[P])
                    k.op("act", lambda h, P=P, ps=ps: h.copy(out=P[:, 3:TT + 3], in_=ps[:, :]), r=[ps], w=[P])
                    k.op("act", lambda h, P=P, c=c: h.copy(out=hist[:, c, :], in_=P[:, TT:TT + 3]), r=[P], w=[hist.sub(c)])
                    for j in range(4):
                        col, ct = cwcol(j, c)
                        if j == 0:
                            k.op("dve", lambda h, A=A, P=P, col=col: h.tensor_scalar(
                                out=A[:, :], in0=P[:, 0:TT], scalar1=col, scalar2=None, op0=ALU.mult),
                                r=[P, ct], w=[A])
                        else:
                            k.op("dve", lambda h, A=A, P=P, col=col, j=j: h.scalar_tensor_tensor(
                                out=A[:, :], in0=P[:, j:j + TT], scalar=col, in1=A[:, :], op0=ALU.mult, op1=ALU.add),
                                r=[P, ct, A], w=[A])
                    if c < 32:
                        k.op("act", lambda h, S=S, A=A: h.activation(out=S[:, :], in_=A[:, :], func=AF.Silu), r=[A], w=[S])
                        q2 = sq[c % 2]
                        k.op("act", lambda h, S=S, q2=q2: h.activation(out=q2[:, :], in_=S[:, :], func=AF.Square), r=[S], w=[q2])
                        k.op("pe", lambda h, q2=q2: h.matmul(pb[5][:, :], lhsT=ones[:, :], rhs=q2[:, :], start=True, stop=True),
                             r=[q2, ones], w=[pb[5]])
                        R = rn[c % 2]
                        k.op("act", lambda h, R=R: h.activation(out=R[:, :], in_=pb[5][:, :], func=AF.Sqrt, scale=1.0,
                                                                bias=epsc[:, :]), r=[pb[5], epsc], w=[R])
                        k.op("dve", lambda h, R=R: h.reciprocal(out=R[:, :], in_=R[:, :]), r=[R], w=[R])
                        scl = (128.0 ** -0.5) if c < 16 else 1.0
                        k.op("dve", lambda h, o_t=o_t, S=S, R=R, scl=scl: h.scalar_tensor_tensor(
                            out=o_t[:, :], in0=S[:, :], scalar=scl, in1=R[:, :], op0=ALU.mult, op1=ALU.mult),
                            r=[S, R], w=[o_t])
                        k.dma("sp", qk_s[c, :, t0:t0 + TT], o_t[:, :], r=[o_t], w=[qk_s.sub((c, ti))])
                    else:
                        k.op("act", lambda h, o_t=o_t, A=A: h.activation(out=o_t[:, :], in_=A[:, :], func=AF.Silu), r=[A], w=[o_t])
                        k.dma("sp", v_s[c - 32, :, t0:t0 + TT], o_t[:, :], r=[o_t], w=[v_s.sub((c, ti))])
                else:
                    k.op("act", lambda h, o_t=o_t, ps=ps: h.activation(out=o_t[:, :], in_=ps[:, :], func=AF.Silu), r=[ps], w=[o_t])
                    k.dma("sp", z_s[c - 64, :, t0:t0 + TT], o_t[:, :], r=[o_t], w=[z_s.sub((c, ti))])
        for sbi in range(TT // 128):
            ps = pb[2 + sbi % 2]
            for kc in range(KC):
                k.op("pe", lambda h, ps=ps, kc=kc, sbi=sbi: h.matmul(
                    ps[:, 0:64], lhsT=uT[:, kc, sbi * 128:(sbi + 1) * 128], rhs=wba[:, kc, :],
                    start=(kc == 0), stop=(kc == KC - 1)), r=[uT, wba], w=[ps], sig=(kc == KC - 1))
            B = bgt[sbi % 2]
            X = bgx[sbi % 2]
            k.op("act", lambda h, B=B, ps=ps: h.activation(out=B[:, 0:32], in_=ps[:, 0:32], func=AF.Sigmoid), r=[ps], w=[B])
            k.op("dve", lambda h, X=X, ps=ps: h.tensor_tensor(out=X[:, :], in0=ps[:, 32:64], in1=dtb[:, :], op=ALU.add),
                 r=[ps, dtb], w=[X])
            k.op("act", lambda h, X=X: h.activation(out=X[:, :], in_=X[:, :], func=AF.Exp), r=[X], w=[X])
            k.op("act", lambda h, X=X: h.activation(out=X[:, :], in_=X[:, :], func=AF.Ln, scale=1.0, bias=onec[:, :]),
                 r=[X, onec], w=[X])
            k.op("dve", lambda h, B=B, X=X: h.tensor_tensor(out=B[:, 32:64], in0=X[:, :], in1=nA[:, :], op=ALU.mult),
                 r=[X, nA, B], w=[B])
            k.dma("sp", bg_s[t0 + sbi * 128:t0 + (sbi + 1) * 128, :], B[:, :], r=[B], w=[bg_s.sub((ti, sbi))])
    k.barrier()
    k.release(m)


def bc(ap, shape, axis):
    return ap.unsqueeze(axis).to_broadcast(list(shape))


def make_consts():
    c = np.zeros((128, 320), np.float32)
    c[:, 0:128] = np.eye(128, dtype=np.float32)
    i = np.arange(64)
    c[0:64, 128:192] = (i[:, None] <= i[None, :]).astype(np.float32)
    c[0:64, 192:256] = np.where(i[:, None] > i[None, :], 0.0, -30000.0)
    c[0:64, 256:320] = np.where(i[None, :] >= i[:, None], 0.0, -30000.0)
    return c


def stage_dn_b(k, ntok, seqlen, consts_d, qk_s, v_s, bg_s, o_s):
    m = k.mark()
    pb = k.pb
    cst = k.sb("cst", [128, 320], F32)
    k.dma("sp", cst[:, :], consts_d, w=[cst])
    ident = cst[:, 0:128]
    U = cst[0:64, 128:192]
    NEG12 = cst[0:64, 192:320]
    ones64 = k.sb("ones64", [64, 128], F32)
    k.op("dve", lambda h: h.memset(ones64[:, :], 1.0), w=[ones64])
    negones = k.sb("negones", [64, 64], F32)
    k.op("dve", lambda h: h.memset(negones[:, :], -1.0), w=[negones])
    onesneg = k.sb("onesneg", [64, 128], F32)
    k.op("dve", lambda h: h.memset(onesneg[:, 0:64], 1.0), w=[onesneg])
    k.op("dve", lambda h: h.memset(onesneg[:, 64:128], -1.0), w=[onesneg])
    S = k.sb("S", [128, NH, 128], F32)
    QK = k.sb("QK", [128, NQK, 128], F32)
    Vt = k.sb("Vt", [128, NH, 64], F32)
    bg = k.sb("bg", [64, 64], F32)
    sm = {n: k.sb(n, [64, 32], F32) for n in ("gc", "egc", "ekd", "c1", "nbeta", "tmpd")}
    dS = k.sb("dS", [128, 32], F32)
    kdec = k.sb("kdec", [64, NH, 128], F32)
    Vb = k.sb("Vb", [64, NH, 128], F32)
    GU = k.sb("GU", [64, NH, 64], F32)
    big1 = k.sb("big1", [64, NH, 128], F32)
    Pb = [k.sb(f"P{i}", [64, NH, 64], F32) for i in range(2)]
    Ptb = [k.sb(f"Pt{i}", [64, NH, 64], F32) for i in range(2)]
    R = k.sb("R", [64, NH, 64], F32)
    intraT = k.sb("intraT", [64, NH, 64], F32)
    QdT = k.sb("QdT", [128, NH, 64], F32)
    vn = k.sb("vn", [64, NH, 128], F32)
    ost = k.sb("ost", [128, NH, 64], F32)
    D12 = big1
    Rr = big1
    cstd = [cst]

    def subs(T, keys):
        return [T.sub(x) for x in keys]

    G8 = range(4)
    for c in range(ntok // CH):
        t0 = c * CH
        if t0 % seqlen == 0:
            k.op("dve", lambda h: h.memset(S[:, :, :], 0.0), w=subs(S, range(NH)))
        k.dma("sp", QK[:, :, 0:64], qk_s[16:32, :, t0:t0 + CH].rearrange("c p t -> p c t"), w=[QK])
        k.dma("sp", QK[:, :, 64:128], qk_s[0:16, :, t0:t0 + CH].rearrange("c p t -> p c t"), w=[QK])
        k.dma("sp", Vt[:, :, :], v_s[:, :, t0:t0 + CH].rearrange("c p t -> p c t"), w=[Vt])
        k.dma("sp", bg[:, :], bg_s[t0:t0 + CH, :], w=[bg])
        beta = bg[:, 0:32]
        g = bg[:, 32:64]
        k.op("pe", lambda h: h.matmul(pb[0][0:64, 0:32], lhsT=U, rhs=g, start=True, stop=True), r=[cst, bg], w=[pb[0]])
        k.op("pe", lambda h: h.matmul(pb[1][:, 0:32], lhsT=ones64[:, :], rhs=g, start=True, stop=True), r=[ones64, bg], w=[pb[1]])
        k.op("act", lambda h: h.copy(out=sm["gc"][:, :], in_=pb[0][0:64, 0:32]), r=[pb[0]], w=[sm["gc"]])
        k.op("act", lambda h: h.activation(out=sm["egc"][:, :], in_=pb[0][0:64, 0:32], func=AF.Exp), r=[pb[0]], w=[sm["egc"]])
        k.op("act", lambda h: h.activation(out=dS[:, :], in_=pb[1][:, 0:32], func=AF.Exp), r=[pb[1]], w=[dS])
        k.op("dve", lambda h: h.tensor_tensor(out=sm["tmpd"][:, :], in0=pb[1][0:64, 0:32], in1=sm["gc"][:, :], op=ALU.subtract),
             r=[pb[1], sm["gc"]], w=[sm["tmpd"]])
        k.op("act", lambda h: h.activation(out=sm["ekd"][:, :], in_=sm["tmpd"][:, :], func=AF.Exp), r=[sm["tmpd"]], w=[sm["ekd"]])
        k.op("dve", lambda h: h.scalar_tensor_tensor(out=sm["c1"][:, :], in0=beta, scalar=-1.0, in1=sm["egc"][:, :],
                                                     op0=ALU.mult, op1=ALU.mult), r=[bg, sm["egc"]], w=[sm["c1"]])
        k.op("dve", lambda h: h.tensor_scalar(out=sm["nbeta"][:, :], in0=beta, scalar1=-1.0, scalar2=None, op0=ALU.mult),
             r=[bg], w=[sm["nbeta"]])
        k.op("dve", lambda h: h.tensor_tensor(out=GU[:, :, :], in0=bc(U, [64, NH, 64], 1), in1=bc(g, [64, NH, 64], 2),
                                              op=ALU.mult), r=[cst, bg], w=[GU])
        for b4 in range(8):
            ps = pb[2 + b4 % 4]
            for hh in range(4):
                h_ = b4 * 4 + hh
                o0 = hh * 128
                k.op("pe", lambda h, ps=ps, h_=h_, o0=o0: h.matmul(ps[0:64, o0:o0 + 128], lhsT=GU[:, h_, :], rhs=onesneg[:, :],
                                                                   start=True, stop=False), r=[GU, onesneg], w=[ps], sig=False)
                k.op("pe", lambda h, ps=ps, h_=h_, o0=o0: h.matmul(ps[0:64, o0:o0 + 64], lhsT=negones[:, :], rhs=GU[:, h_, :],
                                                                   start=False, stop=False), r=[GU, negones], w=[ps], sig=False)
                k.op("pe", lambda h, ps=ps, h_=h_, o0=o0: h.matmul(ps[0:64, o0 + 64:o0 + 128], lhsT=ones64[:, 0:64], rhs=GU[:, h_, :],
                                                                   start=False, stop=False), r=[GU, ones64], w=[ps], sig=False)
                k.op("pe", lambda h, ps=ps, h_=h_, o0=o0: h.matmul(ps[0:64, o0:o0 + 128], lhsT=ident[0:64, 0:64], rhs=NEG12,
                                                                   start=False, stop=True), r=[cst], w=[ps], sig=(hh == 3))
            k.op("act", lambda h, ps=ps, b4=b4: h.activation(out=D12[:, b4 * 4:b4 * 4 + 4, :],
                                                             in_=ps[0:64, :].rearrange("p (a b) -> p a b", a=4), func=AF.Exp),
                 r=[ps], w=[D12.sub(b4)])
        N = Ptb[0]
        M = Pb[0]
        for g8 in G8:
            ps = pb[g8 % 2]
            for q4 in range(4):
                hq = g8 * 4 + q4
                k.op("pe", lambda h, ps=ps, hq=hq, q4=q4: h.matmul(ps[0:64, q4 * 128:(q4 + 1) * 128], lhsT=QK[:, hq, 0:64],
                                                                   rhs=QK[:, hq, :], start=True, stop=True),
                     r=[QK], w=[ps], sig=(q4 == 3))
            psv = ps[0:64, :].rearrange("p (a b) -> p a b", a=4)
            for par in range(2):
                hs = slice(g8 * 8 + par, g8 * 8 + 8, 2)
                k.op("dve", lambda h, psv=psv, hs=hs: h.tensor_tensor(out=N[:, hs, :], in0=psv[:, :, 0:64], in1=D12[:, hs, 0:64],
                                                                      op=ALU.mult),
                     r=[ps, D12.sub(2 * g8), D12.sub(2 * g8 + 1)], w=[N.sub(g8)])
                k.op("dve", lambda h, psv=psv, hs=hs: h.tensor_tensor(out=intraT[:, hs, :], in0=psv[:, :, 64:128],
                                                                      in1=D12[:, hs, 64:128], op=ALU.mult),
                     r=[ps, D12.sub(2 * g8), D12.sub(2 * g8 + 1)], w=[intraT.sub(g8)])
            k.op("dve", lambda h, g8=g8: h.tensor_tensor(out=N[:, g8 * 8:g8 * 8 + 8, :], in0=N[:, g8 * 8:g8 * 8 + 8, :],
                                                         in1=bc(sm["nbeta"][:, g8 * 8:g8 * 8 + 8], [64, 8, 64], 2), op=ALU.mult),
                 r=[N.sub(g8), sm["nbeta"]], w=[N.sub(g8)])
        for g8 in G8:
            ps = pb[2 + g8 % 2]
            for i8 in range(8):
                h_ = g8 * 8 + i8
                k.op("pe", lambda h, ps=ps, h_=h_, i8=i8: h.transpose(ps[0:64, i8 * 64:(i8 + 1) * 64], N[:, h_, :], ident[0:64, 0:64]),
                     r=[N.sub(g8), cst], w=[ps], sig=(i8 == 7))
            k.op("act", lambda h, ps=ps, g8=g8: h.copy(out=M[:, g8 * 8:g8 * 8 + 8, :], in_=ps[0:64, :].rearrange("p (a b) -> p a b", a=8)),
                 r=[ps], w=[M.sub(g8)])
            k.op("dve", lambda h, g8=g8: h.tensor_tensor(out=R[:, g8 * 8:g8 * 8 + 8, :], in0=M[:, g8 * 8:g8 * 8 + 8, :],
                                                         in1=bc(ident[0:64, 0:64], [64, 8, 64], 1), op=ALU.add),
                 r=[M.sub(g8), cst], w=[R.sub(g8)])
        cur = 0
        for lvl in range(1, 6):
            Po, Pto = Pb[cur], Ptb[cur]
            Pn, Ptn = Pb[1 - cur], Ptb[1 - cur]
            for g8 in G8:
                psP = pb[0 + g8 % 2]
                psT = pb[2 + g8 % 2]
                psR = pb[4 + g8 % 2]
                for i8 in range(8):
                    h_ = g8 * 8 + i8
                    cs = slice(i8 * 64, (i8 + 1) * 64)
                    if lvl < 5:
                        k.op("pe", lambda h, psP=psP, h_=h_, cs=cs, Po=Po, Pto=Pto: h.matmul(
                            psP[0:64, cs], lhsT=Pto[:, h_, :], rhs=Po[:, h_, :], start=True, stop=True),
                            r=[Po.sub(g8), Pto.sub(g8)], w=[psP], sig=(i8 == 7))
                    k.op("pe", lambda h, psT=psT, h_=h_, cs=cs, Po=Po, Pto=Pto: h.matmul(
                        psT[0:64, cs], lhsT=Po[:, h_, :], rhs=Pto[:, h_, :], start=True, stop=True),
                        r=[Po.sub(g8), Pto.sub(g8)], w=[psT], sig=(i8 == 7))
                if lvl < 5:
                    k.op("act", lambda h, psP=psP, g8=g8, Pn=Pn: h.copy(out=Pn[:, g8 * 8:g8 * 8 + 8, :],
                                                                        in_=psP[0:64, :].rearrange("p (a b) -> p a b", a=8)),
                         r=[psP], w=[Pn.sub(g8)])
                k.op("dve", lambda h, psT=psT, g8=g8, Ptn=Ptn: h.tensor_copy(out=Ptn[:, g8 * 8:g8 * 8 + 8, :],
                                                                             in_=psT[0:64, :].rearrange("p (a b) -> p a b", a=8)),
                     r=[psT], w=[Ptn.sub(g8)])
                for i8 in range(8):
                    h_ = g8 * 8 + i8
                    cs = slice(i8 * 64, (i8 + 1) * 64)
                    k.op("pe", lambda h, psR=psR, h_=h_, cs=cs, Ptn=Ptn: h.matmul(
                        psR[0:64, cs], lhsT=Ptn[:, h_, :], rhs=R[:, h_, :], start=True, stop=True),
                        r=[Ptn.sub(g8), R.sub(g8)], w=[psR], sig=(i8 == 7))
                k.op("dve", lambda h, psR=psR, g8=g8: h.tensor_tensor(out=R[:, g8 * 8:g8 * 8 + 8, :], in0=R[:, g8 * 8:g8 * 8 + 8, :],
                                                                      in1=psR[0:64, :].rearrange("p (a b) -> p a b", a=8), op=ALU.add),
                     r=[psR, R.sub(g8)], w=[R.sub(g8)])
            cur = 1 - cur
        DG = GU
        k.op("dve", lambda h: h.tensor_tensor(out=DG[:, :, :], in0=bc(ident[0:64, 0:64], [64, NH, 64], 1),
                                              in1=bc(sm["egc"][:, :], [64, NH, 64], 2), op=ALU.mult),
             r=[cst, sm["egc"]], w=[DG])
        for g8 in G8:
            ps = pb[6 + g8 % 2]
            for i8 in range(8):
                h_ = g8 * 8 + i8
                k.op("pe", lambda h, ps=ps, h_=h_, i8=i8: h.matmul(ps[:, i8 * 64:(i8 + 1) * 64], lhsT=ones64[:, :], rhs=DG[:, h_, :],
                                                                   start=True, stop=True), r=[ones64, DG], w=[ps], sig=(i8 == 7))
            psv = ps[:, :].rearrange("p (a b) -> p a b", a=8)
            for par in range(2):
                hs = slice(g8 * 8 + par, g8 * 8 + 8, 2)
                k.op("dve", lambda h, psv=psv, hs=hs, g8=g8, par=par: h.tensor_tensor(
                    out=QdT[:, hs, :], in0=QK[:, g8 * 4:g8 * 4 + 4, 64:128], in1=psv[:, par:8:2, :], op=ALU.mult),
                    r=[ps, QK], w=[QdT.sub(g8)])
        for g8 in G8:
            ps = pb[g8 % 2]
            for q4 in range(4):
                hq = g8 * 4 + q4
                k.op("pe", lambda h, ps=ps, hq=hq, q4=q4: h.transpose(ps[0:64, q4 * 128:(q4 + 1) * 128], QK[:, hq, 0:64], ident),
                     r=[QK, cst], w=[ps], sig=(q4 == 3))
            psv = ps[0:64, :].rearrange("p (a b) -> p a b", a=4)
            for par in range(2):
                hs = slice(g8 * 8 + par, g8 * 8 + 8, 2)
                k.op("dve", lambda h, psv=psv, hs=hs: h.tensor_tensor(out=kdec[:, hs, :], in0=psv,
                                                                      in1=bc(sm["ekd"][:, hs], [64, 4, 128], 2), op=ALU.mult),
                     r=[ps, sm["ekd"]], w=[kdec.sub(g8)])
        for b4 in range(8):
            ps = pb[2 + b4 % 2]
            for hh in range(4):
                h_ = b4 * 4 + hh
                k.op("pe", lambda h, ps=ps, h_=h_, hh=hh: h.transpose(ps[0:64, hh * 128:(hh + 1) * 128], Vt[:, h_, :], ident),
                     r=[Vt, cst], w=[ps], sig=(hh == 3))
            k.op("dve", lambda h, ps=ps, b4=b4: h.tensor_tensor(out=Vb[:, b4 * 4:b4 * 4 + 4, :],
                                                                in0=ps[0:64, :].rearrange("p (a b) -> p a b", a=4),
                                                                in1=bc(beta[:, b4 * 4:b4 * 4 + 4], [64, 4, 128], 2), op=ALU.mult),
                 r=[ps, bg], w=[Vb.sub(b4)])
        for h_ in range(NH):
            hq = h_ // 2
            g8 = h_ // 8
            ks = pb[0][0:64, (h_ % 4) * 128:(h_ % 4 + 1) * 128]
            ksd = pb[0].sub(h_ % 4)
            vp = pb[1][0:64, (h_ % 4) * 128:(h_ % 4 + 1) * 128]
            vpd = pb[1].sub(h_ % 4)
            dsb = pb[2 + (h_ % 8) // 4]
            dsp = dsb[:, (h_ % 4) * 128:(h_ % 4 + 1) * 128]
            dsd = dsb.sub(h_ % 4)
            ob = pb[4 + g8]
            op_ = ob[:, (h_ % 8) * 64:(h_ % 8 + 1) * 64]
            k.op("pe", lambda h, ks=ks, hq=hq, h_=h_: h.matmul(ks, lhsT=QK[:, hq, 0:64], rhs=S[:, h_, :], start=True, stop=True),
                 r=[QK, S.sub(h_)], w=[ksd])
            k.op("dve", lambda h, ks=ks, h_=h_: h.scalar_tensor_tensor(out=Rr[:, h_, :], in0=ks, scalar=sm["c1"][:, h_:h_ + 1],
                                                                       in1=Vb[:, h_, :], op0=ALU.mult, op1=ALU.add),
                 r=[ksd, sm["c1"], Vb.sub(h_ // 4)], w=[Rr.sub(("r", h_))])
            k.op("pe", lambda h, vp=vp, h_=h_: h.matmul(vp, lhsT=R[:, h_, :], rhs=Rr[:, h_, :], start=True, stop=True),
                 r=[R.sub(g8), Rr.sub(("r", h_))], w=[vpd])
            k.op("act", lambda h, vp=vp, h_=h_: h.copy(out=vn[:, h_, :], in_=vp), r=[vpd], w=[vn.sub(h_)])
            k.op("pe", lambda h, op_=op_, h_=h_: h.matmul(op_, lhsT=S[:, h_, :], rhs=QdT[:, h_, :], start=True, stop=False),
                 r=[S.sub(h_), QdT.sub(g8)], w=[ob.sub(h_ % 8)], sig=False)
            k.op("pe", lambda h, op_=op_, h_=h_: h.matmul(op_, lhsT=vn[:, h_, :], rhs=intraT[:, h_, :], start=False, stop=True),
                 r=[vn.sub(h_), intraT.sub(g8)], w=[ob.sub(h_ % 8)])
            k.op("pe", lambda h, dsp=dsp, h_=h_: h.matmul(dsp, lhsT=kdec[:, h_, :], rhs=vn[:, h_, :], start=True, stop=True),
                 r=[kdec.sub(g8), vn.sub(h_)], w=[dsd])
            k.op("dve", lambda h, dsp=dsp, h_=h_: h.scalar_tensor_tensor(out=S[:, h_, :], in0=S[:, h_, :], scalar=dS[:, h_:h_ + 1],
                                                                         in1=dsp, op0=ALU.mult, op1=ALU.add),
                 r=[dsd, dS, S.sub(h_)], w=[S.sub(h_)])
            if h_ % 8 == 7:
                k.op("act", lambda h, ob=ob, g8=g8: h.copy(out=ost[:, g8 * 8:g8 * 8 + 8, :],
                                                           in_=ob[:, :].rearrange("p (a b) -> p a b", a=8)),
                     r=[ob.sub(i) for i in range(8)], w=[ost.sub(g8)])
                k.dma("sp", o_s[g8 * 8:g8 * 8 + 8, :, t0:t0 + CH].rearrange("c p t -> p c t"), ost[:, g8 * 8:g8 * 8 + 8, :],
                      r=[ost.sub(g8)], w=[o_s.sub((g8, c))])
    k.barrier()
    k.release(m)


def stage_dn_c(k, hT, ntok, o_s, z_s, onorm_d, wout_d, ident_d):
    m = k.mark()
    pb = k.pb
    ident = k.sb("ident", [128, 128], F32)
    k.dma("sp", ident[:, :], ident_d, w=[ident])
    oncol = cols_via_T(k, "oncol", onorm_d.rearrange("(c p) -> c p", p=128), 1, ident, pb[7])
    ones = k.sb("ones", [128, 128], BF16)
    k.op("dve", lambda h: h.memset(ones[:, :], 1.0), w=[ones])
    epsc = k.sb("epsc", [128, 1], F32)
    k.op("dve", lambda h: h.memset(epsc[:, :], EPS), w=[epsc])
    yT = k.sb("yT", [128, NH, TT], BF16)
    ob = [k.sb(f"ob{i}", [128, TT], F32) for i in range(2)]
    zb = [k.sb(f"zb{i}", [128, TT], F32) for i in range(2)]
    sq = [k.sb(f"sq{i}", [128, TT], BF16) for i in range(2)]
    rn = [k.sb(f"rn{i}", [128, TT], F32) for i in range(2)]
    hb = [k.sb(f"hb{i}", [128, TT], F32) for i in range(2)]
    wd = [k.sb(f"wd{i}", [128, NH, 128], BF16) for i in range(2)]
    woutv = wout_d.rearrange("(fc p) d -> p fc d", p=128)
    for ti in range(ntok // TT):
        t0 = ti * TT
        for h_ in range(NH):
            O = ob[h_ % 2]
            Z = zb[h_ % 2]
            Q2 = sq[h_ % 2]
            Rn = rn[h_ % 2]
            k.dma("sp", O[:, :], o_s[h_, :, t0:t0 + TT], w=[O])
            k.dma("act", Z[:, :], z_s[h_, :, t0:t0 + TT], w=[Z])
            k.op("act", lambda h, O=O, Q2=Q2: h.activation(out=Q2[:, :], in_=O[:, :], func=AF.Square), r=[O], w=[Q2])
            ps = pb[4 + h_ % 2]
            k.op("pe", lambda h, ps=ps, Q2=Q2: h.matmul(ps[:, :], lhsT=ones[:, :], rhs=Q2[:, :], start=True, stop=True),
                 r=[Q2, ones], w=[ps])
            k.op("act", lambda h, ps=ps, Rn=Rn: h.activation(out=Rn[:, :], in_=ps[:, :], func=AF.Sqrt, scale=1.0 / 128, bias=epsc[:, :]),
                 r=[ps, epsc], w=[Rn])
            k.op("dve", lambda h, Rn=Rn: h.reciprocal(out=Rn[:, :], in_=Rn[:, :]), r=[Rn], w=[Rn])
            k.op("dve", lambda h, O=O, Rn=Rn: h.scalar_tensor_tensor(out=O[:, :], in0=O[:, :], scalar=oncol[:, 0:1], in1=Rn[:, :],
                                                                     op0=ALU.mult, op1=ALU.mult), r=[O, Rn, oncol], w=[O])
            k.op("dve", lambda h, O=O, Z=Z, h_=h_: h.tensor_tensor(out=yT[:, h_, :], in0=O[:, :], in1=Z[:, :], op=ALU.mult),
                 r=[O, Z], w=[yT])
        for dc in range(KC):
            W = wd[dc % 2]
            k.dma("pool", W[:, :, :], woutv[:, :, dc * 128:(dc + 1) * 128], w=[W])
            H = hb[dc % 2]
            k.dma("sp", H[:, :], hT[dc * 128:(dc + 1) * 128, t0:t0 + TT], r=[hT.sub((dc, ti))], w=[H])
            ps = pb[dc % 2]
            for fc in range(NH):
                k.op("pe", lambda h, ps=ps, W=W, fc=fc: h.matmul(ps[:, :], lhsT=W[:, fc, :], rhs=yT[:, fc, :],
                                                                 start=(fc == 0), stop=(fc == NH - 1)),
                     r=[W, yT], w=[ps], sig=(fc == NH - 1))
            k.op("dve", lambda h, ps=ps, H=H: h.tensor_tensor(out=H[:, :], in0=H[:, :], in1=ps[:, :], op=ALU.add),
                 r=[ps, H], w=[H])
            k.dma("sp", hT[dc * 128:(dc + 1) * 128, t0:t0 + TT], H[:, :], r=[H], w=[hT.sub((dc, ti))])
    k.barrier()
    k.release(m)


def dn_layer(k, hT, ntok, seqlen, consts_d, sc, p, wsc=None):
    stage_dn_a(k, hT, ntok, seqlen, consts_d[:, 0:128], p["g"], p["w_in"], p["conv_w"], p["a_log"], p["dt_bias"],
               sc["qk"], sc["v"], sc["z"], sc["bg"], wsc=wsc)
    stage_dn_b(k, ntok, seqlen, consts_d, sc["qk"], sc["v"], sc["bg"], sc["o"])
    stage_dn_c(k, hT, ntok, sc["o"], sc["z"], p["out_norm"], p["w_out"], consts_d[:, 0:128])


NEGM = -30000.0


def fm_rmsnorm(k, src, n, width, gcols, out, sq, pss, rn, ones, epsc, tag):
    for c in range(n):
        k.op("act", lambda h, c=c: h.activation(out=sq[c % 2][:, :], in_=src[:, c, :], func=AF.Square), r=[src], w=[sq[c % 2]])
        k.op("pe", lambda h, c=c: h.matmul(pss[:, :], lhsT=ones[:, :], rhs=sq[c % 2][:, :], start=(c == 0), stop=(c == n - 1)),
             r=[sq[c % 2], ones], w=[pss])
    k.op("act", lambda h: h.activation(out=rn[:, :], in_=pss[:, :], func=AF.Sqrt, scale=1.0 / width, bias=epsc[:, :]),
         r=[pss, epsc], w=[rn])
    k.op("dve", lambda h: h.reciprocal(out=rn[:, :], in_=rn[:, :]), r=[rn], w=[rn])
    for c in range(n):
        k.op("dve", lambda h, c=c: h.scalar_tensor_tensor(out=out[:, c, :], in0=src[:, c, :], scalar=gcols[:, c:c + 1], in1=rn[:, :],
                                                          op0=ALU.mult, op1=ALU.mult), r=[src, rn, gcols], w=[out])


def stage_dsa_a(k, hT, ntok, ident_d, p, sc):
    m = k.mark()
    pb = k.pb
    ident = k.sb("ident", [128, 128], F32)
    k.dma("sp", ident[:, :], ident_d, w=[ident])
    identb = k.sb("identb", [128, 128], BF16)
    k.op("dve", lambda h: h.tensor_copy(out=identb[:, :], in_=ident[:, :]), r=[ident], w=[identb])
    gcol = cols_via_T(k, "gcol", p["g"].rearrange("(c p) -> c p", p=128), KC, ident, pb[7])
    qncol = cols_via_T(k, "qncol", p["q_norm"].rearrange("(c p) -> c p", p=128), 4, ident, pb[7])
    kvcol = cols_via_T(k, "kvcol", p["kv_norm"].rearrange("(c p) -> c p", p=128), 2, ident, pb[7])
    kicol = cols_via_T(k, "kicol", p["kidx_norm"].rearrange("(c p) -> c p", p=128), 1, ident, pb[7])
    ones = k.sb("ones", [128, 128], BF16)
    k.op("dve", lambda h: h.memset(ones[:, :], 1.0), w=[ones])
    epsc = k.sb("epsc", [128, 1], F32)
    k.op("dve", lambda h: h.memset(epsc[:, :], EPS), w=[epsc])
    hbuf = k.sb("hbuf", [128, KC, TT], F32)
    uT = k.sb("uT", [128, KC, TT], BF16)
    sq = [k.sb(f"sq{i}", [128, TT], BF16) for i in range(2)]
    rstd = k.sb("rstd", [128, TT], F32)
    Wd = k.sb("Wd", [128, KC, 912], BF16)
    k.dma("pool", Wd[:, :, :], p["w_in"].rearrange("(kc p) f -> p kc f", p=128), w=[Wd])
    Wuq = k.sb("Wuq", [128, 4, 4096], BF16)
    k.dma("pool", Wuq[:, :, :], p["w_uq"].rearrange("(kc p) f -> p kc f", p=128), w=[Wuq])
    Wuk = k.sb("Wuk", [128, 16, 256], BF16)
    k.dma("pool", Wuk[:, :, :], p["w_uk"].rearrange("h d r -> d h r"), w=[Wuk])
    pj = k.sb("pj", [128, 7, TT], F32)
    qn = k.sb("qn", [128, 4, TT], BF16)
    ckn = k.sb("ckn", [128, 2, TT], BF16)
    kin = k.sb("kin", [128, 1, TT], BF16)
    rn = k.sb("rn", [128, TT], F32)
    ctm = [k.sb(f"ctm{i}", [128, 256], BF16) for i in range(2)]
    wi = [k.sb(f"wi{i}", [128, 32], F32) for i in range(2)]
    qT = [k.sb(f"qT{i}", [128, TT], BF16) for i in range(2)]
    ob = [k.sb(f"ob{i}", [128, TT], BF16) for i in range(3)]
    oi = 0
    wscale = (16.0 ** -0.5) * (128.0 ** -0.5)
    for ti in range(ntok // TT):
        t0 = ti * TT
        rms_to_uT(k, hT, t0, gcol, hbuf, uT, sq, rstd, ones, pb[4], epsc)
        for fc in range(7):
            ps = pb[fc % 2]
            for kc in range(KC):
                k.op("pe", lambda h, ps=ps, kc=kc, fc=fc: h.matmul(ps[:, :], lhsT=Wd[:, kc, fc * 128:(fc + 1) * 128], rhs=uT[:, kc, :],
                                                                   start=(kc == 0), stop=(kc == KC - 1)),
                     r=[Wd, uT], w=[ps], sig=(kc == KC - 1))
            k.op("act", lambda h, ps=ps, fc=fc: h.copy(out=pj[:, fc, :], in_=ps[:, :]), r=[ps], w=[pj])
        fm_rmsnorm(k, Tl(pj[:, 0:4, :]) if False else pj, 4, 512, qncol, qn, sq, pb[5], rn, ones, epsc, "q")
        pj_kv = Tl(pj[:, 4:6, :]); pj_kv.d = pj.d
        pj_ki = Tl(pj[:, 6:7, :]); pj_ki.d = pj.d
        fm_rmsnorm(k, pj_kv, 2, 256, kvcol, ckn, sq, pb[5], rn, ones, epsc, "kv")
        fm_rmsnorm(k, pj_ki, 1, 128, kicol, kin, sq, pb[5], rn, ones, epsc, "ki")
        k.dma("sp", sc["kidx"][:, t0:t0 + TT], kin[:, 0, :], r=[kin], w=[sc["kidx"].sub(ti)])
        for rc in range(2):
            k.dma("sp", sc["ckvT"][rc, :, t0:t0 + TT], ckn[:, rc, :], r=[ckn], w=[sc["ckvT"].sub((rc, ti))])
        for tb in range(TT // 128):
            psb = pb[2 + tb % 2]
            pv = psb[:, :].bitcast(BF16)
            for rc in range(2):
                k.op("pe", lambda h, pv=pv, rc=rc, tb=tb: h.transpose(pv[:, rc * 128:(rc + 1) * 128], ckn[:, rc, tb * 128:(tb + 1) * 128],
                                                                      identb[:, :]), r=[ckn, identb], w=[psb], sig=(rc == 1))
            C = ctm[tb % 2]
            k.op("act", lambda h, pv=pv, C=C: h.copy(out=C[:, :], in_=pv[:, 0:256]), r=[psb], w=[C])
            k.dma("sp", sc["ckvtm"][t0 + tb * 128:t0 + (tb + 1) * 128, :], C[:, :], r=[C], w=[sc["ckvtm"].sub((ti, tb))])
            ps = pb[6 + tb % 2]
            for kc in range(KC):
                k.op("pe", lambda h, ps=ps, kc=kc, tb=tb: h.matmul(ps[:, 0:16], lhsT=uT[:, kc, tb * 128:(tb + 1) * 128], rhs=Wd[:, kc, 896:912],
                                                                   start=(kc == 0), stop=(kc == KC - 1)),
                     r=[uT, Wd], w=[ps], sig=(kc == KC - 1))
            W_ = wi[tb % 2]
            k.op("act", lambda h, ps=ps, W_=W_: h.activation(out=W_[:, 0:16], in_=ps[:, 0:16], func=AF.Abs, scale=wscale), r=[ps], w=[W_])
            k.op("act", lambda h, ps=ps, W_=W_: h.activation(out=W_[:, 16:32], in_=ps[:, 0:16], func=AF.Sign), r=[ps, W_], w=[W_])
            k.dma("sp", sc["widx"][t0 + tb * 128:t0 + (tb + 1) * 128, :], W_[:, :], r=[W_], w=[sc["widx"].sub((ti, tb))])
        for c in range(32):
            ps = pb[c % 2]
            for kc in range(4):
                k.op("pe", lambda h, ps=ps, kc=kc, c=c: h.matmul(ps[:, :], lhsT=Wuq[:, kc, c * 128:(c + 1) * 128], rhs=qn[:, kc, :],
                                                                 start=(kc == 0), stop=(kc == 3)), r=[Wuq, qn], w=[ps], sig=(kc == 3))
            if c >= 16:
                O = ob[oi % 3]; oi += 1
                k.op("act", lambda h, ps=ps, O=O: h.copy(out=O[:, :], in_=ps[:, :]), r=[ps], w=[O])
                k.dma("sp", sc["qidxT"][c - 16, :, t0:t0 + TT], O[:, :], r=[O], w=[sc["qidxT"].sub((c, ti))])
            else:
                Q = qT[c % 2]
                k.op("act", lambda h, ps=ps, Q=Q: h.copy(out=Q[:, :], in_=ps[:, :]), r=[ps], w=[Q])
                for rc in range(2):
                    ps2 = pb[2 + rc]
                    k.op("pe", lambda h, ps2=ps2, Q=Q, c=c, rc=rc: h.matmul(ps2[:, :], lhsT=Wuk[:, c, rc * 128:(rc + 1) * 128], rhs=Q[:, :],
                                                                            start=True, stop=True), r=[Wuk, Q], w=[ps2])
                    O = ob[oi % 3]; oi += 1
                    k.op("dve", lambda h, ps2=ps2, O=O: h.tensor_scalar(out=O[:, :], in0=ps2[:, :], scalar1=128.0 ** -0.5, scalar2=None,
                                                                        op0=ALU.mult), r=[ps2], w=[O])
                    k.dma("sp", sc["qabsT"][c, rc, :, t0:t0 + TT], O[:, :], r=[O], w=[sc["qabsT"].sub((c, rc, ti))])
    k.barrier()
    k.release(m)


def stage_dsa_b(k, hT, ntok, seqlen, ident_d, p, sc):
    m = k.mark()
    pb = k.pb
    ident = k.sb("ident", [128, 128], F32)
    k.dma("sp", ident[:, :], ident_d, w=[ident])
    identb = k.sb("identb", [128, 128], BF16)
    k.op("dve", lambda h: h.tensor_copy(out=identb[:, :], in_=ident[:, :]), r=[ident], w=[identb])
    ones = k.sb("ones", [128, 128], BF16)
    k.op("dve", lambda h: h.memset(ones[:, :], 1.0), w=[ones])
    negd = k.sb("negd", [128, 128], F32)
    k.op("dve", lambda h: h.memset(negd[:, :], 0.0), w=[negd])
    k.op("dve", lambda h: h.memset(negd[0:64, 64:128], NEGM), r=[negd], w=[negd])
    Wout = k.sb("Wout", [128, 16, 2048], BF16)
    k.dma("pool", Wout[:, :, :], p["w_out"].rearrange("(c p) d -> p c d", p=128), w=[Wout])
    Wuv = k.sb("Wuv", [128, 16, 2, 128], BF16)
    for rc in range(2):
        k.dma("pool", Wuv[:, :, rc, :], p["w_uv"][:, rc * 128:(rc + 1) * 128, :].rearrange("h r d -> r h d"), w=[Wuv])
    nkb_all = seqlen // 128
    kidxT = k.sb("kidxT", [128, seqlen], BF16)
    ckvT = k.sb("ckvT", [128, 2, seqlen], BF16)
    ckvtm = k.sb("ckvtm", [128, nkb_all, 256], BF16)
    qidxT = k.sb("qidxT", [128, 16, 128], BF16)
    qabsT = k.sb("qabsT", [128, 16, 2, 128], BF16)
    widx = k.sb("widx", [128, 32], F32)
    score = k.sb("score", [128, seqlen], F32)
    work = k.sb("work", [128, seqlen], F32)
    m8 = k.sb("m8", [128, 8], F32)
    mask01 = k.sb("mask01", [128, seqlen], BF16)
    maskT = k.sb("maskT", [128, nkb_all, 128], BF16)
    rl = [k.sb(f"rl{i}", [128, 512], F32) for i in range(2)]
    pT = [k.sb(f"pT{i}", [128, 512], BF16) for i in range(2)]
    pTm = [k.sb(f"pTm{i}", [128, 512], BF16) for i in range(2)]
    rs = k.sb("rs", [128, 512], F32)
    olatT = k.sb("olatT", [128, 16, 2, 128], BF16)
    oT = k.sb("oT", [128, 16, 128], BF16)
    H = [k.sb(f"H{i}", [128, 4, 128], F32) for i in range(2)]
    li = 0
    for b in range(ntok // seqlen):
        s0 = b * seqlen
        k.dma("sp", kidxT[:, :], sc["kidx"][:, s0:s0 + seqlen], r=[sc["kidx"].sub(i) for i in range(ntok // TT)], w=[kidxT])
        for rc in range(2):
            k.dma("sp", ckvT[:, rc, :], sc["ckvT"][rc, :, s0:s0 + seqlen],
                  r=[sc["ckvT"].sub((rc, i)) for i in range(ntok // TT)], w=[ckvT])
        k.dma("sp", ckvtm[:, :, :], sc["ckvtm"][s0:s0 + seqlen, :].rearrange("(kb p) r -> p kb r", p=128),
              r=[sc["ckvtm"].sub((i, j)) for i in range(ntok // TT) for j in range(4)], w=[ckvtm])
        for qb in range(seqlen // 128):
            tq = s0 + qb * 128
            ti = tq // TT
            nkb = qb + 1
            n = nkb * 128
            k.dma("sp", qidxT[:, :, :], sc["qidxT"][:, :, tq:tq + 128].rearrange("h p t -> p h t"),
                  r=[sc["qidxT"].sub((c, ti)) for c in range(16, 32)], w=[qidxT])
            for rc in range(2):
                k.dma("sp", qabsT[:, :, rc, :], sc["qabsT"][:, rc, :, tq:tq + 128].rearrange("h p t -> p h t"),
                      r=[sc["qabsT"].sub((c, rc, ti)) for c in range(16)], w=[qabsT])
            k.dma("sp", widx[:, :], sc["widx"][tq:tq + 128, :], r=[sc["widx"].sub((ti, (tq % TT) // 128))], w=[widx])
            for kg in range((n + 511) // 512):
                wdt = min(512, n - kg * 512)
                cs = slice(kg * 512, kg * 512 + wdt)
                for h_ in range(16):
                    ps = pb[h_ % 2]
                    k.op("pe", lambda h, ps=ps, h_=h_, cs=cs, wdt=wdt: h.matmul(ps[:, 0:wdt], lhsT=qidxT[:, h_, :], rhs=kidxT[:, cs],
                                                                                start=True, stop=True), r=[qidxT, kidxT], w=[ps])
                    T_ = rl[li % 2]; li += 1
                    k.op("act", lambda h, ps=ps, T_=T_, h_=h_, wdt=wdt: h.activation(out=T_[:, 0:wdt], in_=ps[:, 0:wdt], func=AF.Relu,
                                                                                     scale=widx[:, h_:h_ + 1]), r=[ps, widx], w=[T_])
                    if h_ == 0:
                        k.op("dve", lambda h, T_=T_, cs=cs, wdt=wdt: h.tensor_scalar(out=score[:, cs], in0=T_[:, 0:wdt],
                                                                                     scalar1=widx[:, 16:17], scalar2=None, op0=ALU.mult),
                             r=[T_, widx], w=[score])
                    else:
                        k.op("dve", lambda h, T_=T_, cs=cs, wdt=wdt, h_=h_: h.scalar_tensor_tensor(
                            out=score[:, cs], in0=T_[:, 0:wdt], scalar=widx[:, 16 + h_:17 + h_], in1=score[:, cs],
                            op0=ALU.mult, op1=ALU.add), r=[T_, widx, score], w=[score])
            k.op("dve", lambda h, n=n: h.tensor_tensor(out=score[:, n - 128:n], in0=score[:, n - 128:n], in1=negd[:, :], op=ALU.add),
                 r=[score, negd], w=[score])
            if n > 256:
                k.op("dve", lambda h, n=n: h.tensor_copy(out=work[:, 0:n], in_=score[:, 0:n]), r=[score], w=[work])
                for rnd in range(32):
                    k.op("dve", lambda h, n=n: h.max(out=m8[:, :], in_=work[:, 0:n]), r=[work], w=[m8])
                    if rnd < 31:
                        k.op("dve", lambda h, n=n: h.match_replace(out=work[:, 0:n], in_to_replace=m8[:, :], in_values=work[:, 0:n],
                                                                   imm_value=-1e9), r=[work, m8], w=[work])
                k.op("dve", lambda h, n=n: h.tensor_scalar(out=mask01[:, 0:n], in0=score[:, 0:n], scalar1=m8[:, 7:8], scalar2=None,
                                                           op0=ALU.is_ge), r=[score, m8], w=[mask01])
            else:
                k.op("dve", lambda h, n=n: h.tensor_scalar(out=mask01[:, 0:n], in0=score[:, 0:n], scalar1=-10000.0, scalar2=None,
                                                           op0=ALU.is_ge), r=[score], w=[mask01])
            for kb in range(nkb):
                psb = pb[kb % 2]
                pv = psb[:, :].bitcast(BF16)
                k.op("pe", lambda h, pv=pv, kb=kb: h.transpose(pv[:, 0:128], mask01[:, kb * 128:(kb + 1) * 128], identb[:, :]),
                     r=[mask01, identb], w=[psb])
                k.op("act", lambda h, pv=pv, kb=kb: h.copy(out=maskT[:, kb, :], in_=pv[:, 0:128]), r=[psb], w=[maskT])
            for hg in range(4):
                acc = [pb[2 + 3 * (hg % 2) + j] for j in range(3)]
                for kb in range(nkb):
                    ps = pb[kb % 2]
                    for rc in range(2):
                        k.op("pe", lambda h, ps=ps, kb=kb, rc=rc, hg=hg: h.matmul(
                            ps[:, :].rearrange("p (a b) -> p a b", a=4), lhsT=ckvT[:, rc, kb * 128:(kb + 1) * 128],
                            rhs=qabsT[:, hg * 4:hg * 4 + 4, rc, :], start=(rc == 0), stop=(rc == 1)),
                            r=[ckvT, qabsT], w=[ps], sig=(rc == 1))
                    P_ = pT[kb % 2]
                    Pm = pTm[kb % 2]
                    k.op("act", lambda h, ps=ps, P_=P_: h.activation(out=P_[:, :], in_=ps[:, :], func=AF.Exp), r=[ps], w=[P_])
                    k.op("pool", lambda h, P_=P_, Pm=Pm, kb=kb: h.tensor_tensor(
                        out=Pm[:, :].rearrange("p (a b) -> p a b", a=4), in0=P_[:, :].rearrange("p (a b) -> p a b", a=4),
                        in1=bc(maskT[:, kb, :], [128, 4, 128], 1), op=ALU.mult), r=[P_, maskT], w=[Pm])
                    for rc in range(2):
                        k.op("pe", lambda h, kb=kb, rc=rc, Pm=Pm, acc=acc: h.matmul(acc[rc][:, :], lhsT=ckvtm[:, kb, rc * 128:(rc + 1) * 128],
                                                                                    rhs=Pm[:, :], start=(kb == 0), stop=(kb == nkb - 1)),
                             r=[ckvtm, Pm], w=[acc[rc]], sig=False)
                    k.op("pe", lambda h, kb=kb, Pm=Pm, acc=acc: h.matmul(acc[2][:, :], lhsT=ones[:, :], rhs=Pm[:, :],
                                                                         start=(kb == 0), stop=(kb == nkb - 1)),
                         r=[ones, Pm], w=[acc[2]])
                k.op("dve", lambda h, acc=acc: h.reciprocal(out=rs[:, :], in_=acc[2][:, :]), r=[acc[2]], w=[rs])
                for rc in range(2):
                    k.op("dve", lambda h, acc=acc, rc=rc, hg=hg: h.tensor_tensor(
                        out=olatT[:, hg * 4:hg * 4 + 4, rc, :], in0=acc[rc][:, :].rearrange("p (a b) -> p a b", a=4),
                        in1=rs[:, :].rearrange("p (a b) -> p a b", a=4), op=ALU.mult), r=[acc[rc], rs], w=[olatT])
            for hg in range(4):
                ps = pb[hg % 2]
                for hh in range(4):
                    h_ = hg * 4 + hh
                    for rc in range(2):
                        k.op("pe", lambda h, ps=ps, h_=h_, hh=hh, rc=rc: h.matmul(ps[:, hh * 128:(hh + 1) * 128], lhsT=Wuv[:, h_, rc, :],
                                                                                  rhs=olatT[:, h_, rc, :], start=(rc == 0), stop=(rc == 1)),
                             r=[Wuv, olatT], w=[ps], sig=(hh == 3 and rc == 1))
                k.op("act", lambda h, ps=ps, hg=hg: h.copy(out=oT[:, hg * 4:hg * 4 + 4, :], in_=ps[:, :].rearrange("p (a b) -> p a b", a=4)),
                     r=[ps], w=[oT])
            for d4 in range(4):
                ps = pb[d4 % 2]
                Hh = H[d4 % 2]
                k.dma("sp", Hh[:, :, :], hT[d4 * 512:(d4 + 1) * 512, tq:tq + 128].rearrange("(c p) t -> p c t", p=128),
                      r=[hT.sub((d4 * 4 + j, ti)) for j in range(4)], w=[Hh])
                for dd in range(4):
                    dc = d4 * 4 + dd
                    for h_ in range(16):
                        k.op("pe", lambda h, ps=ps, dd=dd, dc=dc, h_=h_: h.matmul(ps[:, dd * 128:(dd + 1) * 128], lhsT=Wout[:, h_, dc * 128:(dc + 1) * 128],
                                                                                  rhs=oT[:, h_, :], start=(h_ == 0), stop=(h_ == 15)),
                             r=[Wout, oT], w=[ps], sig=(dd == 3 and h_ == 15))
                k.op("dve", lambda h, ps=ps, Hh=Hh: h.tensor_tensor(out=Hh[:, :, :], in0=Hh[:, :, :], in1=ps[:, :].rearrange("p (a b) -> p a b", a=4),
                                                                    op=ALU.add), r=[ps, Hh], w=[Hh])
                k.dma("sp", hT[d4 * 512:(d4 + 1) * 512, tq:tq + 128].rearrange("(c p) t -> p c t", p=128), Hh[:, :, :],
                      r=[Hh], w=[hT.sub((d4 * 4 + j, ti)) for j in range(4)])
    k.barrier()
    k.release(m)


def dsa_scratch(k, ntok, tag):
    return {"qidxT": k.dram(f"qidxT{tag}", [16, 128, ntok], BF16), "qabsT": k.dram(f"qabsT{tag}", [16, 2, 128, ntok], BF16),
            "kidx": k.dram(f"kidx{tag}", [128, ntok], BF16), "ckvT": k.dram(f"ckvT{tag}", [2, 128, ntok], BF16),
            "ckvtm": k.dram(f"ckvtm{tag}", [ntok, 256], BF16), "widx": k.dram(f"widx{tag}", [ntok, 32], F32)}


def dsa_layer(k, hT, ntok, seqlen, consts_d, sc, p):
    stage_dsa_a(k, hT, ntok, consts_d[:, 0:128], p, sc)
    stage_dsa_b(k, hT, ntok, seqlen, consts_d[:, 0:128], p, sc)


def stage_in(k, x_d, hT, ntok, ident_d):
    m = k.mark()
    pb = k.pb
    ident = k.sb("ident", [128, 128], F32)
    k.dma("sp", ident[:, :], ident_d, w=[ident])
    X = [k.sb(f"X{i}", [128, D], F32) for i in range(2)]
    stg = [k.sb(f"stg{i}", [128, 4, 128], F32) for i in range(2)]
    si = 0
    for tb in range(ntok // 128):
        Xt = X[tb % 2]
        k.dma("sp", Xt[:, :], x_d[tb * 128:(tb + 1) * 128, :], w=[Xt])
        for k4 in range(4):
            ps = pb[k4 % 2]
            for j in range(4):
                kc = k4 * 4 + j
                k.op("pe", lambda h, ps=ps, Xt=Xt, kc=kc, j=j: h.transpose(ps[:, j * 128:(j + 1) * 128], Xt[:, kc * 128:(kc + 1) * 128], ident[:, :]),
                     r=[Xt, ident], w=[ps], sig=(j == 3))
            St = stg[si % 2]; si += 1
            k.op("act" if k4 % 2 else "dve",
                 (lambda h, ps=ps, St=St: h.copy(out=St[:, :, :], in_=ps[:, :].rearrange("p (a b) -> p a b", a=4))) if k4 % 2 else
                 (lambda h, ps=ps, St=St: h.tensor_copy(out=St[:, :, :], in_=ps[:, :].rearrange("p (a b) -> p a b", a=4))),
                 r=[ps], w=[St])
            k.dma("sp", hT[k4 * 512:(k4 + 1) * 512, tb * 128:(tb + 1) * 128].rearrange("(c p) t -> p c t", p=128), St[:, :, :],
                  r=[St], w=[hT.sub((k4 * 4 + j, (tb * 128) // TT)) for j in range(4)])
    k.barrier()
    k.release(m)


def stage_out(k, hT, out_d, ntok, g_d, ident_d):
    m = k.mark()
    pb = k.pb
    ident = k.sb("ident", [128, 128], F32)
    k.dma("sp", ident[:, :], ident_d, w=[ident])
    gcol = cols_via_T(k, "gcol", g_d.rearrange("(c p) -> c p", p=128), KC, ident, pb[7])
    ones = k.sb("ones", [128, 128], BF16)
    k.op("dve", lambda h: h.memset(ones[:, :], 1.0), w=[ones])
    epsc = k.sb("epsc", [128, 1], F32)
    k.op("dve", lambda h: h.memset(epsc[:, :], EPS), w=[epsc])
    hbuf = k.sb("hbuf", [128, KC, TT], F32)
    yT = k.sb("yT", [128, KC, TT], F32)
    sq = [k.sb(f"sq{i}", [128, TT], BF16) for i in range(2)]
    rstd = k.sb("rstd", [128, TT], F32)
    O = [k.sb(f"O{i}", [128, D], F32) for i in range(2)]
    for ti in range(ntok // TT):
        t0 = ti * TT
        rms_to_uT(k, hT, t0, gcol, hbuf, yT, sq, rstd, ones, pb[4], epsc)
        for tb in range(TT // 128):
            Ot = O[tb % 2]
            for k4 in range(4):
                ps = pb[k4 % 2]
                for j in range(4):
                    kc = k4 * 4 + j
                    k.op("pe", lambda h, ps=ps, kc=kc, j=j, tb=tb: h.transpose(ps[:, j * 128:(j + 1) * 128], yT[:, kc, tb * 128:(tb + 1) * 128],
                                                                               ident[:, :]), r=[yT, ident], w=[ps], sig=(j == 3))
                if k4 % 2:
                    k.op("act", lambda h, ps=ps, Ot=Ot, k4=k4: h.copy(out=Ot[:, k4 * 512:(k4 + 1) * 512], in_=ps[:, :]), r=[ps], w=[Ot])
                else:
                    k.op("dve", lambda h, ps=ps, Ot=Ot, k4=k4: h.tensor_copy(out=Ot[:, k4 * 512:(k4 + 1) * 512], in_=ps[:, :]), r=[ps], w=[Ot])
            k.dma("sp", out_d[t0 + tb * 128:t0 + (tb + 1) * 128, :], Ot[:, :], r=[Ot], w=[out_d])
    k.barrier()
    k.release(m)


NTOK = 4096
SEQ = 2048
PARAM_SHAPES = {
    "norm_mix": [4, 2048], "norm_mlp": [4, 2048], "norm_final": [2048],
    "dn_w_in": [2, 2048, 12352], "dn_conv_w": [2, 4, 8192], "dn_a_log": [2, 32], "dn_dt_bias": [2, 32],
    "dn_out_norm": [2, 128], "dn_w_out": [2, 4096, 2048],
    "dsa_w_in": [2, 2048, 912], "dsa_q_norm": [2, 512], "dsa_kv_norm": [2, 256], "dsa_kidx_norm": [2, 128],
    "dsa_w_uq": [2, 512, 4096], "dsa_w_uk": [2, 16, 128, 256], "dsa_w_uv": [2, 16, 256, 128], "dsa_w_out": [2, 2048, 2048],
    "mlp_w_up": [4, 2048, 8192], "mlp_w_down": [4, 8192, 2048],
}


def build_program(ntok=NTOK, seqlen=SEQ, depth=4):
    nc = bass.Bass("TRN2", target_bir_lowering=False)
    k = K(nc)
    x_d = k.dram("x", [ntok, D], F32, kind="ExternalInput")
    consts = k.dram("consts", [128, 320], F32, kind="ExternalInput")
    P = {nm: k.dram(nm, shp, F32, kind="ExternalInput").t for nm, shp in PARAM_SHAPES.items()}
    out_d = k.dram("out", [ntok, D], F32, kind="ExternalOutput")
    hT = k.dram("hT", [D, ntok], F32)
    dn_sc = {"qk": k.dram("qk_s", [32, 128, ntok], F32), "v": k.dram("v_s", [NH, 128, ntok], F32),
             "z": k.dram("z_s", [NH, 128, ntok], F32), "bg": k.dram("bg_s", [ntok, 64], F32),
             "o": k.dram("o_s", [NH, 128, ntok], F32)}
    dsa_sc = dsa_scratch(k, ntok, "0")
    wsc = {"up": k.dram("wup_bf", [D, 8192], BF16), "dn": k.dram("wdn_bf", [8192, D], BF16),
           "win": k.dram("win_bf", [D, DN_IN], BF16)} if BF16_WSCRATCH else None
    ident_d = consts.t[:, 0:128]
    stage_in(k, x_d.t, hT, ntok, ident_d)
    for i in range(depth):
        j = i // 2
        if i % 2 == 0:
            p = {"g": P["norm_mix"][i], "w_in": P["dn_w_in"][j], "conv_w": P["dn_conv_w"][j], "a_log": P["dn_a_log"][j],
                 "dt_bias": P["dn_dt_bias"][j], "out_norm": P["dn_out_norm"][j], "w_out": P["dn_w_out"][j]}
            dn_layer(k, hT, ntok, seqlen, consts.t, dn_sc, p, wsc=wsc)
        else:
            p = {"g": P["norm_mix"][i], "w_in": P["dsa_w_in"][j], "q_norm": P["dsa_q_norm"][j], "kv_norm": P["dsa_kv_norm"][j],
                 "kidx_norm": P["dsa_kidx_norm"][j], "w_uq": P["dsa_w_uq"][j], "w_uk": P["dsa_w_uk"][j], "w_uv": P["dsa_w_uv"][j],
                 "w_out": P["dsa_w_out"][j]}
            dsa_layer(k, hT, ntok, seqlen, consts.t, dsa_sc, p)
        stage_mlp(k, hT, ntok, P["norm_mlp"][i], P["mlp_w_up"][i], P["mlp_w_down"][i], 8192, wsc=wsc)
    stage_out(k, hT, out_d, ntok, P["norm_final"], ident_d)
    k.finish()
    return nc, k


def kernel(**inputs):
    n = 8
    x = np.ascontiguousarray(np.asarray(inputs["x"], dtype=np.float32))
    B, T, Dm = x.shape
    per = B // n
    consts = make_consts()
    params = {nm: np.ascontiguousarray(np.asarray(inputs[nm], dtype=np.float32)) for nm in PARAM_SHAPES}
    nc, _ = build_program(per * T, T, 4)
    in_maps = []
    for c in range(n):
        mp = {"x": x[c * per:(c + 1) * per].reshape(per * T, Dm), "consts": consts}
        mp.update(params)
        in_maps.append(mp)
    res = run_bass_kernel_spmd(nc, in_maps, core_ids=list(range(n)))
    outs = [np.asarray(r["out"]).reshape(per, T, Dm) for r in res.results]
    return np.concatenate(outs, axis=0).astype(np.float32)
```

```python
import numpy as np
from contextlib import ExitStack
import concourse.bass as bass
import concourse.mybir as mybir
from concourse.bass_utils import run_bass_kernel_spmd

F32 = mybir.dt.float32
BF16 = mybir.dt.bfloat16
AF = mybir.ActivationFunctionType
ALU = mybir.AluOpType
AX = mybir.AxisListType
ARENA_BYTES = 200 * 1024


class Dep:
    __slots__ = ("w", "r", "parent", "children")

    def __init__(self, parent=None):
        self.w = None
        self.r = []
        self.parent = parent
        self.children = []

    def related(self):
        out = [self]
        if self.parent is not None:
            out.append(self.parent)
        out.extend(self.children)
        return out


class Tl:
    def __init__(self, t):
        self.t = t
        self.d = Dep()
        self.subs = {}

    def sub(self, key):
        s = self.subs.get(key)
        if s is None:
            s = self.subs[key] = Dep(self.d)
            self.d.children.append(s)
        return s

    def __getitem__(self, idx):
        return self.t[idx]


def _dep(x):
    return x.d if isinstance(x, Tl) else x


class Eng:
    def __init__(self, name, h, sem, is_pe=False):
        self.name = name
        self.h = h
        self.sem = sem
        self.cnt = 0
        self.pending = False
        self.seen = {}
        self.prog = []
        self.is_pe = is_pe
        self.dsems = []
        self.dcnt = []
        self.dnext = 0


class K:
    def __init__(self, nc, n_dma_sems=8):
        self.nc = nc
        self.es = ExitStack()
        self.e = {}
        for name, h, pe in (("pe", nc.tensor, True), ("act", nc.scalar, False),
                            ("dve", nc.vector, False), ("pool", nc.gpsimd, False),
                            ("sp", nc.sync, False)):
            sem = self.es.enter_context(nc.semaphore("s_" + name))
            self.e[name] = Eng(name, h, sem, pe)
        for q in ("sp", "pool", "act"):
            en = self.e[q]
            for i in range(n_dma_sems):
                en.dsems.append(self.es.enter_context(nc.semaphore(f"d_{q}{i}")))
                en.dcnt.append(0)
        self.n_inst = 0
        self.asize = ARENA_BYTES
        self.aoff = 0
        self.arena = self.es.enter_context(nc.sbuf_tensor("arena", [128, ARENA_BYTES // 4], F32))
        self.psum = self.es.enter_context(nc.psum_tensor("psum", [128, 8 * 512], F32))
        self.pb = [Tl(self.psum[:, i * 512:(i + 1) * 512]) for i in range(8)]

    def sb(self, name, shape, dt):
        shape = list(shape)
        esz = 2 if dt == BF16 else 4
        n = 1
        for x in shape[1:]:
            n *= x
        nbytes = (n * esz + 63) // 64 * 64
        if self.aoff + nbytes > self.asize:
            raise RuntimeError(f"SBUF arena overflow at {name}: {self.aoff}+{nbytes}>{self.asize}")
        v = self.arena[0:shape[0], self.aoff // 4:(self.aoff + nbytes) // 4]
        self.aoff += nbytes
        if dt != F32:
            v = v.bitcast(dt)
        v = v[:, 0:n]
        if len(shape) == 3:
            v = v.rearrange("p (a b) -> p a b", a=shape[1])
        elif len(shape) == 4:
            v = v.rearrange("p (a b c) -> p a b c", a=shape[1], b=shape[2])
        return Tl(v)

    def mark(self):
        return self.aoff

    def release(self, m):
        self.aoff = m

    def dram(self, name, shape, dt, kind="Internal"):
        return Tl(self.nc.dram_tensor(name, list(shape), dt, kind=kind).ap())

    def barrier(self):
        tgt = []
        for name, en in self.e.items():
            if en.pending:
                raise RuntimeError("barrier with pending unsignalled instr on " + name)
            if en.cnt > 0:
                tgt.append((en.sem, en.cnt))
            for s_, c in zip(en.dsems, en.dcnt):
                if c > 0:
                    tgt.append((s_, c))
        for name, en in self.e.items():
            for (s_, v) in tgt:
                if s_ is en.sem:
                    continue
                if en.seen.get(s_, 0) < v:
                    en.seen[s_] = v
                    en.prog.append(("w", s_, v))

    def _waits(self, en, r, w):
        need = {}
        for x in r:
            for d in _dep(x).related():
                if d.w is not None:
                    s, v = d.w
                    if need.get(s, 0) < v:
                        need[s] = v
        for x in w:
            for d in _dep(x).related():
                if d.w is not None:
                    s, v = d.w
                    if need.get(s, 0) < v:
                        need[s] = v
                for (s, v) in d.r:
                    if need.get(s, 0) < v:
                        need[s] = v
        for s, v in need.items():
            if s is en.sem and en.is_pe:
                continue
            if en.seen.get(s, 0) >= v:
                continue
            en.seen[s] = v
            en.prog.append(("w", s, v))

    def _mark(self, ident, r, w):
        for x in w:
            d = _dep(x)
            d.w = ident
            d.r = []
        for x in r:
            d = _dep(x)
            d.r.append(ident)
            if len(d.r) > 64:
                mx = {}
                for (s, v) in d.r:
                    if mx.get(s, 0) < v:
                        mx[s] = v
                d.r = list(mx.items())

    def op(self, eng, fn, r=(), w=(), sig=True):
        en = self.e[eng]
        self._waits(en, r, w)
        if sig:
            en.cnt += 1
            en.pending = False
            en.prog.append(("i", fn, True))
            ident = (en.sem, en.cnt)
        else:
            en.pending = True
            en.prog.append(("i", fn, False))
            ident = (en.sem, en.cnt + 1)
        self._mark(ident, r, w)
        self.n_inst += 1

    def dma(self, q, out, in_, r=(), w=(), **kw):
        en = self.e[q]
        self._waits(en, r, w)
        i = en.dnext
        en.dnext = (i + 1) % len(en.dsems)
        s = en.dsems[i]
        if en.dcnt[i] > 0 and en.seen.get(s, 0) < en.dcnt[i]:
            en.seen[s] = en.dcnt[i]
            en.prog.append(("w", s, en.dcnt[i]))
        en.dcnt[i] += 16
        en.prog.append(("d", out, in_, s, kw))
        self._mark((s, en.dcnt[i]), r, w)
        self.n_inst += 1

    def finish(self, final_deps=()):
        nc = self.nc
        for name, en in self.e.items():
            if en.pending:
                raise RuntimeError(f"engine {name} has unsignalled trailing instructions")
        sp = self.e["sp"]
        self._waits(sp, list(final_deps), [])
        for q in ("sp", "pool", "act"):
            en = self.e[q]
            for s, c in zip(en.dsems, en.dcnt):
                if c > 0 and sp.seen.get(s, 0) < c:
                    sp.seen[s] = c
                    sp.prog.append(("w", s, c))
        for name in ("pe", "act", "dve", "pool"):
            en = self.e[name]
            if en.cnt > 0 and sp.seen.get(en.sem, 0) < en.cnt:
                sp.prog.append(("w", en.sem, en.cnt))

        def run(en, h):
            for it in en.prog:
                if it[0] == "w":
                    h.wait_ge(it[1], it[2])
                elif it[0] == "i":
                    ins = it[1](h)
                    if it[2]:
                        ins.then_inc(en.sem, 1)
                else:
                    h.dma_start(out=it[1], in_=it[2], **it[4]).then_inc(it[3], 16)

        with nc.Block() as block:
            @block.tensor
            def _(h):
                run(self.e["pe"], h)

            @block.scalar
            def _(h):
                run(self.e["act"], h)

            @block.vector
            def _(h):
                run(self.e["dve"], h)

            @block.gpsimd
            def _(h):
                run(self.e["pool"], h)

            @block.sync
            def _(h):
                run(self.e["sp"], h)
        self.es.close()


D = 2048
KC = D // 128
EPS = 1e-6
TT = 512


def load_cols(k, name, src, n):
    t = k.sb(name, [128, n], F32)
    k.dma("sp", t[:, :], src.rearrange("(c p) -> p c", p=128), w=[t],
          allow_slow_non_contiguous=True)
    return t


def rms_to_uT(k, hT, t0, gcol, hbuf, uT, sq, rstd, ones, pss, epsc, u32=False):
    for kc in range(KC):
        k.dma("sp", hbuf[:, kc, :], hT[kc * 128:(kc + 1) * 128, t0:t0 + TT], w=[hbuf.sub(kc)])
    for kc in range(KC):
        k.op("act", lambda h, kc=kc: h.activation(out=sq[kc % 2][:, :], in_=hbuf[:, kc, :], func=AF.Square),
             r=[hbuf.sub(kc)], w=[sq[kc % 2]])
        k.op("pe", lambda h, kc=kc: h.matmul(pss[:, :], lhsT=ones[:, :], rhs=sq[kc % 2][:, :],
                                            start=(kc == 0), stop=(kc == KC - 1)),
             r=[sq[kc % 2], ones], w=[pss])
    k.op("act", lambda h: h.activation(out=rstd[:, :], in_=pss[:, :], func=AF.Sqrt, scale=1.0 / D, bias=epsc[:, :]),
         r=[pss, epsc], w=[rstd])
    k.op("dve", lambda h: h.reciprocal(out=rstd[:, :], in_=rstd[:, :]), r=[rstd], w=[rstd])
    for kc in range(KC):
        k.op("dve", lambda h, kc=kc: h.scalar_tensor_tensor(out=uT[:, kc, :], in0=hbuf[:, kc, :],
                                                            scalar=gcol[:, kc:kc + 1], in1=rstd[:, :],
                                                            op0=ALU.mult, op1=ALU.mult),
             r=[hbuf.sub(kc), gcol, rstd], w=[uT])
        if u32:
            k.op("dve", lambda h, kc=kc: h.scalar_tensor_tensor(out=hbuf[:, kc, :], in0=hbuf[:, kc, :],
                                                                scalar=gcol[:, kc:kc + 1], in1=rstd[:, :],
                                                                op0=ALU.mult, op1=ALU.mult),
                 r=[hbuf.sub(kc), gcol, rstd], w=[hbuf.sub(kc)])


def to_bf16_dram(k, dst, src, rows, blk=128):
    deps = []
    for rb in range(rows // blk):
        k.dma("pool", dst[rb * blk:(rb + 1) * blk, :], src[rb * blk:(rb + 1) * blk, :], w=[dst.sub(rb)])
        deps.append(dst.sub(rb))
    return deps


def stage_mlp(k, hT, ntok, g_d, wup_d, wdn_d, F, wsc=None):
    m = k.mark()
    updeps, dndeps = [], []
    if wsc is not None:
        updeps = to_bf16_dram(k, wsc["up"], wup_d, D)
        dndeps = to_bf16_dram(k, wsc["dn"], wdn_d, F)
        wup_d = wsc["up"].t
        wdn_d = wsc["dn"].t
    FC = F // 128
    gcol = load_cols(k, "gcol", g_d, KC)
    ones = k.sb("ones", [128, 128], BF16)
    k.op("dve", lambda h: h.memset(ones[:, :], 1.0), w=[ones])
    epsc = k.sb("epsc", [128, 1], F32)
    k.op("dve", lambda h: h.memset(epsc[:, :], EPS), w=[epsc])
    hbuf = k.sb("hbuf", [128, KC, TT], F32)
    uT = [k.sb(f"uT{i}", [128, KC, TT], BF16) for i in range(1)]
    aT = k.sb("aT", [128, FC, TT], BF16)
    sq = [k.sb(f"sq{i}", [128, TT], BF16) for i in range(2)]
    rstd = k.sb("rstd", [128, TT], F32)
    tmp = [k.sb(f"tmp{i}", [128, TT], BF16) for i in range(2)]
    wu = [k.sb(f"wu{i}", [128, KC, 512], BF16) for i in range(2)]
    wd = [k.sb(f"wd{i}", [128, FC, 128], BF16) for i in range(2)]
    wupv = wup_d.rearrange("(kc p) f -> p kc f", p=128)
    wdnv = wdn_d.rearrange("(fc p) d -> p fc d", p=128)
    pb = k.pb
    nmm = 0
    for ti in range(ntok // TT):
        t0 = ti * TT
        u = uT[0]
        rms_to_uT(k, hT, t0, gcol, hbuf, u, sq, rstd, ones, pb[4], epsc)
        for fb in range(F // 512):
            W = wu[fb % 2]
            k.dma("pool", W[:, :, :], wupv[:, :, fb * 512:(fb + 1) * 512], r=updeps, w=[W])
            for fc in range(4):
                fi = fb * 4 + fc
                ps = pb[fi % 2]
                for kc in range(KC):
                    k.op("pe", lambda h, ps=ps, W=W, kc=kc, fc=fc, u=u: h.matmul(
                        ps[:, :], lhsT=W[:, kc, fc * 128:(fc + 1) * 128], rhs=u[:, kc, :],
                        start=(kc == 0), stop=(kc == KC - 1)),
                        r=[W, u], w=[ps], sig=(kc == KC - 1))
                tm = tmp[fi % 2]
                k.op("act", lambda h, ps=ps, tm=tm: h.activation(out=tm[:, :], in_=ps[:, :], func=AF.Relu),
                     r=[ps], w=[tm])
                k.op("dve", lambda h, tm=tm, fi=fi: h.tensor_tensor(out=aT[:, fi, :], in0=tm[:, :], in1=tm[:, :],
                                                                    op=ALU.mult),
                     r=[tm], w=[aT])
        for dc in range(KC):
            W = wd[dc % 2]
            k.dma("pool", W[:, :, :], wdnv[:, :, dc * 128:(dc + 1) * 128], r=dndeps, w=[W])
            ps = pb[2 + dc % 2]
            for fc in range(FC):
                k.op("pe", lambda h, ps=ps, W=W, fc=fc: h.matmul(
                    ps[:, :], lhsT=W[:, fc, :], rhs=aT[:, fc, :], start=(fc == 0), stop=(fc == FC - 1)),
                    r=[W, aT], w=[ps], sig=(fc == FC - 1))
            k.op("dve", lambda h, ps=ps, dc=dc: h.tensor_tensor(out=hbuf[:, dc, :], in0=hbuf[:, dc, :],
                                                                in1=ps[:, :], op=ALU.add),
                 r=[ps, hbuf.sub(dc)], w=[hbuf.sub(dc)])
            k.dma("sp", hT[dc * 128:(dc + 1) * 128, t0:t0 + TT], hbuf[:, dc, :],
                  r=[hbuf.sub(dc)], w=[hT.sub((dc, ti))])
    k.barrier()
    k.release(m)


NH = 32
NQK = 16
CH = 64
QK_FP32 = False
BF16_WSCRATCH = True
DN_IN = 12352


def bcast_row(k, name, src, n, q="sp"):
    t = k.sb(name, [128, n], F32)
    k.dma(q, t[:, :], src.partition_broadcast(128), w=[t])
    return t


def cols_via_T(k, name, src2d, R, ident, ps):
    tmp = k.sb(name + "_r", [R, 128], F32)
    out = k.sb(name, [128, R], F32)
    k.dma("sp", tmp[:, :], src2d, w=[tmp])
    k.op("pe", lambda h: h.transpose(ps[:, 0:R], tmp[:, :], ident[0:R, 0:R]), r=[tmp, ident], w=[ps])
    k.op("dve", lambda h: h.tensor_copy(out=out[:, :], in_=ps[:, 0:R]), r=[ps], w=[out])
    return out


def stage_dn_a(k, hT, ntok, seqlen, ident_d, g_d, win_d, convw_d, alog_d, dtb_d, qk_s, v_s, z_s, bg_s, wsc=None):
    m = k.mark()
    wdeps = []
    win32_d = win_d
    if wsc is not None:
        wdeps = to_bf16_dram(k, wsc["win"], win_d, D)
        win_d = wsc["win"].t
    pb = k.pb
    ident = k.sb("ident", [128, 128], F32)
    k.dma("sp", ident[:, :], ident_d, w=[ident])
    gcol = cols_via_T(k, "gcol", g_d.rearrange("(c p) -> c p", p=128), KC, ident, pb[7])
    cwv = convw_d.rearrange("j (c p) -> (j c) p", p=128)
    cw0 = cols_via_T(k, "cw0", cwv[0:128, :], 128, ident, pb[7])
    cw1 = cols_via_T(k, "cw1", cwv[128:256, :], 128, ident, pb[7])

    def cwcol(j, c):
        idx = j * 64 + c
        t = cw0 if idx < 128 else cw1
        idx = idx % 128
        return t[:, idx:idx + 1], t

    ones = k.sb("ones", [128, 128], BF16)
    k.op("dve", lambda h: h.memset(ones[:, :], 1.0), w=[ones])
    epsc = k.sb("epsc", [128, 1], F32)
    k.op("dve", lambda h: h.memset(epsc[:, :], EPS), w=[epsc])
    onec = k.sb("onec", [128, 1], F32)
    k.op("dve", lambda h: h.memset(onec[:, :], 1.0), w=[onec])
    nA = bcast_row(k, "nA", alog_d, NH)
    k.op("act", lambda h: h.activation(out=nA[:, :], in_=nA[:, :], func=AF.Exp), r=[nA], w=[nA])
    k.op("dve", lambda h: h.tensor_scalar(out=nA[:, :], in0=nA[:, :], scalar1=-1.0, scalar2=None, op0=ALU.mult),
         r=[nA], w=[nA])
    dtb = bcast_row(k, "dtb", dtb_d, NH)
    hbuf = k.sb("hbuf", [128, KC, TT], F32)
    uT = k.sb("uT", [128, KC, TT], BF16)
    sq = [k.sb(f"sq{i}", [128, TT], BF16) for i in range(2)]
    rstd = k.sb("rstd", [128, TT], F32)
    wt = [k.sb(f"wt{i}", [128, KC, 512], BF16) for i in range(2)]
    wba = k.sb("wba", [128, KC, 64], BF16)
    wt32 = [k.sb(f"wt32_{i}", [128, KC, 512], F32) for i in range(2)]
    hist = k.sb("hist", [128, 64, 3], F32)
    pre = [k.sb(f"pre{i}", [128, TT + 3], F32) for i in range(2)]
    acc = [k.sb(f"acc{i}", [128, TT], F32) for i in range(2)]
    sl = [k.sb(f"sl{i}", [128, TT], F32) for i in range(2)]
    rn = [k.sb(f"rn{i}", [128, TT], F32) for i in range(2)]
    ost = [k.sb(f"ost{i}", [128, TT], F32) for i in range(3)]
    bgt = [k.sb(f"bgt{i}", [128, 64], F32) for i in range(2)]
    bgx = [k.sb(f"bgx{i}", [128, 32], F32) for i in range(2)]
    winv = win_d.rearrange("(kc p) f -> p kc f", p=128)
    winv32 = win32_d.rearrange("(kc p) f -> p kc f", p=128)
    k.dma("pool", wba[:, :, :], winv[:, :, 12288:12352], r=wdeps, w=[wba])
    ci = 0
    for ti in range(ntok // TT):
        t0 = ti * TT
        rms_to_uT(k, hT, t0, gcol, hbuf, uT, sq, rstd, ones, pb[4], epsc, u32=True)
        if t0 % seqlen == 0:
            k.op("dve", lambda h: h.memset(hist[:, :, :], 0.0), w=[hist])
        for fb in range(24):
            hi = QK_FP32 and fb < 8
            if hi:
                W = wt32[fb % 2]
                k.dma("act" if fb % 2 else "sp", W[:, :, :], winv32[:, :, fb * 512:(fb + 1) * 512], w=[W])
            else:
                W = wt[fb % 2]
                k.dma("pool", W[:, :, :], winv[:, :, fb * 512:(fb + 1) * 512], r=wdeps, w=[W])
            for fc in range(4):
                c = fb * 4 + fc
                ps = pb[c % 2]
                for kc in range(KC):
                    if hi:
                        k.op("pe", lambda h, ps=ps, W=W, kc=kc, fc=fc: h.matmul(
                            ps[:, :], lhsT=W[:, kc, fc * 128:(fc + 1) * 128], rhs=hbuf[:, kc, :],
                            start=(kc == 0), stop=(kc == KC - 1)), r=[W, hbuf.sub(kc)], w=[ps], sig=(kc == KC - 1))
                    else:
                        k.op("pe", lambda h, ps=ps, W=W, kc=kc, fc=fc: h.matmul(
                            ps[:, :], lhsT=W[:, kc, fc * 128:(fc + 1) * 128], rhs=uT[:, kc, :],
                            start=(kc == 0), stop=(kc == KC - 1)), r=[W, uT], w=[ps], sig=(kc == KC - 1))
                o_t = ost[ci % 3]
                ci += 1
                if c < 64:
                    P = pre[c % 2]
                    A = acc[c % 2]
                    S = sl[c % 2]
                    k.op("act", lambda h, P=P, c=c: h.copy(out=P[:, 0:3], in_=hist[:, c, :]), r=[hist.sub(c)], w=[P])
                    k.op("act", lambda h, P=P, ps=ps: h.copy(out=P[:, 3:TT + 3], in_=ps[:, :]), r=[ps], w=[P])
                    k.op("act", lambda h, P=P, c=c: h.copy(out=hist[:, c, :], in_=P[:, TT:TT + 3]), r=[P], w=[hist.sub(c)])
                    for j in range(4):
                        col, ct = cwcol(j, c)
                        if j == 0:
                            k.op("dve", lambda h, A=A, P=P, col=col: h.tensor_scalar(
                                out=A[:, :], in0=P[:, 0:TT], scalar1=col, scalar2=None, op0=ALU.mult),
                                r=[P, ct], w=[A])
                        else:
                            k.op("dve", lambda h, A=A, P=P, col=col, j=j: h.scalar_tensor_tensor(
                                out=A[:, :], in0=P[:, j:j + TT], scalar=col, in1=A[:, :], op0=ALU.mult, op1=ALU.add),
                                r=[P, ct, A], w=[A])
                    if c < 32:
                        k.op("act", lambda h, S=S, A=A: h.activation(out=S[:, :], in_=A[:, :], func=AF.Silu), r=[A], w=[S])
                        q2 = sq[c % 2]
                        k.op("act", lambda h, S=S, q2=q2: h.activation(out=q2[:, :], in_=S[:, :], func=AF.Square), r=[S], w=[q2])
                        k.op("pe", lambda h, q2=q2: h.matmul(pb[5][:, :], lhsT=ones[:, :], rhs=q2[:, :], start=True, stop=True),
                             r=[q2, ones], w=[pb[5]])
                        R = rn[c % 2]
                        k.op("act", lambda h, R=R: h.activation(out=R[:, :], in_=pb[5][:, :], func=AF.Sqrt, scale=1.0,
                                                                bias=epsc[:, :]), r=[pb[5], epsc], w=[R])
                        k.op("dve", lambda h, R=R: h.reciprocal(out=R[:, :], in_=R[:, :]), r=[R], w=[R])
                        scl = (128.0 ** -0.5) if c < 16 else 1.0
                        k.op("dve", lambda h, o_t=o_t, S=S, R=R, scl=scl: h.scalar_tensor_tensor(
                            out=o_t[:, :], in0=S[:, :], scalar=scl, in1=R[:, :], op0=ALU.mult, op1=ALU.mult),
                            r=[S, R], w=[o_t])
                        k.dma("sp", qk_s[c, :, t0:t0 + TT], o_t[:, :], r=[o_t], w=[qk_s.sub((c, ti))])
                    else:
                        k.op("act", lambda h, o_t=o_t, A=A: h.activation(out=o_t[:, :], in_=A[:, :], func=AF.Silu), r=[A], w=[o_t])
                        k.dma("sp", v_s[c - 32, :, t0:t0 + TT], o_t[:, :], r=[o_t], w=[v_s.sub((c, ti))])
                else:
                    k.op("act", lambda h, o_t=o_t, ps=ps: h.activation(out=o_t[:, :], in_=ps[:, :], func=AF.Silu), r=[ps], w=[o_t])
                    k.dma("sp", z_s[c - 64, :, t0:t0 + TT], o_t[:, :], r=[o_t], w=[z_s.sub((c, ti))])
        for sbi in range(TT // 128):
            ps = pb[2 + sbi % 2]
            for kc in range(KC):
                k.op("pe", lambda h, ps=ps, kc=kc, sbi=sbi: h.matmul(
                    ps[:, 0:64], lhsT=uT[:, kc, sbi * 128:(sbi + 1) * 128], rhs=wba[:, kc, :],
                    start=(kc == 0), stop=(kc == KC - 1)), r=[uT, wba], w=[ps], sig=(kc == KC - 1))
            B = bgt[sbi % 2]
            X = bgx[sbi % 2]
            k.op("act", lambda h, B=B, ps=ps: h.activation(out=B[:, 0:32], in_=ps[:, 0:32], func=AF.Sigmoid), r=[ps], w=[B])
            k.op("dve", lambda h, X=X, ps=ps: h.tensor_tensor(out=X[:, :], in0=ps[:, 32:64], in1=dtb[:, :], op=ALU.add),
                 r=[ps, dtb], w=[X])
            k.op("act", lambda h, X=X: h.activation(out=X[:, :], in_=X[:, :], func=AF.Exp), r=[X], w=[X])
            k.op("act", lambda h, X=X: h.activation(out=X[:, :], in_=X[:, :], func=AF.Ln, scale=1.0, bias=onec[:, :]),
                 r=[X, onec], w=[X])
            k.op("dve", lambda h, B=B, X=X: h.tensor_tensor(out=B[:, 32:64], in0=X[:, :], in1=nA[:, :], op=ALU.mult),
                 r=[X, nA, B], w=[B])
            k.dma("sp", bg_s[t0 + sbi * 128:t0 + (sbi + 1) * 128, :], B[:, :], r=[B], w=[bg_s.sub((ti, sbi))])
    k.barrier()
    k.release(m)


def bc(ap, shape, axis):
    return ap.unsqueeze(axis).to_broadcast(list(shape))


def make_consts():
    c = np.zeros((128, 320), np.float32)
    c[:, 0:128] = np.eye(128, dtype=np.float32)
    i = np.arange(64)
    c[0:64, 128:192] = (i[:, None] <= i[None, :]).astype(np.float32)
    c[0:64, 192:256] = np.where(i[:, None] > i[None, :], 0.0, -30000.0)
    c[0:64, 256:320] = np.where(i[None, :] >= i[:, None], 0.0, -30000.0)
    return c


def stage_dn_b(k, ntok, seqlen, consts_d, qk_s, v_s, bg_s, o_s):
    m = k.mark()
    pb = k.pb
    cst = k.sb("cst", [128, 320], F32)
    k.dma("sp", cst[:, :], consts_d, w=[cst])
    ident = cst[:, 0:128]
    U = cst[0:64, 128:192]
    NEG12 = cst[0:64, 192:320]
    ones64 = k.sb("ones64", [64, 128], F32)
    k.op("dve", lambda h: h.memset(ones64[:, :], 1.0), w=[ones64])
    negones = k.sb("negones", [64, 64], F32)
    k.op("dve", lambda h: h.memset(negones[:, :], -1.0), w=[negones])
    onesneg = k.sb("onesneg", [64, 128], F32)
    k.op("dve", lambda h: h.memset(onesneg[:, 0:64], 1.0), w=[onesneg])
    k.op("dve", lambda h: h.memset(onesneg[:, 64:128], -1.0), w=[onesneg])
    S = k.sb("S", [128, NH, 128], F32)
    QK = k.sb("QK", [128, NQK, 128], F32)
    Vt = k.sb("Vt", [128, NH, 64], F32)
    bg = k.sb("bg", [64, 64], F32)
    sm = {n: k.sb(n, [64, 32], F32) for n in ("gc", "egc", "ekd", "c1", "nbeta", "tmpd")}
    dS = k.sb("dS", [128, 32], F32)
    kdec = k.sb("kdec", [64, NH, 128], F32)
    Vb = k.sb("Vb", [64, NH, 128], F32)
    GU = k.sb("GU", [64, NH, 64], F32)
    big1 = k.sb("big1", [64, NH, 128], F32)
    Pb = [k.sb(f"P{i}", [64, NH, 64], F32) for i in range(2)]
    Ptb = [k.sb(f"Pt{i}", [64, NH, 64], F32) for i in range(2)]
    R = k.sb("R", [64, NH, 64], F32)
    intraT = k.sb("intraT", [64, NH, 64], F32)
    QdT = k.sb("QdT", [128, NH, 64], F32)
    vn = k.sb("vn", [64, NH, 128], F32)
    ost = k.sb("ost", [128, NH, 64], F32)
    D12 = big1
    Rr = big1
    cstd = [cst]

    def subs(T, keys):
        return [T.sub(x) for x in keys]

    G8 = range(4)
    for c in range(ntok // CH):
        t0 = c * CH
        if t0 % seqlen == 0:
            k.op("dve", lambda h: h.memset(S[:, :, :], 0.0), w=subs(S, range(NH)))
        k.dma("sp", QK[:, :, 0:64], qk_s[16:32, :, t0:t0 + CH].rearrange("c p t -> p c t"), w=[QK])
        k.dma("sp", QK[:, :, 64:128], qk_s[0:16, :, t0:t0 + CH].rearrange("c p t -> p c t"), w=[QK])
        k.dma("sp", Vt[:, :, :], v_s[:, :, t0:t0 + CH].rearrange("c p t -> p c t"), w=[Vt])
        k.dma("sp", bg[:, :], bg_s[t0:t0 + CH, :], w=[bg])
        beta = bg[:, 0:32]
        g = bg[:, 32:64]
        k.op("pe", lambda h: h.matmul(pb[0][0:64, 0:32], lhsT=U, rhs=g, start=True, stop=True), r=[cst, bg], w=[pb[0]])
        k.op("pe", lambda h: h.matmul(pb[1][:, 0:32], lhsT=ones64[:, :], rhs=g, start=True, stop=True), r=[ones64, bg], w=[pb[1]])
        k.op("act", lambda h: h.copy(out=sm["gc"][:, :], in_=pb[0][0:64, 0:32]), r=[pb[0]], w=[sm["gc"]])
        k.op("act", lambda h: h.activation(out=sm["egc"][:, :], in_=pb[0][0:64, 0:32], func=AF.Exp), r=[pb[0]], w=[sm["egc"]])
        k.op("act", lambda h: h.activation(out=dS[:, :], in_=pb[1][:, 0:32], func=AF.Exp), r=[pb[1]], w=[dS])
        k.op("dve", lambda h: h.tensor_tensor(out=sm["tmpd"][:, :], in0=pb[1][0:64, 0:32], in1=sm["gc"][:, :], op=ALU.subtract),
             r=[pb[1], sm["gc"]], w=[sm["tmpd"]])
        k.op("act", lambda h: h.activation(out=sm["ekd"][:, :], in_=sm["tmpd"][:, :], func=AF.Exp), r=[sm["tmpd"]], w=[sm["ekd"]])
        k.op("dve", lambda h: h.scalar_tensor_tensor(out=sm["c1"][:, :], in0=beta, scalar=-1.0, in1=sm["egc"][:, :],
                                                     op0=ALU.mult, op1=ALU.mult), r=[bg, sm["egc"]], w=[sm["c1"]])
        k.op("dve", lambda h: h.tensor_scalar(out=sm["nbeta"][:, :], in0=beta, scalar1=-1.0, scalar2=None, op0=ALU.mult),
             r=[bg], w=[sm["nbeta"]])
        k.op("dve", lambda h: h.tensor_tensor(out=GU[:, :, :], in0=bc(U, [64, NH, 64], 1), in1=bc(g, [64, NH, 64], 2),
                                              op=ALU.mult), r=[cst, bg], w=[GU])
        for b4 in range(8):
            ps = pb[2 + b4 % 4]
            for hh in range(4):
                h_ = b4 * 4 + hh
                o0 = hh * 128
                k.op("pe", lambda h, ps=ps, h_=h_, o0=o0: h.matmul(ps[0:64, o0:o0 + 128], lhsT=GU[:, h_, :], rhs=onesneg[:, :],
                                                                   start=True, stop=False), r=[GU, onesneg], w=[ps], sig=False)
                k.op("pe", lambda h, ps=ps, h_=h_, o0=o0: h.matmul(ps[0:64, o0:o0 + 64], lhsT=negones[:, :], rhs=GU[:, h_, :],
                                                                   start=False, stop=False), r=[GU, negones], w=[ps], sig=False)
                k.op("pe", lambda h, ps=ps, h_=h_, o0=o0: h.matmul(ps[0:64, o0 + 64:o0 + 128], lhsT=ones64[:, 0:64], rhs=GU[:, h_, :],
                                                                   start=False, stop=False), r=[GU, ones64], w=[ps], sig=False)
                k.op("pe", lambda h, ps=ps, h_=h_, o0=o0: h.matmul(ps[0:64, o0:o0 + 128], lhsT=ident[0:64, 0:64], rhs=NEG12,
                                                                   start=False, stop=True), r=[cst], w=[ps], sig=(hh == 3))
            k.op("act", lambda h, ps=ps, b4=b4: h.activation(out=D12[:, b4 * 4:b4 * 4 + 4, :],
                                                             in_=ps[0:64, :].rearrange("p (a b) -> p a b", a=4), func=AF.Exp),
                 r=[ps], w=[D12.sub(b4)])
        N = Ptb[0]
        M = Pb[0]
        for g8 in G8:
            ps = pb[g8 % 2]
            for q4 in range(4):
                hq = g8 * 4 + q4
                k.op("pe", lambda h, ps=ps, hq=hq, q4=q4: h.matmul(ps[0:64, q4 * 128:(q4 + 1) * 128], lhsT=QK[:, hq, 0:64],
                                                                   rhs=QK[:, hq, :], start=True, stop=True),
                     r=[QK], w=[ps], sig=(q4 == 3))
            psv = ps[0:64, :].rearrange("p (a b) -> p a b", a=4)
            for par in range(2):
                hs = slice(g8 * 8 + par, g8 * 8 + 8, 2)
                k.op("dve", lambda h, psv=psv, hs=hs: h.tensor_tensor(out=N[:, hs, :], in0=psv[:, :, 0:64], in1=D12[:, hs, 0:64],
                                                                      op=ALU.mult),
                     r=[ps, D12.sub(2 * g8), D12.sub(2 * g8 + 1)], w=[N.sub(g8)])
                k.op("dve", lambda h, psv=psv, hs=hs: h.tensor_tensor(out=intraT[:, hs, :], in0=psv[:, :, 64:128],
                                                                      in1=D12[:, hs, 64:128], op=ALU.mult),
                     r=[ps, D12.sub(2 * g8), D12.sub(2 * g8 + 1)], w=[intraT.sub(g8)])
            k.op("dve", lambda h, g8=g8: h.tensor_tensor(out=N[:, g8 * 8:g8 * 8 + 8, :], in0=N[:, g8 * 8:g8 * 8 + 8, :],
                                                         in1=bc(sm["nbeta"][:, g8 * 8:g8 * 8 + 8], [64, 8, 64], 2), op=ALU.mult),
                 r=[N.sub(g8), sm["nbeta"]], w=[N.sub(g8)])
        for g8 in G8:
            ps = pb[2 + g8 % 2]
            for i8 in range(8):
                h_ = g8 * 8 + i8
                k.op("pe", lambda h, ps=ps, h_=h_, i8=i8: h.transpose(ps[0:64, i8 * 64:(i8 + 1) * 64], N[:, h_, :], ident[0:64, 0:64]),
                     r=[N.sub(g8), cst], w=[ps], sig=(i8 == 7))
            k.op("act", lambda h, ps=ps, g8=g8: h.copy(out=M[:, g8 * 8:g8 * 8 + 8, :], in_=ps[0:64, :].rearrange("p (a b) -> p a b", a=8)),
                 r=[ps], w=[M.sub(g8)])
            k.op("dve", lambda h, g8=g8: h.tensor_tensor(out=R[:, g8 * 8:g8 * 8 + 8, :], in0=M[:, g8 * 8:g8 * 8 + 8, :],
                                                         in1=bc(ident[0:64, 0:64], [64, 8, 64], 1), op=ALU.add),
                 r=[M.sub(g8), cst], w=[R.sub(g8)])
        cur = 0
        for lvl in range(1, 6):
            Po, Pto = Pb[cur], Ptb[cur]
            Pn, Ptn = Pb[1 - cur], Ptb[1 - cur]
            for g8 in G8:
                psP = pb[0 + g8 % 2]
                psT = pb[2 + g8 % 2]
                psR = pb[4 + g8 % 2]
                for i8 in range(8):
                    h_ = g8 * 8 + i8
                    cs = slice(i8 * 64, (i8 + 1) * 64)
                    if lvl < 5:
                        k.op("pe", lambda h, psP=psP, h_=h_, cs=cs, Po=Po, Pto=Pto: h.matmul(
                            psP[0:64, cs], lhsT=Pto[:, h_, :], rhs=Po[:, h_, :], start=True, stop=True),
                            r=[Po.sub(g8), Pto.sub(g8)], w=[psP], sig=(i8 == 7))
                    k.op("pe", lambda h, psT=psT, h_=h_, cs=cs, Po=Po, Pto=Pto: h.matmul(
                        psT[0:64, cs], lhsT=Po[:, h_, :], rhs=Pto[:, h_, :], start=True, stop=True),
                        r=[Po.sub(g8), Pto.sub(g8)], w=[psT], sig=(i8 == 7))
                if lvl < 5:
                    k.op("act", lambda h, psP=psP, g8=g8, Pn=Pn: h.copy(out=Pn[:, g8 * 8:g8 * 8 + 8, :],
                                                                        in_=psP[0:64, :].rearrange("p (a b) -> p a b", a=8)),
                         r=[psP], w=[Pn.sub(g8)])
                k.op("dve", lambda h, psT=psT, g8=g8, Ptn=Ptn: h.tensor_copy(out=Ptn[:, g8 * 8:g8 * 8 + 8, :],
                                                                             in_=psT[0:64, :].rearrange("p (a b) -> p a b", a=8)),
                     r=[psT], w=[Ptn.sub(g8)])
                for i8 in range(8):
                    h_ = g8 * 8 + i8
                    cs = slice(i8 * 64, (i8 + 1) * 64)
                    k.op("pe", lambda h, psR=psR, h_=h_, cs=cs, Ptn=Ptn: h.matmul(
                        psR[0:64, cs], lhsT=Ptn[:, h_, :], rhs=R[:, h_, :], start=True, stop=True),
                        r=[Ptn.sub(g8), R.sub(g8)], w=[psR], sig=(i8 == 7))
                k.op("dve", lambda h, psR=psR, g8=g8: h.tensor_tensor(out=R[:, g8 * 8:g8 * 8 + 8, :], in0=R[:, g8 * 8:g8 * 8 + 8, :],
                                                                      in1=psR[0:64, :].rearrange("p (a b) -> p a b", a=8), op=ALU.add),
                     r=[psR, R.sub(g8)], w=[R.sub(g8)])
            cur = 1 - cur
        DG = GU
        k.op("dve", lambda h: h.tensor_tensor(out=DG[:, :, :], in0=bc(ident[0:64, 0:64], [64, NH, 64], 1),
                                              in1=bc(sm["egc"][:, :], [64, NH, 64], 2), op=ALU.mult),
             r=[cst, sm["egc"]], w=[DG])
        for g8 in G8:
            ps = pb[6 + g8 % 2]
            for i8 in range(8):
                h_ = g8 * 8 + i8
                k.op("pe", lambda h, ps=ps, h_=h_, i8=i8: h.matmul(ps[:, i8 * 64:(i8 + 1) * 64], lhsT=ones64[:, :], rhs=DG[:, h_, :],
                                                                   start=True, stop=True), r=[ones64, DG], w=[ps], sig=(i8 == 7))
            psv = ps[:, :].rearrange("p (a b) -> p a b", a=8)
            for par in range(2):
                hs = slice(g8 * 8 + par, g8 * 8 + 8, 2)
                k.op("dve", lambda h, psv=psv, hs=hs, g8=g8, par=par: h.tensor_tensor(
                    out=QdT[:, hs, :], in0=QK[:, g8 * 4:g8 * 4 + 4, 64:128], in1=psv[:, par:8:2, :], op=ALU.mult),
                    r=[ps, QK], w=[QdT.sub(g8)])
        for g8 in G8:
            ps = pb[g8 % 2]
            for q4 in range(4):
                hq = g8 * 4 + q4
                k.op("pe", lambda h, ps=ps, hq=hq, q4=q4: h.transpose(ps[0:64, q4 * 128:(q4 + 1) * 128], QK[:, hq, 0:64], ident),
                     r=[QK, cst], w=[ps], sig=(q4 == 3))
            psv = ps[0:64, :].rearrange("p (a b) -> p a b", a=4)
            for par in range(2):
                hs = slice(g8 * 8 + par, g8 * 8 + 8, 2)
                k.op("dve", lambda h, psv=psv, hs=hs: h.tensor_tensor(out=kdec[:, hs, :], in0=psv,
                                                                      in1=bc(sm["ekd"][:, hs], [64, 4, 128], 2), op=ALU.mult),
                     r=[ps, sm["ekd"]], w=[kdec.sub(g8)])
        for b4 in range(8):
            ps = pb[2 + b4 % 2]
            for hh in range(4):
                h_ = b4 * 4 + hh
                k.op("pe", lambda h, ps=ps, h_=h_, hh=hh: h.transpose(ps[0:64, hh * 128:(hh + 1) * 128], Vt[:, h_, :], ident),
                     r=[Vt, cst], w=[ps], sig=(hh == 3))
            k.op("dve", lambda h, ps=ps, b4=b4: h.tensor_tensor(out=Vb[:, b4 * 4:b4 * 4 + 4, :],
                                                                in0=ps[0:64, :].rearrange("p (a b) -> p a b", a=4),
                                                                in1=bc(beta[:, b4 * 4:b4 * 4 + 4], [64, 4, 128], 2), op=ALU.mult),
                 r=[ps, bg], w=[Vb.sub(b4)])
        for h_ in range(NH):
            hq = h_ // 2
            g8 = h_ // 8
            ks = pb[0][0:64, (h_ % 4) * 128:(h_ % 4 + 1) * 128]
            ksd = pb[0].sub(h_ % 4)
            vp = pb[1][0:64, (h_ % 4) * 128:(h_ % 4 + 1) * 128]
            vpd = pb[1].sub(h_ % 4)
            dsb = pb[2 + (h_ % 8) // 4]
            dsp = dsb[:, (h_ % 4) * 128:(h_ % 4 + 1) * 128]
            dsd = dsb.sub(h_ % 4)
            ob = pb[4 + g8]
            op_ = ob[:, (h_ % 8) * 64:(h_ % 8 + 1) * 64]
            k.op("pe", lambda h, ks=ks, hq=hq, h_=h_: h.matmul(ks, lhsT=QK[:, hq, 0:64], rhs=S[:, h_, :], start=True, stop=True),
                 r=[QK, S.sub(h_)], w=[ksd])
            k.op("dve", lambda h, ks=ks, h_=h_: h.scalar_tensor_tensor(out=Rr[:, h_, :], in0=ks, scalar=sm["c1"][:, h_:h_ + 1],
                                                                       in1=Vb[:, h_, :], op0=ALU.mult, op1=ALU.add),
                 r=[ksd, sm["c1"], Vb.sub(h_ // 4)], w=[Rr.sub(("r", h_))])
            k.op("pe", lambda h, vp=vp, h_=h_: h.matmul(vp, lhsT=R[:, h_, :], rhs=Rr[:, h_, :], start=True, stop=True),
                 r=[R.sub(g8), Rr.sub(("r", h_))], w=[vpd])
            k.op("act", lambda h, vp=vp, h_=h_: h.copy(out=vn[:, h_, :], in_=vp), r=[vpd], w=[vn.sub(h_)])
            k.op("pe", lambda h, op_=op_, h_=h_: h.matmul(op_, lhsT=S[:, h_, :], rhs=QdT[:, h_, :], start=True, stop=False),
                 r=[S.sub(h_), QdT.sub(g8)], w=[ob.sub(h_ % 8)], sig=False)
            k.op("pe", lambda h, op_=op_, h_=h_: h.matmul(op_, lhsT=vn[:, h_, :], rhs=intraT[:, h_, :], start=False, stop=True),
                 r=[vn.sub(h_), intraT.sub(g8)], w=[ob.sub(h_ % 8)])
            k.op("pe", lambda h, dsp=dsp, h_=h_: h.matmul(dsp, lhsT=kdec[:, h_, :], rhs=vn[:, h_, :], start=True, stop=True),
                 r=[kdec.sub(g8), vn.sub(h_)], w=[dsd])
            k.op("dve", lambda h, dsp=dsp, h_=h_: h.scalar_tensor_tensor(out=S[:, h_, :], in0=S[:, h_, :], scalar=dS[:, h_:h_ + 1],
                                                                         in1=dsp, op0=ALU.mult, op1=ALU.add),
                 r=[dsd, dS, S.sub(h_)], w=[S.sub(h_)])
            if h_ % 8 == 7:
                k.op("act", lambda h, ob=ob, g8=g8: h.copy(out=ost[:, g8 * 8:g8 * 8 + 8, :],
                                                           in_=ob[:, :].rearrange("p (a b) -> p a b", a=8)),
                     r=[ob.sub(i) for i in range(8)], w=[ost.sub(g8)])
                k.dma("sp", o_s[g8 * 8:g8 * 8 + 8, :, t0:t0 + CH].rearrange("c p t -> p c t"), ost[:, g8 * 8:g8 * 8 + 8, :],
                      r=[ost.sub(g8)], w=[o_s.sub((g8, c))])
    k.barrier()
    k.release(m)


def stage_dn_c(k, hT, ntok, o_s, z_s, onorm_d, wout_d, ident_d):
    m = k.mark()
    pb = k.pb
    ident = k.sb("ident", [128, 128], F32)
    k.dma("sp", ident[:, :], ident_d, w=[ident])
    oncol = cols_via_T(k, "oncol", onorm_d.rearrange("(c p) -> c p", p=128), 1, ident, pb[7])
    ones = k.sb("ones", [128, 128], BF16)
    k.op("dve", lambda h: h.memset(ones[:, :], 1.0), w=[ones])
    epsc = k.sb("epsc", [128, 1], F32)
    k.op("dve", lambda h: h.memset(epsc[:, :], EPS), w=[epsc])
    yT = k.sb("yT", [128, NH, TT], BF16)
    ob = [k.sb(f"ob{i}", [128, TT], F32) for i in range(2)]
    zb = [k.sb(f"zb{i}", [128, TT], F32) for i in range(2)]
    sq = [k.sb(f"sq{i}", [128, TT], BF16) for i in range(2)]
    rn = [k.sb(f"rn{i}", [128, TT], F32) for i in range(2)]
    hb = [k.sb(f"hb{i}", [128, TT], F32) for i in range(2)]
    wd = [k.sb(f"wd{i}", [128, NH, 128], BF16) for i in range(2)]
    woutv = wout_d.rearrange("(fc p) d -> p fc d", p=128)
    for ti in range(ntok // TT):
        t0 = ti * TT
        for h_ in range(NH):
            O = ob[h_ % 2]
            Z = zb[h_ % 2]
            Q2 = sq[h_ % 2]
            Rn = rn[h_ % 2]
            k.dma("sp", O[:, :], o_s[h_, :, t0:t0 + TT], w=[O])
            k.dma("act", Z[:, :], z_s[h_, :, t0:t0 + TT], w=[Z])
            k.op("act", lambda h, O=O, Q2=Q2: h.activation(out=Q2[:, :], in_=O[:, :], func=AF.Square), r=[O], w=[Q2])
            ps = pb[4 + h_ % 2]
            k.op("pe", lambda h, ps=ps, Q2=Q2: h.matmul(ps[:, :], lhsT=ones[:, :], rhs=Q2[:, :], start=True, stop=True),
                 r=[Q2, ones], w=[ps])
            k.op("act", lambda h, ps=ps, Rn=Rn: h.activation(out=Rn[:, :], in_=ps[:, :], func=AF.Sqrt, scale=1.0 / 128, bias=epsc[:, :]),
                 r=[ps, epsc], w=[Rn])
            k.op("dve", lambda h, Rn=Rn: h.reciprocal(out=Rn[:, :], in_=Rn[:, :]), r=[Rn], w=[Rn])
            k.op("dve", lambda h, O=O, Rn=Rn: h.scalar_tensor_tensor(out=O[:, :], in0=O[:, :], scalar=oncol[:, 0:1], in1=Rn[:, :],
                                                                     op0=ALU.mult, op1=ALU.mult), r=[O, Rn, oncol], w=[O])
            k.op("dve", lambda h, O=O, Z=Z, h_=h_: h.tensor_tensor(out=yT[:, h_, :], in0=O[:, :], in1=Z[:, :], op=ALU.mult),
                 r=[O, Z], w=[yT])
        for dc in range(KC):
            W = wd[dc % 2]
            k.dma("pool", W[:, :, :], woutv[:, :, dc * 128:(dc + 1) * 128], w=[W])
            H = hb[dc % 2]
            k.dma("sp", H[:, :], hT[dc * 128:(dc + 1) * 128, t0:t0 + TT], r=[hT.sub((dc, ti))], w=[H])
            ps = pb[dc % 2]
            for fc in range(NH):
                k.op("pe", lambda h, ps=ps, W=W, fc=fc: h.matmul(ps[:, :], lhsT=W[:, fc, :], rhs=yT[:, fc, :],
                                                                 start=(fc == 0), stop=(fc == NH - 1)),
                     r=[W, yT], w=[ps], sig=(fc == NH - 1))
            k.op("dve", lambda h, ps=ps, H=H: h.tensor_tensor(out=H[:, :], in0=H[:, :], in1=ps[:, :], op=ALU.add),
                 r=[ps, H], w=[H])
            k.dma("sp", hT[dc * 128:(dc + 1) * 128, t0:t0 + TT], H[:, :], r=[H], w=[hT.sub((dc, ti))])
    k.barrier()
    k.release(m)


def dn_layer(k, hT, ntok, seqlen, consts_d, sc, p, wsc=None):
    stage_dn_a(k, hT, ntok, seqlen, consts_d[:, 0:128], p["g"], p["w_in"], p["conv_w"], p["a_log"], p["dt_bias"],
               sc["qk"], sc["v"], sc["z"], sc["bg"], wsc=wsc)
    stage_dn_b(k, ntok, seqlen, consts_d, sc["qk"], sc["v"], sc["bg"], sc["o"])
    stage_dn_c(k, hT, ntok, sc["o"], sc["z"], p["out_norm"], p["w_out"], consts_d[:, 0:128])


NEGM = -30000.0


def fm_rmsnorm(k, src, n, width, gcols, out, sq, pss, rn, ones, epsc, tag):
    for c in range(n):
        k.op("act", lambda h, c=c: h.activation(out=sq[c % 2][:, :], in_=src[:, c, :], func=AF.Square), r=[src], w=[sq[c % 2]])
        k.op("pe", lambda h, c=c: h.matmul(pss[:, :], lhsT=ones[:, :], rhs=sq[c % 2][:, :], start=(c == 0), stop=(c == n - 1)),
             r=[sq[c % 2], ones], w=[pss])
    k.op("act", lambda h: h.activation(out=rn[:, :], in_=pss[:, :], func=AF.Sqrt, scale=1.0 / width, bias=epsc[:, :]),
         r=[pss, epsc], w=[rn])
    k.op("dve", lambda h: h.reciprocal(out=rn[:, :], in_=rn[:, :]), r=[rn], w=[rn])
    for c in range(n):
        k.op("dve", lambda h, c=c: h.scalar_tensor_tensor(out=out[:, c, :], in0=src[:, c, :], scalar=gcols[:, c:c + 1], in1=rn[:, :],
                                                          op0=ALU.mult, op1=ALU.mult), r=[src, rn, gcols], w=[out])


def stage_dsa_a(k, hT, ntok, ident_d, p, sc):
    m = k.mark()
    pb = k.pb
    ident = k.sb("ident", [128, 128], F32)
    k.dma("sp", ident[:, :], ident_d, w=[ident])
    identb = k.sb("identb", [128, 128], BF16)
    k.op("dve", lambda h: h.tensor_copy(out=identb[:, :], in_=ident[:, :]), r=[ident], w=[identb])
    gcol = cols_via_T(k, "gcol", p["g"].rearrange("(c p) -> c p", p=128), KC, ident, pb[7])
    qncol = cols_via_T(k, "qncol", p["q_norm"].rearrange("(c p) -> c p", p=128), 4, ident, pb[7])
    kvcol = cols_via_T(k, "kvcol", p["kv_norm"].rearrange("(c p) -> c p", p=128), 2, ident, pb[7])
    kicol = cols_via_T(k, "kicol", p["kidx_norm"].rearrange("(c p) -> c p", p=128), 1, ident, pb[7])
    ones = k.sb("ones", [128, 128], BF16)
    k.op("dve", lambda h: h.memset(ones[:, :], 1.0), w=[ones])
    epsc = k.sb("epsc", [128, 1], F32)
    k.op("dve", lambda h: h.memset(epsc[:, :], EPS), w=[epsc])
    hbuf = k.sb("hbuf", [128, KC, TT], F32)
    uT = k.sb("uT", [128, KC, TT], BF16)
    sq = [k.sb(f"sq{i}", [128, TT], BF16) for i in range(2)]
    rstd = k.sb("rstd", [128, TT], F32)
    Wd = k.sb("Wd", [128, KC, 912], BF16)
    k.dma("pool", Wd[:, :, :], p["w_in"].rearrange("(kc p) f -> p kc f", p=128), w=[Wd])
    Wuq = k.sb("Wuq", [128, 4, 4096], BF16)
    k.dma("pool", Wuq[:, :, :], p["w_uq"].rearrange("(kc p) f -> p kc f", p=128), w=[Wuq])
    Wuk = k.sb("Wuk", [128, 16, 256], BF16)
    k.dma("pool", Wuk[:, :, :], p["w_uk"].rearrange("h d r -> d h r"), w=[Wuk])
    pj = k.sb("pj", [128, 7, TT], F32)
    qn = k.sb("qn", [128, 4, TT], BF16)
    ckn = k.sb("ckn", [128, 2, TT], BF16)
    kin = k.sb("kin", [128, 1, TT], BF16)
    rn = k.sb("rn", [128, TT], F32)
    ctm = [k.sb(f"ctm{i}", [128, 256], BF16) for i in range(2)]
    wi = [k.sb(f"wi{i}", [128, 32], F32) for i in range(2)]
    qT = [k.sb(f"qT{i}", [128, TT], BF16) for i in range(2)]
    ob = [k.sb(f"ob{i}", [128, TT], BF16) for i in range(3)]
    oi = 0
    wscale = (16.0 ** -0.5) * (128.0 ** -0.5)
    for ti in range(ntok // TT):
        t0 = ti * TT
        rms_to_uT(k, hT, t0, gcol, hbuf, uT, sq, rstd, ones, pb[4], epsc)
        for fc in range(7):
            ps = pb[fc % 2]
            for kc in range(KC):
                k.op("pe", lambda h, ps=ps, kc=kc, fc=fc: h.matmul(ps[:, :], lhsT=Wd[:, kc, fc * 128:(fc + 1) * 128], rhs=uT[:, kc, :],
                                                                   start=(kc == 0), stop=(kc == KC - 1)),
                     r=[Wd, uT], w=[ps], sig=(kc == KC - 1))
            k.op("act", lambda h, ps=ps, fc=fc: h.copy(out=pj[:, fc, :], in_=ps[:, :]), r=[ps], w=[pj])
        fm_rmsnorm(k, Tl(pj[:, 0:4, :]) if False else pj, 4, 512, qncol, qn, sq, pb[5], rn, ones, epsc, "q")
        pj_kv = Tl(pj[:, 4:6, :]); pj_kv.d = pj.d
        pj_ki = Tl(pj[:, 6:7, :]); pj_ki.d = pj.d
        fm_rmsnorm(k, pj_kv, 2, 256, kvcol, ckn, sq, pb[5], rn, ones, epsc, "kv")
        fm_rmsnorm(k, pj_ki, 1, 128, kicol, kin, sq, pb[5], rn, ones, epsc, "ki")
        k.dma("sp", sc["kidx"][:, t0:t0 + TT], kin[:, 0, :], r=[kin], w=[sc["kidx"].sub(ti)])
        for rc in range(2):
            k.dma("sp", sc["ckvT"][rc, :, t0:t0 + TT], ckn[:, rc, :], r=[ckn], w=[sc["ckvT"].sub((rc, ti))])
        for tb in range(TT // 128):
            psb = pb[2 + tb % 2]
            pv = psb[:, :].bitcast(BF16)
            for rc in range(2):
                k.op("pe", lambda h, pv=pv, rc=rc, tb=tb: h.transpose(pv[:, rc * 128:(rc + 1) * 128], ckn[:, rc, tb * 128:(tb + 1) * 128],
                                                                      identb[:, :]), r=[ckn, identb], w=[psb], sig=(rc == 1))
            C = ctm[tb % 2]
            k.op("act", lambda h, pv=pv, C=C: h.copy(out=C[:, :], in_=pv[:, 0:256]), r=[psb], w=[C])
            k.dma("sp", sc["ckvtm"][t0 + tb * 128:t0 + (tb + 1) * 128, :], C[:, :], r=[C], w=[sc["ckvtm"].sub((ti, tb))])
            ps = pb[6 + tb % 2]
            for kc in range(KC):
                k.op("pe", lambda h, ps=ps, kc=kc, tb=tb: h.matmul(ps[:, 0:16], lhsT=uT[:, kc, tb * 128:(tb + 1) * 128], rhs=Wd[:, kc, 896:912],
                                                                   start=(kc == 0), stop=(kc == KC - 1)),
                     r=[uT, Wd], w=[ps], sig=(kc == KC - 1))
            W_ = wi[tb % 2]
            k.op("act", lambda h, ps=ps, W_=W_: h.activation(out=W_[:, 0:16], in_=ps[:, 0:16], func=AF.Abs, scale=wscale), r=[ps], w=[W_])
            k.op("act", lambda h, ps=ps, W_=W_: h.activation(out=W_[:, 16:32], in_=ps[:, 0:16], func=AF.Sign), r=[ps, W_], w=[W_])
            k.dma("sp", sc["widx"][t0 + tb * 128:t0 + (tb + 1) * 128, :], W_[:, :], r=[W_], w=[sc["widx"].sub((ti, tb))])
        for c in range(32):
            ps = pb[c % 2]
            for kc in range(4):
                k.op("pe", lambda h, ps=ps, kc=kc, c=c: h.matmul(ps[:, :], lhsT=Wuq[:, kc, c * 128:(c + 1) * 128], rhs=qn[:, kc, :],
                                                                 start=(kc == 0), stop=(kc == 3)), r=[Wuq, qn], w=[ps], sig=(kc == 3))
            if c >= 16:
                O = ob[oi % 3]; oi += 1
                k.op("act", lambda h, ps=ps, O=O: h.copy(out=O[:, :], in_=ps[:, :]), r=[ps], w=[O])
                k.dma("sp", sc["qidxT"][c - 16, :, t0:t0 + TT], O[:, :], r=[O], w=[sc["qidxT"].sub((c, ti))])
            else:
                Q = qT[c % 2]
                k.op("act", lambda h, ps=ps, Q=Q: h.copy(out=Q[:, :], in_=ps[:, :]), r=[ps], w=[Q])
                for rc in range(2):
                    ps2 = pb[2 + rc]
                    k.op("pe", lambda h, ps2=ps2, Q=Q, c=c, rc=rc: h.matmul(ps2[:, :], lhsT=Wuk[:, c, rc * 128:(rc + 1) * 128], rhs=Q[:, :],
                                                                            start=True, stop=True), r=[Wuk, Q], w=[ps2])
                    O = ob[oi % 3]; oi += 1
                    k.op("dve", lambda h, ps2=ps2, O=O: h.tensor_scalar(out=O[:, :], in0=ps2[:, :], scalar1=128.0 ** -0.5, scalar2=None,
                                                                        op0=ALU.mult), r=[ps2], w=[O])
                    k.dma("sp", sc["qabsT"][c, rc, :, t0:t0 + TT], O[:, :], r=[O], w=[sc["qabsT"].sub((c, rc, ti))])
    k.barrier()
    k.release(m)


def stage_dsa_b(k, hT, ntok, seqlen, ident_d, p, sc):
    m = k.mark()
    pb = k.pb
    ident = k.sb("ident", [128, 128], F32)
    k.dma("sp", ident[:, :], ident_d, w=[ident])
    identb = k.sb("identb", [128, 128], BF16)
    k.op("dve", lambda h: h.tensor_copy(out=identb[:, :], in_=ident[:, :]), r=[ident], w=[identb])
    ones = k.sb("ones", [128, 128], BF16)
    k.op("dve", lambda h: h.memset(ones[:, :], 1.0), w=[ones])
    negd = k.sb("negd", [128, 128], F32)
    k.op("dve", lambda h: h.memset(negd[:, :], 0.0), w=[negd])
    k.op("dve", lambda h: h.memset(negd[0:64, 64:128], NEGM), r=[negd], w=[negd])
    Wout = k.sb("Wout", [128, 16, 2048], BF16)
    k.dma("pool", Wout[:, :, :], p["w_out"].rearrange("(c p) d -> p c d", p=128), w=[Wout])
    Wuv = k.sb("Wuv", [128, 16, 2, 128], BF16)
    for rc in range(2):
        k.dma("pool", Wuv[:, :, rc, :], p["w_uv"][:, rc * 128:(rc + 1) * 128, :].rearrange("h r d -> r h d"), w=[Wuv])
    nkb_all = seqlen // 128
    kidxT = k.sb("kidxT", [128, seqlen], BF16)
    ckvT = k.sb("ckvT", [128, 2, seqlen], BF16)
    ckvtm = k.sb("ckvtm", [128, nkb_all, 256], BF16)
    qidxT = k.sb("qidxT", [128, 16, 128], BF16)
    qabsT = k.sb("qabsT", [128, 16, 2, 128], BF16)
    widx = k.sb("widx", [128, 32], F32)
    score = k.sb("score", [128, seqlen], F32)
    work = k.sb("work", [128, seqlen], F32)
    m8 = k.sb("m8", [128, 8], F32)
    mask01 = k.sb("mask01", [128, seqlen], BF16)
    maskT = k.sb("maskT", [128, nkb_all, 128], BF16)
    rl = [k.sb(f"rl{i}", [128, 512], F32) for i in range(2)]
    pT = [k.sb(f"pT{i}", [128, 512], BF16) for i in range(2)]
    pTm = [k.sb(f"pTm{i}", [128, 512], BF16) for i in range(2)]
    rs = k.sb("rs", [128, 512], F32)
    olatT = k.sb("olatT", [128, 16, 2, 128], BF16)
    oT = k.sb("oT", [128, 16, 128], BF16)
    H = [k.sb(f"H{i}", [128, 4, 128], F32) for i in range(2)]
    maskT2 = [maskT, k.sb("maskTb", [128, nkb_all, 128], BF16)]
    st = {"li": 0}

    def gen_index(s0, qb):
        tq = s0 + qb * 128
        ti = tq // TT
        nkb = qb + 1
        n = nkb * 128
        mT = maskT2[qb % 2]
        k.dma("sp", qidxT[:, :, :], sc["qidxT"][:, :, tq:tq + 128].rearrange("h p t -> p h t"),
              r=[sc["qidxT"].sub((c, ti)) for c in range(16, 32)], w=[qidxT])
        k.dma("sp", widx[:, :], sc["widx"][tq:tq + 128, :], r=[sc["widx"].sub((ti, (tq % TT) // 128))], w=[widx])
        for kg in range((n + 511) // 512):
            wdt = min(512, n - kg * 512)
            cs = slice(kg * 512, kg * 512 + wdt)
            for h_ in range(16):
                ps = pb[h_ % 2]
                k.op("pe", lambda h, ps=ps, h_=h_, cs=cs, wdt=wdt: h.matmul(ps[:, 0:wdt], lhsT=qidxT[:, h_, :], rhs=kidxT[:, cs],
                                                                            start=True, stop=True), r=[qidxT, kidxT], w=[ps])
                T_ = rl[st["li"] % 2]; st["li"] += 1
                k.op("act", lambda h, ps=ps, T_=T_, h_=h_, wdt=wdt: h.activation(out=T_[:, 0:wdt], in_=ps[:, 0:wdt], func=AF.Relu,
                                                                                 scale=widx[:, h_:h_ + 1]), r=[ps, widx], w=[T_])
                if h_ == 0:
                    k.op("dve", lambda h, T_=T_, cs=cs, wdt=wdt: h.tensor_scalar(out=score[:, cs], in0=T_[:, 0:wdt],
                                                                                 scalar1=widx[:, 16:17], scalar2=None, op0=ALU.mult),
                         r=[T_, widx], w=[score])
                else:
                    k.op("dve", lambda h, T_=T_, cs=cs, wdt=wdt, h_=h_: h.scalar_tensor_tensor(
                        out=score[:, cs], in0=T_[:, 0:wdt], scalar=widx[:, 16 + h_:17 + h_], in1=score[:, cs],
                        op0=ALU.mult, op1=ALU.add), r=[T_, widx, score], w=[score])
                yield
        k.op("dve", lambda h, n=n: h.tensor_tensor(out=score[:, n - 128:n], in0=score[:, n - 128:n], in1=negd[:, :], op=ALU.add),
             r=[score, negd], w=[score])
        if n > 256:
            k.op("dve", lambda h, n=n: h.tensor_copy(out=work[:, 0:n], in_=score[:, 0:n]), r=[score], w=[work])
            for rnd in range(32):
                k.op("dve", lambda h, n=n: h.max(out=m8[:, :], in_=work[:, 0:n]), r=[work], w=[m8])
                if rnd < 31:
                    k.op("dve", lambda h, n=n: h.match_replace(out=work[:, 0:n], in_to_replace=m8[:, :], in_values=work[:, 0:n],
                                                               imm_value=-1e9), r=[work, m8], w=[work])
                yield
            k.op("dve", lambda h, n=n: h.tensor_scalar(out=mask01[:, 0:n], in0=score[:, 0:n], scalar1=m8[:, 7:8], scalar2=None,
                                                       op0=ALU.is_ge), r=[score, m8], w=[mask01])
        else:
            k.op("dve", lambda h, n=n: h.tensor_scalar(out=mask01[:, 0:n], in0=score[:, 0:n], scalar1=-10000.0, scalar2=None,
                                                       op0=ALU.is_ge), r=[score], w=[mask01])
        for kb in range(nkb):
            psb = pb[kb % 2]
            pv = psb[:, :].bitcast(BF16)
            k.op("pe", lambda h, pv=pv, kb=kb: h.transpose(pv[:, 0:128], mask01[:, kb * 128:(kb + 1) * 128], identb[:, :]),
                 r=[mask01, identb], w=[psb])
            k.op("act", lambda h, pv=pv, kb=kb, mT=mT: h.copy(out=mT[:, kb, :], in_=pv[:, 0:128]), r=[psb], w=[mT])
            yield

    def gen_attn(s0, qb):
        tq = s0 + qb * 128
        ti = tq // TT
        nkb = qb + 1
        mT = maskT2[qb % 2]
        for rc in range(2):
            k.dma("sp", qabsT[:, :, rc, :], sc["qabsT"][:, rc, :, tq:tq + 128].rearrange("h p t -> p h t"),
                  r=[sc["qabsT"].sub((c, rc, ti)) for c in range(16)], w=[qabsT])
        for hg in range(4):
            acc = [pb[2 + 3 * (hg % 2) + j] for j in range(3)]
            for kb in range(nkb):
                ps = pb[kb % 2]
                for rc in range(2):
                    k.op("pe", lambda h, ps=ps, kb=kb, rc=rc, hg=hg: h.matmul(
                        ps[:, :].rearrange("p (a b) -> p a b", a=4), lhsT=ckvT[:, rc, kb * 128:(kb + 1) * 128],
                        rhs=qabsT[:, hg * 4:hg * 4 + 4, rc, :], start=(rc == 0), stop=(rc == 1)),
                        r=[ckvT, qabsT], w=[ps], sig=(rc == 1))
                P_ = pT[kb % 2]
                Pm = pTm[kb % 2]
                k.op("act", lambda h, ps=ps, P_=P_: h.activation(out=P_[:, :], in_=ps[:, :], func=AF.Exp), r=[ps], w=[P_])
                k.op("pool", lambda h, P_=P_, Pm=Pm, kb=kb, mT=mT: h.tensor_tensor(
                    out=Pm[:, :].rearrange("p (a b) -> p a b", a=4), in0=P_[:, :].rearrange("p (a b) -> p a b", a=4),
                    in1=bc(mT[:, kb, :], [128, 4, 128], 1), op=ALU.mult), r=[P_, mT], w=[Pm])
                for rc in range(2):
                    k.op("pe", lambda h, kb=kb, rc=rc, Pm=Pm, acc=acc: h.matmul(acc[rc][:, :], lhsT=ckvtm[:, kb, rc * 128:(rc + 1) * 128],
                                                                                rhs=Pm[:, :], start=(kb == 0), stop=(kb == nkb - 1)),
                         r=[ckvtm, Pm], w=[acc[rc]], sig=False)
                k.op("pe", lambda h, kb=kb, Pm=Pm, acc=acc: h.matmul(acc[2][:, :], lhsT=ones[:, :], rhs=Pm[:, :],
                                                                     start=(kb == 0), stop=(kb == nkb - 1)),
                     r=[ones, Pm], w=[acc[2]])
                yield
            k.op("dve", lambda h, acc=acc: h.reciprocal(out=rs[:, :], in_=acc[2][:, :]), r=[acc[2]], w=[rs])
            for rc in range(2):
                k.op("dve", lambda h, acc=acc, rc=rc, hg=hg: h.tensor_tensor(
                    out=olatT[:, hg * 4:hg * 4 + 4, rc, :], in0=acc[rc][:, :].rearrange("p (a b) -> p a b", a=4),
                    in1=rs[:, :].rearrange("p (a b) -> p a b", a=4), op=ALU.mult), r=[acc[rc], rs], w=[olatT])
            yield
        for hg in range(4):
            ps = pb[hg % 2]
            for hh in range(4):
                h_ = hg * 4 + hh
                for rc in range(2):
                    k.op("pe", lambda h, ps=ps, h_=h_, hh=hh, rc=rc: h.matmul(ps[:, hh * 128:(hh + 1) * 128], lhsT=Wuv[:, h_, rc, :],
                                                                              rhs=olatT[:, h_, rc, :], start=(rc == 0), stop=(rc == 1)),
                         r=[Wuv, olatT], w=[ps], sig=(hh == 3 and rc == 1))
            k.op("act", lambda h, ps=ps, hg=hg: h.copy(out=oT[:, hg * 4:hg * 4 + 4, :], in_=ps[:, :].rearrange("p (a b) -> p a b", a=4)),
                 r=[ps], w=[oT])
            yield
        for d4 in range(4):
            ps = pb[d4 % 2]
            Hh = H[d4 % 2]
            k.dma("sp", Hh[:, :, :], hT[d4 * 512:(d4 + 1) * 512, tq:tq + 128].rearrange("(c p) t -> p c t", p=128),
                  r=[hT.sub((d4 * 4 + j, ti)) for j in range(4)], w=[Hh])
            for dd in range(4):
                dc = d4 * 4 + dd
                for h_ in range(16):
                    k.op("pe", lambda h, ps=ps, dd=dd, dc=dc, h_=h_: h.matmul(ps[:, dd * 128:(dd + 1) * 128], lhsT=Wout[:, h_, dc * 128:(dc + 1) * 128],
                                                                              rhs=oT[:, h_, :], start=(h_ == 0), stop=(h_ == 15)),
                         r=[Wout, oT], w=[ps], sig=(dd == 3 and h_ == 15))
            k.op("dve", lambda h, ps=ps, Hh=Hh: h.tensor_tensor(out=Hh[:, :, :], in0=Hh[:, :, :], in1=ps[:, :].rearrange("p (a b) -> p a b", a=4),
                                                                op=ALU.add), r=[ps, Hh], w=[Hh])
            k.dma("sp", hT[d4 * 512:(d4 + 1) * 512, tq:tq + 128].rearrange("(c p) t -> p c t", p=128), Hh[:, :, :],
                  r=[Hh], w=[hT.sub((d4 * 4 + j, ti)) for j in range(4)])
            yield

    def drive(g1, g2):
        gens = [g for g in (g1, g2) if g is not None]
        while gens:
            for g in list(gens):
                try:
                    next(g)
                except StopIteration:
                    gens.remove(g)

    nqb = seqlen // 128
    for b in range(ntok // seqlen):
        s0 = b * seqlen
        k.dma("sp", kidxT[:, :], sc["kidx"][:, s0:s0 + seqlen], r=[sc["kidx"].sub(i) for i in range(ntok // TT)], w=[kidxT])
        for rc in range(2):
            k.dma("sp", ckvT[:, rc, :], sc["ckvT"][rc, :, s0:s0 + seqlen],
                  r=[sc["ckvT"].sub((rc, i)) for i in range(ntok // TT)], w=[ckvT])
        k.dma("sp", ckvtm[:, :, :], sc["ckvtm"][s0:s0 + seqlen, :].rearrange("(kb p) r -> p kb r", p=128),
              r=[sc["ckvtm"].sub((i, j)) for i in range(ntok // TT) for j in range(4)], w=[ckvtm])
        drive(gen_index(s0, 0), None)
        for qb in range(nqb):
            drive(gen_attn(s0, qb), gen_index(s0, qb + 1) if qb + 1 < nqb else None)
    k.barrier()
    k.release(m)


def dsa_scratch(k, ntok, tag):
    return {"qidxT": k.dram(f"qidxT{tag}", [16, 128, ntok], BF16), "qabsT": k.dram(f"qabsT{tag}", [16, 2, 128, ntok], BF16),
            "kidx": k.dram(f"kidx{tag}", [128, ntok], BF16), "ckvT": k.dram(f"ckvT{tag}", [2, 128, ntok], BF16),
            "ckvtm": k.dram(f"ckvtm{tag}", [ntok, 256], BF16), "widx": k.dram(f"widx{tag}", [ntok, 32], F32)}


def dsa_layer(k, hT, ntok, seqlen, consts_d, sc, p):
    stage_dsa_a(k, hT, ntok, consts_d[:, 0:128], p, sc)
    stage_dsa_b(k, hT, ntok, seqlen, consts_d[:, 0:128], p, sc)


def stage_in(k, x_d, hT, ntok, ident_d):
    m = k.mark()
    pb = k.pb
    ident = k.sb("ident", [128, 128], F32)
    k.dma("sp", ident[:, :], ident_d, w=[ident])
    X = [k.sb(f"X{i}", [128, D], F32) for i in range(2)]
    stg = [k.sb(f"stg{i}", [128, 4, 128], F32) for i in range(2)]
    si = 0
    for tb in range(ntok // 128):
        Xt = X[tb % 2]
        k.dma("sp", Xt[:, :], x_d[tb * 128:(tb + 1) * 128, :], w=[Xt])
        for k4 in range(4):
            ps = pb[k4 % 2]
            for j in range(4):
                kc = k4 * 4 + j
                k.op("pe", lambda h, ps=ps, Xt=Xt, kc=kc, j=j: h.transpose(ps[:, j * 128:(j + 1) * 128], Xt[:, kc * 128:(kc + 1) * 128], ident[:, :]),
                     r=[Xt, ident], w=[ps], sig=(j == 3))
            St = stg[si % 2]; si += 1
            k.op("act" if k4 % 2 else "dve",
                 (lambda h, ps=ps, St=St: h.copy(out=St[:, :, :], in_=ps[:, :].rearrange("p (a b) -> p a b", a=4))) if k4 % 2 else
                 (lambda h, ps=ps, St=St: h.tensor_copy(out=St[:, :, :], in_=ps[:, :].rearrange("p (a b) -> p a b", a=4))),
                 r=[ps], w=[St])
            k.dma("sp", hT[k4 * 512:(k4 + 1) * 512, tb * 128:(tb + 1) * 128].rearrange("(c p) t -> p c t", p=128), St[:, :, :],
                  r=[St], w=[hT.sub((k4 * 4 + j, (tb * 128) // TT)) for j in range(4)])
    k.barrier()
    k.release(m)


def stage_out(k, hT, out_d, ntok, g_d, ident_d):
    m = k.mark()
    pb = k.pb
    ident = k.sb("ident", [128, 128], F32)
    k.dma("sp", ident[:, :], ident_d, w=[ident])
    gcol = cols_via_T(k, "gcol", g_d.rearrange("(c p) -> c p", p=128), KC, ident, pb[7])
    ones = k.sb("ones", [128, 128], BF16)
    k.op("dve", lambda h: h.memset(ones[:, :], 1.0), w=[ones])
    epsc = k.sb("epsc", [128, 1], F32)
    k.op("dve", lambda h: h.memset(epsc[:, :], EPS), w=[epsc])
    hbuf = k.sb("hbuf", [128, KC, TT], F32)
    yT = k.sb("yT", [128, KC, TT], F32)
    sq = [k.sb(f"sq{i}", [128, TT], BF16) for i in range(2)]
    rstd = k.sb("rstd", [128, TT], F32)
    O = [k.sb(f"O{i}", [128, D], F32) for i in range(2)]
    for ti in range(ntok // TT):
        t0 = ti * TT
        rms_to_uT(k, hT, t0, gcol, hbuf, yT, sq, rstd, ones, pb[4], epsc)
        for tb in range(TT // 128):
            Ot = O[tb % 2]
            for k4 in range(4):
                ps = pb[k4 % 2]
                for j in range(4):
                    kc = k4 * 4 + j
                    k.op("pe", lambda h, ps=ps, kc=kc, j=j, tb=tb: h.transpose(ps[:, j * 128:(j + 1) * 128], yT[:, kc, tb * 128:(tb + 1) * 128],
                                                                               ident[:, :]), r=[yT, ident], w=[ps], sig=(j == 3))
                if k4 % 2:
                    k.op("act", lambda h, ps=ps, Ot=Ot, k4=k4: h.copy(out=Ot[:, k4 * 512:(k4 + 1) * 512], in_=ps[:, :]), r=[ps], w=[Ot])
                else:
                    k.op("dve", lambda h, ps=ps, Ot=Ot, k4=k4: h.tensor_copy(out=Ot[:, k4 * 512:(k4 + 1) * 512], in_=ps[:, :]), r=[ps], w=[Ot])
            k.dma("sp", out_d[t0 + tb * 128:t0 + (tb + 1) * 128, :], Ot[:, :], r=[Ot], w=[out_d])
    k.barrier()
    k.release(m)


NTOK = 4096
SEQ = 2048
PARAM_SHAPES = {
    "norm_mix": [4, 2048], "norm_mlp": [4, 2048], "norm_final": [2048],
    "dn_w_in": [2, 2048, 12352], "dn_conv_w": [2, 4, 8192], "dn_a_log": [2, 32], "dn_dt_bias": [2, 32],
    "dn_out_norm": [2, 128], "dn_w_out": [2, 4096, 2048],
    "dsa_w_in": [2, 2048, 912], "dsa_q_norm": [2, 512], "dsa_kv_norm": [2, 256], "dsa_kidx_norm": [2, 128],
    "dsa_w_uq": [2, 512, 4096], "dsa_w_uk": [2, 16, 128, 256], "dsa_w_uv": [2, 16, 256, 128], "dsa_w_out": [2, 2048, 2048],
    "mlp_w_up": [4, 2048, 8192], "mlp_w_down": [4, 8192, 2048],
}


def build_program(ntok=NTOK, seqlen=SEQ, depth=4):
    nc = bass.Bass("TRN2", target_bir_lowering=False)
    k = K(nc)
    x_d = k.dram("x", [ntok, D], F32, kind="ExternalInput")
    consts = k.dram("consts", [128, 320], F32, kind="ExternalInput")
    P = {nm: k.dram(nm, shp, F32, kind="ExternalInput").t for nm, shp in PARAM_SHAPES.items()}
    out_d = k.dram("out", [ntok, D], F32, kind="ExternalOutput")
    hT = k.dram("hT", [D, ntok], F32)
    dn_sc = {"qk": k.dram("qk_s", [32, 128, ntok], F32), "v": k.dram("v_s", [NH, 128, ntok], F32),
             "z": k.dram("z_s", [NH, 128, ntok], F32), "bg": k.dram("bg_s", [ntok, 64], F32),
             "o": k.dram("o_s", [NH, 128, ntok], F32)}
    dsa_sc = dsa_scratch(k, ntok, "0")
    wsc = {"up": k.dram("wup_bf", [D, 8192], BF16), "dn": k.dram("wdn_bf", [8192, D], BF16),
           "win": k.dram("win_bf", [D, DN_IN], BF16)} if BF16_WSCRATCH else None
    ident_d = consts.t[:, 0:128]
    stage_in(k, x_d.t, hT, ntok, ident_d)
    for i in range(depth):
        j = i // 2
        if i % 2 == 0:
            p = {"g": P["norm_mix"][i], "w_in": P["dn_w_in"][j], "conv_w": P["dn_conv_w"][j], "a_log": P["dn_a_log"][j],
                 "dt_bias": P["dn_dt_bias"][j], "out_norm": P["dn_out_norm"][j], "w_out": P["dn_w_out"][j]}
            dn_layer(k, hT, ntok, seqlen, consts.t, dn_sc, p, wsc=wsc)
        else:
            p = {"g": P["norm_mix"][i], "w_in": P["dsa_w_in"][j], "q_norm": P["dsa_q_norm"][j], "kv_norm": P["dsa_kv_norm"][j],
                 "kidx_norm": P["dsa_kidx_norm"][j], "w_uq": P["dsa_w_uq"][j], "w_uk": P["dsa_w_uk"][j], "w_uv": P["dsa_w_uv"][j],
                 "w_out": P["dsa_w_out"][j]}
            dsa_layer(k, hT, ntok, seqlen, consts.t, dsa_sc, p)
        stage_mlp(k, hT, ntok, P["norm_mlp"][i], P["mlp_w_up"][i], P["mlp_w_down"][i], 8192, wsc=wsc)
    stage_out(k, hT, out_d, ntok, P["norm_final"], ident_d)
    k.finish()
    return nc, k


def kernel(**inputs):
    n = 8
    x = np.ascontiguousarray(np.asarray(inputs["x"], dtype=np.float32))
    B, T, Dm = x.shape
    per = B // n
    consts = make_consts()
    params = {nm: np.ascontiguousarray(np.asarray(inputs[nm], dtype=np.float32)) for nm in PARAM_SHAPES}
    nc, _ = build_program(per * T, T, 4)
    in_maps = []
    for c in range(n):
        mp = {"x": x[c * per:(c + 1) * per].reshape(per * T, Dm), "consts": consts}
        mp.update(params)
        in_maps.append(mp)
    res = run_bass_kernel_spmd(nc, in_maps, core_ids=list(range(n)))
    outs = [np.asarray(r["out"]).reshape(per, T, Dm) for r in res.results]
    return np.concatenate(outs, axis=0).astype(np.float32)
```
